# Optimizing a Trainium2 kernel written in Bass

```python
import jax, jax.numpy as jnp
from jax import lax
import numpy as np

D_MODEL = 1024
BATCH = 8
SEQ = 2048
DEPTH = 2
DEC_BATCH = 16
DEC_SEQ = 16
PAST_LEN = 4096

CHUNK = 64
N_MIXERS = 2
N_HEADS = 8
HEAD_K = 128
HEAD_V = D_MODEL // N_HEADS
D_KEY = N_HEADS * HEAD_K
HGRN_BLOCK = 16
CONV_W = 31
D_FF = 4 * D_MODEL
N_A = (DEPTH + 1) // 2
N_B = DEPTH // 2
ALPHA = (2 * DEPTH) ** 0.25
BETA = (8 * DEPTH) ** -0.25
LN_EPS = 1e-5
RMS_EPS = 1e-5

kernel_name = "hgrn2_conformer_conv_streaming_step"


def layer_norm(x, g, b):
    xf = x.astype(jnp.float32)
    mu = jnp.mean(xf, -1, keepdims=True)
    var = jnp.mean(jnp.square(xf - mu), -1, keepdims=True)
    return ((xf - mu) * lax.rsqrt(var + LN_EPS) * g + b).astype(x.dtype)


def swiglu(x, w_gate, w_up, w_down):
    return (jax.nn.silu(x @ w_gate) * (x @ w_up)) @ w_down


def hgrn2_recurrence(q, k, v, logf, s0):
    bsz, t, h, _ = q.shape
    dv = v.shape[-1]
    n = -(-t // HGRN_BLOCK)
    pad = n * HGRN_BLOCK - t
    padw = ((0, 0), (0, pad), (0, 0), (0, 0))
    q, k, v, logf = (jnp.pad(a.astype(jnp.float32), padw).reshape(bsz, n, HGRN_BLOCK, h, a.shape[-1])
                     for a in (q, k, v, logf))
    b = jnp.cumsum(logf, axis=2)
    ref = b[:, :, HGRN_BLOCK // 2:HGRN_BLOCK // 2 + 1]
    q_rel = q * jnp.exp(b - ref)
    k_rel = k * jnp.exp(ref - b)
    scores = jnp.einsum('bnthk,bnshk->bnhts', q_rel, k_rel)
    mask = jnp.tril(jnp.ones((HGRN_BLOCK, HGRN_BLOCK), dtype=bool))
    scores = jnp.where(mask, scores, 0.0)
    o_intra = jnp.einsum('bnhts,bnshv->bnthv', scores, v)
    q_inter = q * jnp.exp(b)
    b_last = b[:, :, -1]
    k_state = k * jnp.exp(b_last[:, :, None] - b)

    def step(s, inp):
        qi, ki, vi, dl = inp
        o = jnp.einsum('bthk,bhkv->bthv', qi, s)
        s = s * jnp.exp(dl)[..., None] + jnp.einsum('bthk,bthv->bhkv', ki, vi)
        return s, o

    xs = (jnp.moveaxis(q_inter, 1, 0), jnp.moveaxis(k_state, 1, 0),
          jnp.moveaxis(v, 1, 0), jnp.moveaxis(b_last, 1, 0))
    s_final, o_inter = lax.scan(step, s0.astype(jnp.float32), xs)
    o = o_intra + jnp.moveaxis(o_inter, 0, 1)
    o = o.reshape(bsz, n * HGRN_BLOCK, h, dv)[:, :t]
    return o, s_final


def hgrn2_mixer(x, s0, lb, w_in, norm_g, w_out):
    bsz, t, _ = x.shape
    q, f, v, g = jnp.split(x @ w_in, [D_KEY, 2 * D_KEY, 2 * D_KEY + D_MODEL], axis=-1)
    q = jax.nn.silu(q.astype(jnp.float32)) * HEAD_K ** -0.5
    forget = lb + (1.0 - lb) * jax.nn.sigmoid(f.astype(jnp.float32))
    o, s = hgrn2_recurrence(q.reshape(bsz, t, N_HEADS, HEAD_K),
                            (1.0 - forget).reshape(bsz, t, N_HEADS, HEAD_K),
                            v.reshape(bsz, t, N_HEADS, HEAD_V),
                            jnp.log(forget).reshape(bsz, t, N_HEADS, HEAD_K), s0)
    o = o * lax.rsqrt(jnp.mean(jnp.square(o), -1, keepdims=True) + RMS_EPS) * norm_g
    o = o.reshape(bsz, t, D_MODEL) * jax.nn.silu(g.astype(jnp.float32))
    return o.astype(x.dtype) @ w_out, s


def conv_mixer(x, cache, w_pw1, b_pw1, w_dw, b_dw, ln_g, ln_b, w_pw2, b_pw2):
    a, gate = jnp.split(x @ w_pw1 + b_pw1, 2, axis=-1)
    u = a * jax.nn.sigmoid(gate)
    u_ext = jnp.concatenate([cache.astype(u.dtype), u], axis=1)
    y = lax.conv_general_dilated(u_ext, w_dw[:, None, :].astype(u.dtype), window_strides=(1,),
                                 padding='VALID', dimension_numbers=('NWC', 'WIO', 'NWC'),
                                 feature_group_count=D_MODEL) + b_dw
    new_cache = u_ext[:, -(CONV_W - 1):]
    y = jax.nn.silu(layer_norm(y, ln_g, ln_b))
    return y @ w_pw2 + b_pw2, new_cache


def trunk(x, hgrn_state, conv_cache, w):
    new_h, new_c = [], []
    lb_all = jnp.cumsum(jax.nn.softmax(w['hgrn_lb'].astype(jnp.float32), axis=0), axis=0)
    for i in range(DEPTH):
        j = i // N_MIXERS
        x = layer_norm(ALPHA * x + 0.5 * swiglu(x, w['ffn_w_gate'][i, 0], w['ffn_w_up'][i, 0], w['ffn_w_down'][i, 0]),
                       w['ln_g'][i, 0], w['ln_b'][i, 0])
        if i % N_MIXERS == 0:
            m, s = hgrn2_mixer(x, hgrn_state[j], lb_all[i], w['hgrn_w_in'][j], w['hgrn_norm_g'][j], w['hgrn_w_out'][j])
            new_h.append(s.astype(x.dtype))
        else:
            m, c = conv_mixer(x, conv_cache[j], w['conv_w_pw1'][j], w['conv_b_pw1'][j], w['conv_w_dw'][j],
                              w['conv_b_dw'][j], w['conv_ln_g'][j], w['conv_ln_b'][j],
                              w['conv_w_pw2'][j], w['conv_b_pw2'][j])
            new_c.append(c)
        x = layer_norm(ALPHA * x + m, w['ln_g'][i, 1], w['ln_b'][i, 1])
        x = layer_norm(ALPHA * x + 0.5 * swiglu(x, w['ffn_w_gate'][i, 1], w['ffn_w_up'][i, 1], w['ffn_w_down'][i, 1]),
                       w['ln_g'][i, 2], w['ln_b'][i, 2])
    return x, jnp.stack(new_h), jnp.stack(new_c)


def setup_inputs(seed: int = 0) -> dict:
    key = jax.random.key(seed)
    ks = jax.random.split(key, 24)

    def nrm(k, shape, scale):
        return jax.random.normal(k, shape, jnp.float32) * scale

    return {
        'x_prompt': nrm(ks[0], (BATCH, SEQ, D_MODEL), 1.0),
        'x_sample': nrm(ks[1], (DEC_BATCH, DEC_SEQ, D_MODEL), 1.0),
        'state_hgrn': nrm(ks[2], (N_A, DEC_BATCH, N_HEADS, HEAD_K, HEAD_V), 0.5),
        'cache_conv': nrm(ks[3], (N_B, DEC_BATCH, CONV_W - 1, D_MODEL), 0.5),
        'ffn_w_gate': nrm(ks[4], (DEPTH, 2, D_MODEL, D_FF), D_MODEL ** -0.5),
        'ffn_w_up': nrm(ks[5], (DEPTH, 2, D_MODEL, D_FF), D_MODEL ** -0.5),
        'ffn_w_down': nrm(ks[6], (DEPTH, 2, D_FF, D_MODEL), BETA * D_FF ** -0.5),
        'ln_g': 1.0 + nrm(ks[7], (DEPTH, 3, D_MODEL), 0.02),
        'ln_b': nrm(ks[8], (DEPTH, 3, D_MODEL), 0.02),
        'hgrn_w_in': nrm(ks[9], (N_A, D_MODEL, 2 * D_KEY + 2 * D_MODEL), D_MODEL ** -0.5),
        'hgrn_lb': nrm(ks[10], (DEPTH + 1, D_KEY), 1.0),
        'hgrn_norm_g': 1.0 + nrm(ks[11], (N_A, HEAD_V), 0.02),
        'hgrn_w_out': nrm(ks[12], (N_A, D_MODEL, D_MODEL), BETA * D_MODEL ** -0.5),
        'conv_w_pw1': nrm(ks[13], (N_B, D_MODEL, 2 * D_MODEL), D_MODEL ** -0.5),
        'conv_b_pw1': nrm(ks[14], (N_B, 2 * D_MODEL), 0.02),
        'conv_w_dw': nrm(ks[15], (N_B, CONV_W, D_MODEL), CONV_W ** -0.5),
        'conv_b_dw': nrm(ks[16], (N_B, D_MODEL), 0.02),
        'conv_ln_g': 1.0 + nrm(ks[17], (N_B, D_MODEL), 0.02),
        'conv_ln_b': nrm(ks[18], (N_B, D_MODEL), 0.02),
        'conv_w_pw2': nrm(ks[19], (N_B, D_MODEL, D_MODEL), BETA * D_MODEL ** -0.5),
        'conv_b_pw2': nrm(ks[20], (N_B, D_MODEL), 0.02),
    }


def reference(x_prompt, x_sample, state_hgrn, cache_conv, ffn_w_gate, ffn_w_up, ffn_w_down, ln_g, ln_b,
              hgrn_w_in, hgrn_lb, hgrn_norm_g, hgrn_w_out, conv_w_pw1, conv_b_pw1, conv_w_dw, conv_b_dw,
              conv_ln_g, conv_ln_b, conv_w_pw2, conv_b_pw2):
    w = {'ffn_w_gate': ffn_w_gate, 'ffn_w_up': ffn_w_up, 'ffn_w_down': ffn_w_down, 'ln_g': ln_g, 'ln_b': ln_b,
         'hgrn_w_in': hgrn_w_in, 'hgrn_lb': hgrn_lb, 'hgrn_norm_g': hgrn_norm_g, 'hgrn_w_out': hgrn_w_out,
         'conv_w_pw1': conv_w_pw1, 'conv_b_pw1': conv_b_pw1, 'conv_w_dw': conv_w_dw, 'conv_b_dw': conv_b_dw,
         'conv_ln_g': conv_ln_g, 'conv_ln_b': conv_ln_b, 'conv_w_pw2': conv_w_pw2, 'conv_b_pw2': conv_b_pw2}
    b_p = x_prompt.shape[0]
    h0 = jnp.zeros((N_A, b_p, N_HEADS, HEAD_K, HEAD_V), x_prompt.dtype)
    c0 = jnp.zeros((N_B, b_p, CONV_W - 1, D_MODEL), x_prompt.dtype)
    y_prompt, hgrn_state_prompt, conv_cache_prompt = trunk(x_prompt, h0, c0, w)
    y_sample, hgrn_state_sample, conv_cache_sample = trunk(x_sample, state_hgrn, cache_conv, w)
    return (y_prompt, y_sample, hgrn_state_prompt, hgrn_state_sample, conv_cache_prompt, conv_cache_sample)
```

```python
import numpy as np
from contextlib import ExitStack
import concourse.bass as bass
import concourse.mybir as mybir
from concourse.bass_utils import run_bass_kernel_spmd

F32 = mybir.dt.float32
BF16 = mybir.dt.bfloat16
U32 = mybir.dt.uint32
AF = mybir.ActivationFunctionType
ALU = mybir.AluOpType

NCORES = 8
D = 1024
TP = 2048
TS = 16
T = TP + 2 * TS
ALPHA = 2.0 ** 0.5
LN_EPS = 1e-5
TILES = [(0, 512), (512, 512), (1024, 512), (1536, 512), (2048, 32)]
CW = 31

LNG, LNB, LBO, NGO, BAO, BGO, WDWO, BDWO, CLGO, CLBO, B2O, NV = 0, 48, 96, 120, 121, 129, 137, 385, 393, 401, 409, 417
NDVE_TAPS = 18


class Res:
    __slots__ = ("w", "r")

    def __init__(self):
        self.w = {}
        self.r = {}


class Q:
    def __init__(self, name):
        self.name = name
        self.cnt = 0
        self.items = []
        self.seen = {}


class Kb:
    def __init__(self):
        self.res = {}
        self.dcnt = {}
        self.deferred = []
        self.q = {n: Q(n) for n in ("sp", "act", "dve", "pool", "pe")}
        self.arena = Res()
        self.semkeys = ["sp", "act", "dve", "pool", "pe"]

    def R(self, key):
        r = self.res.get(key)
        if r is None:
            r = self.res[key] = Res()
        return r

    def _wait(self, q, reads, writes):
        need = {}
        for r in reads:
            for k, v in r.w.items():
                if need.get(k, 0) < v:
                    need[k] = v
        for w in writes:
            for k, v in w.w.items():
                if k != q.name and need.get(k, 0) < v:
                    need[k] = v
            for k, v in w.r.items():
                if k != q.name and need.get(k, 0) < v:
                    need[k] = v
        for k, v in need.items():
            if q.seen.get(k, 0) < v:
                q.items.append(("w", k, v))
                q.seen[k] = v

    def op(self, qn, fn, reads=(), writes=(), arena=False):
        q = self.q[qn]
        reads = list(reads)
        if arena:
            reads.append(self.arena)
        self._wait(q, reads, writes)
        q.cnt += 1
        q.items.append(("o", fn))
        for r in reads:
            r.r[q.name] = q.cnt
        for w in writes:
            w.r = {}
            w.w = {q.name: q.cnt}

    def dma(self, qn, semkey, fn, reads=(), writes=(), arena=False):
        q = self.q[qn]
        reads = list(reads)
        if arena:
            reads.append(self.arena)
        if semkey not in self.dcnt:
            self.dcnt[semkey] = 0
            self.semkeys.append(semkey)
        self._wait(q, reads, writes)
        self.dcnt[semkey] += 16
        c = self.dcnt[semkey]
        q.items.append(("d", fn, semkey))
        for r in reads:
            r.r[semkey] = c
        for w in writes:
            w.r = {}
            w.w = {semkey: c}

    def fence(self, fn):
        self.op("dve", fn, reads=(), writes=[self.arena])

    def defer(self, n, fn):
        self.deferred.append([n, fn])

    def tick(self):
        run = []
        keep = []
        for it in self.deferred:
            it[0] -= 1
            (run if it[0] <= 0 else keep).append(it)
        self.deferred = keep
        for it in run:
            it[1]()

    def flush(self):
        while self.deferred:
            self.tick()


def act(out, in_, func, bias=None, scale=None):
    kw = {}
    if bias is not None:
        kw["bias"] = bias
    if scale is not None:
        kw["scale"] = scale
    return lambda e: e.activation(out, in_, func, **kw)


def tt(out, a, b, op):
    return lambda e: e.tensor_tensor(out, a, b, op)


def ts(out, a, s1, s2, op0, op1=None):
    if op1 is None:
        return lambda e: e.tensor_scalar(out, a, s1, None, op0)
    return lambda e: e.tensor_scalar(out, a, s1, s2, op0, op1)


def stt(out, a, s, b, op0, op1):
    return lambda e: e.scalar_tensor_tensor(out, a, s, b, op0, op1)


def build_nc(nsub=6):
    nc = bass.Bass("TRN2", target_bir_lowering=False)
    dt_in = lambda name, shape, dt=F32: nc.dram_tensor(name, shape, dt, kind="ExternalInput").ap()
    dt_out = lambda name, shape: nc.dram_tensor(name, shape, F32, kind="ExternalOutput").ap()
    xT = dt_in("xT", [D, T])
    st_in = dt_in("st", [2, 8, 128, 128])
    ccT = dt_in("ccT", [2, D, 30])
    w_gate = dt_in("wg", [2, 2, D, 4 * D])
    w_up = dt_in("wu", [2, 2, D, 4 * D])
    w_down = dt_in("wd", [2, 2, 4 * D, D])
    w_in = dt_in("win", [D, 4 * D])
    w_out = dt_in("wout", [D, D])
    w_pw1 = dt_in("pw1", [D, 2 * D])
    w_pw2 = dt_in("pw2", [D, D])
    vecs_d = dt_in("vecs", [128, NV])
    tri_d = dt_in("tri", [128, 512], U32)
    tri16_d = dt_in("tri16", [128, 32], U32)
    ident_d = dt_in("ident", [128, 128])
    yT = dt_out("yT", [D, T])
    so_p = dt_out("so_p", [8, 128, 128])
    so_s = dt_out("so_s", [2, 8, 128, 128])
    cco_p = dt_out("cco_p", [D, 30])
    cco_s = dt_out("cco_s", [2, D, 30])

    K = Kb()
    R = K.R
    es = ExitStack()
    with es:
        sb = lambda name, shape, dt: es.enter_context(nc.sbuf_tensor(name, shape, dt))
        xres = sb("xres", [128, 8, T], F32)
        xbf = sb("xbf", [128, 8, T], BF16)
        A = [sb(f"A{i}", [128, 8, 512], BF16) for i in range(4)]
        Dm = [sb(f"D{i}", [128, 4, 1024], BF16) for i in range(2)]
        vecs = sb("vecs_sb", [128, NV], F32)
        vecsA = sb("vecsA", [128, 96], F32)
        lbv = sb("lbv", [128, 16], F32)
        e3 = sb("e3", [128, 32], F32)
        cst = sb("cst", [128, 4], F32)
        tri = sb("tri_sb", [128, 512], U32)
        tri16 = sb("tri16_sb", [128, 32], U32)
        ones = sb("ones", [128, 128], BF16)
        ident = sb("ident_sb", [128, 128], BF16)
        scm16 = sb("scm16", [128, 64], BF16)
        fz = sb("fz", [128, 2], F32)
        zb = sb("zb", [128, 8, 256], BF16)
        zsq = sb("zsq", [128, 8, 256], BF16)
        mean = sb("mean", [128, 256], F32)
        var = sb("var", [128, 256], F32)
        sd = sb("sd", [128, 256], F32)
        rstd = sb("rstd", [128, 256], F32)
        ARENA_F = 11264
        arena = sb("arena", [128, ARENA_F], F32)
        PS = [es.enter_context(nc.psum_tensor(f"ps{i}", [128, 512], F32)) for i in range(7)]
        PST = es.enter_context(nc.psum_tensor("pst", [128, 1024], BF16))

        def af(off_b, nelem):
            assert off_b % 4 == 0 and off_b // 4 + nelem <= ARENA_F
            return arena[:, off_b // 4: off_b // 4 + nelem]

        def ab(off_b, nelem):
            assert off_b % 4 == 0 and nelem % 2 == 0 and off_b // 4 + nelem // 2 <= ARENA_F
            return arena[:, off_b // 4: off_b // 4 + nelem // 2].bitcast(BF16)

        SG = [af(0, 512), af(2048, 512)]
        H = [ab(4096, 2048).rearrange("p (j t) -> p j t", t=512), ab(8192, 2048).rearrange("p (j t) -> p j t", t=512)]
        HS = []
        for x in range(2):
            b0 = x * 14336
            HS.append(dict(qs=af(b0, 512), fb=af(b0 + 2048, 512), km=af(b0 + 4096, 512), bb=af(b0 + 6144, 512), gs=af(b0 + 8192, 512),
                           q_rel=ab(b0 + 10240, 512), k_rel=ab(b0 + 11264, 512), k_relT=ab(b0 + 12288, 512), scm=ab(b0 + 13312, 512),
                           Sp_bf=ab(43008 + 256 * x, 128), tmpU=af(43520 + 512 * x, 128), dd=af(44544 + 48 * x, 12)))
        obf = ab(28672, 2048).rearrange("p (j t) -> p j t", t=512)
        v_tok = ab(32768, 2048).rearrange("p (c v) -> p c v", v=512)
        S_p = af(36864, 512).rearrange("p (h v) -> p h v", v=128)
        S_s = [af(38912, 512).rearrange("p (h v) -> p h v", v=128), af(40960, 512).rearrange("p (h v) -> p h v", v=128)]
        U = af(0, 8 * 542).rearrange("p (c t) -> p c t", t=542)
        ybuf = af(17408, 4096).rearrange("p (c t) -> p c t", t=512)
        U_s = af(33792, 8 * 2 * 46).rearrange("p (c s t) -> p c s t", s=2, t=46)
        sgt2 = [af(36736, 512), af(38784, 512)]

        RA = [R(f"A{i}") for i in range(4)]
        RD = [R(f"D{i}") for i in range(2)]
        RPS = [R(f"ps{i}") for i in range(7)]
        RPST = R("pst")
        RXB = [R(f"xbf{t}") for t in range(5)]
        RXR = [[R(f"xr{t}_{c}") for c in range(8)] for t in range(5)]
        Rvec, RvecA, Rlb, Rcst, Rtri, Rones, Rident = R("vecs"), R("vecsA"), R("lbv"), R("cst"), R("tri"), R("ones"), R("ident")
        Rzb, Rzsq, Rmean, Rvar, Rsd, Rrstd = R("zb"), R("zsq"), R("mean"), R("var"), R("sd"), R("rstd")

        vc = lambda col: vecs[:, col:col + 1]

        def mm(out_ap, pairs, reads, bank, arena_=False):
            def fn(e, pairs=pairs, out_ap=out_ap):
                n = len(pairs)
                ins = None
                for i, (l, r) in enumerate(pairs):
                    ins = e.matmul(out_ap, l, r, start=(i == 0), stop=(i == n - 1))
                return ins
            K.op("pe", fn, reads=reads, writes=[bank], arena=arena_)

        def loadA(slot, w2d, col0):
            src = w2d[:, col0:col0 + 512].rearrange("(k p) c -> p k c", p=128)
            K.dma("pool", f"wA{slot}", lambda e: e.dma_start(out=A[slot][:], in_=src), writes=[RA[slot]])

        def loadD(slot, w2d, row0):
            src = w2d[row0:row0 + 512, :].rearrange("(j p) c -> p j c", p=128)
            K.dma("pool", f"wD{slot}", lambda e: e.dma_start(out=Dm[slot][:], in_=src), writes=[RD[slot]])

        def ffn_loads(l, s, g):
            par = g % 2
            return lambda: (loadA(2 * par, w_gate[l, s], g * 512), loadA(2 * par + 1, w_up[l, s], g * 512),
                            loadD(par, w_down[l, s], g * 512))

        nop = lambda: None

        xTv = xT.rearrange("(c p) t -> p c t", p=128)
        yTv = yT.rearrange("(c p) t -> p c t", p=128)
        K.dma("sp", "iv", lambda e: e.dma_start(out=vecs[:], in_=vecs_d), writes=[Rvec])
        K.dma("sp", "it", lambda e: e.dma_start(out=tri[:], in_=tri_d), writes=[Rtri])
        K.dma("sp", "it16", lambda e: e.dma_start(out=tri16[:], in_=tri16_d), writes=[R("tri16")])
        K.dma("pool", "ii", lambda e: e.dma_start(out=ident[:], in_=ident_d), writes=[Rident])
        for ti, (t0, n) in enumerate(TILES):
            K.dma("sp", f"ix{ti}", lambda e, t0=t0, n=n: e.dma_start(out=xres[:, :, t0:t0 + n], in_=xTv[:, :, t0:t0 + n]),
                  writes=RXR[ti])
            K.dma("pool", f"ib{ti}", lambda e, t0=t0, n=n: e.dma_start(out=xbf[:, :, t0:t0 + n], in_=xTv[:, :, t0:t0 + n]),
                  writes=[RXB[ti]])
            if ti == 0:
                ffn_loads(0, 0, 0)()
                ffn_loads(0, 0, 1)()
        K.op("dve", lambda e: e.memset(ones[:], 1.0), writes=[Rones])
        K.op("dve", lambda e: e.memset(cst[:, 0:1], LN_EPS), writes=[Rcst])
        K.op("dve", lambda e: e.memset(cst[:, 1:2], float(np.log(128.0 ** -0.5))), writes=[Rcst])
        K.op("dve", lambda e: e.memset(cst[:, 2:3], 1.0), writes=[Rcst])
        K.op("dve", lambda e: e.memset(scm16[:], 0.0), writes=[R("scm16")])
        K.op("dve", ts(vecsA[:], vecs[:, 0:96], ALPHA, None, ALU.mult), reads=[Rvec], writes=[RvecA])
        K.op("act", act(e3[:, 0:24], vecs[:, LBO:LBO + 24], AF.Exp), reads=[Rvec], writes=[R("e3")])
        K.op("dve", tt(e3[:, 24:32], e3[:, 0:8], e3[:, 8:16], ALU.add), reads=[R("e3")], writes=[R("e3b")])
        K.op("dve", tt(e3[:, 24:32], e3[:, 24:32], e3[:, 16:24], ALU.add), reads=[R("e3"), R("e3b")], writes=[R("e3b")])
        K.op("dve", lambda e: e.reciprocal(e3[:, 24:32], e3[:, 24:32]), reads=[R("e3b")], writes=[R("e3b")])
        K.op("dve", tt(lbv[:, 0:8], e3[:, 0:8], e3[:, 24:32], ALU.mult), reads=[R("e3"), R("e3b")], writes=[Rlb])
        K.op("dve", ts(lbv[:, 8:16], lbv[:, 0:8], -1.0, 1.0, ALU.mult, ALU.add), reads=[Rlb], writes=[Rlb])
        for ti, (t0, n) in enumerate(TILES):
            K.op("dve", ts(xres[:, :, t0:t0 + n], xres[:, :, t0:t0 + n], ALPHA, None, ALU.mult), reads=RXR[ti], writes=RXR[ti])

        def ln_pre(zap, n, rz, ar):
            K.op("act", act(zsq[:, :, :n], zap, AF.Square), reads=rz, writes=[Rzsq], arena=ar)
            K.op("dve", lambda e: e.tensor_copy(zb[:, :, :n], zap), reads=rz, writes=[Rzb], arena=ar)

        def ln_post(zap, n, rz, ar, epilogue):
            mm(PS[5][:, :n], [(ones[:], zb[:, c, :n]) for c in range(8)], reads=[Rzb, Rones], bank=RPS[5])
            mm(PS[6][:, :n], [(ones[:], zsq[:, c, :n]) for c in range(8)], reads=[Rzsq, Rones], bank=RPS[6])
            K.op("dve", ts(mean[:, :n], PS[5][:, :n], 1.0 / D, None, ALU.mult), reads=[RPS[5]], writes=[Rmean])
            K.op("dve", tt(var[:, :n], mean[:, :n], mean[:, :n], ALU.mult), reads=[Rmean], writes=[Rvar])
            K.op("dve", stt(var[:, :n], PS[6][:, :n], 1.0 / D, var[:, :n], ALU.mult, ALU.subtract), reads=[RPS[6], Rvar], writes=[Rvar])
            K.op("act", act(sd[:, :n], var[:, :n], AF.Ln, bias=cst[:, 0:1]), reads=[Rvar, Rcst], writes=[Rsd])
            K.op("act", act(rstd[:, :n], sd[:, :n], AF.Exp, scale=-0.5), reads=[Rsd], writes=[Rrstd])
            K.op("dve", tt(zap, zap, mean[:, :n].unsqueeze(1).to_broadcast([128, 8, n]), ALU.subtract),
                 reads=rz + [Rmean], writes=rz, arena=ar)
            K.op("dve", tt(zap, zap, rstd[:, :n].unsqueeze(1).to_broadcast([128, 8, n]), ALU.mult),
                 reads=rz + [Rrstd], writes=rz, arena=ar)
            epilogue()

        def halves(t0, n):
            return [(t0, 256), (t0 + 256, 256)] if n == 512 else [(t0, n)]

        def res_ln(ti, lnidx, final):
            t0, n = TILES[ti]
            rz = RXR[ti]
            for (h0, nh) in halves(t0, n):
                zap = xres[:, :, h0:h0 + nh]
                ln_pre(zap, nh, rz, False)

                def epi(h0=h0, nh=nh):
                    for c in range(8):
                        gcol, bcol = LNG + lnidx * 8 + c, LNB + lnidx * 8 + c
                        zc = xres[:, c, h0:h0 + nh]
                        if not final:
                            K.op("act", act(xbf[:, c, h0:h0 + nh], zc, AF.Identity, bias=vc(bcol), scale=vc(gcol)),
                                 reads=[RXR[ti][c], Rvec], writes=[RXB[ti]])
                            K.op("act", act(zc, zc, AF.Identity, bias=vecsA[:, bcol:bcol + 1], scale=vecsA[:, gcol:gcol + 1]),
                                 reads=[RXR[ti][c], RvecA], writes=[RXR[ti][c]])
                        else:
                            K.op("act", act(zc, zc, AF.Identity, bias=vc(bcol), scale=vc(gcol)),
                                 reads=[RXR[ti][c], Rvec], writes=[RXR[ti][c]])
                ln_post(zap, nh, rz, False, epi)
            if final:
                out_tile(ti)

        def out_tile(ti):
            t0, n = TILES[ti]
            K.dma("sp", "oy", lambda e: e.dma_start(out=yTv[:, :, t0:t0 + n], in_=xres[:, :, t0:t0 + n]), reads=RXR[ti])

        def ffn_phase(l, s, lnidx, final, nxt):
            K.fence(lambda e: e.memset(fz[:, 0:1], 0.0))
            unit = 0
            for g in range(8):
                par = g % 2
                ag, au, dw = A[2 * par], A[2 * par + 1], Dm[par]
                rag, rau, rdw = RA[2 * par], RA[2 * par + 1], RD[par]
                for ti, (t0, n) in enumerate(TILES):
                    up = unit % 2
                    unit += 1
                    hb = H[up]
                    RH = [R(f"h{up}_{j}") for j in range(4)]
                    for j in range(4):
                        gi, ui = (j % 2) * 2, (j % 2) * 2 + 1
                        mm(PS[gi][:, :n], [(ag[:, k, j * 128:(j + 1) * 128], xbf[:, k, t0:t0 + n]) for k in range(8)],
                           reads=[rag, RXB[ti]], bank=RPS[gi])
                        mm(PS[ui][:, :n], [(au[:, k, j * 128:(j + 1) * 128], xbf[:, k, t0:t0 + n]) for k in range(8)],
                           reads=[rau, RXB[ti]], bank=RPS[ui])
                        sgb = SG[j % 2]
                        rsg = R(f"sg{j % 2}")
                        K.op("act", act(sgb[:, :n], PS[gi][:, :n], AF.Silu), reads=[RPS[gi]], writes=[rsg], arena=True)
                        K.op("dve", tt(hb[:, j, :n], sgb[:, :n], PS[ui][:, :n], ALU.mult), reads=[rsg, RPS[ui]], writes=[RH[j]], arena=True)

                    def down(g=g, ti=ti, t0=t0, n=n, hb=hb, RH=RH, dw=dw, rdw=rdw):
                        for d in range(8):
                            bi = 4 + d % 3
                            mm(PS[bi][:, :n], [(dw[:, j, d * 128:(d + 1) * 128], hb[:, j, :n]) for j in range(4)],
                               reads=[rdw] + RH, bank=RPS[bi], arena_=True)
                            xr = xres[:, d, t0:t0 + n]
                            K.op("dve", stt(xr, PS[bi][:, :n], 0.5, xr, ALU.mult, ALU.add), reads=[RPS[bi], RXR[ti][d]], writes=[RXR[ti][d]])
                        if g == 7:
                            K.defer(1, lambda: res_ln(ti, lnidx, final))
                    K.tick()
                    if ti == 0 and g >= 1:
                        if g + 1 < 8:
                            ffn_loads(l, s, g + 1)()
                        else:
                            nxt[0]()
                    K.defer(1, down)
            K.flush()
            nxt[1]()

        def hgrn_phase(nxt):
            K.fence(lambda e: e.memset(fz[:, 0:1], 0.0))
            for x in range(2):
                K.op("dve", lambda e, x=x: e.memset(HS[x]["scm"][:], 0.0), writes=[R(f"scm{x}")], arena=True)
            K.op("dve", ts(e3[:, 0:8], lbv[:, 8:16], 0.5, None, ALU.mult), reads=[Rlb, R("e3"), R("e3b")], writes=[R("e3")])
            K.op("dve", ts(e3[:, 8:16], lbv[:, 8:16], -0.5, None, ALU.mult), reads=[Rlb], writes=[R("e3")])
            K.op("dve", ts(e3[:, 16:24], lbv[:, 0:8], 0.5, 0.5, ALU.mult, ALU.add), reads=[Rlb], writes=[R("e3")])
            Rhl = R("e3")
            Rvt = R("v_tok")
            Robf = [R(f"obf{j}") for j in range(4)]
            RSP, RSS = R("S0"), [R("S1"), R("S2")]

            def head(hg, ti, j, x):
                t0, n = TILES[ti]
                nch, cl = (4, 128) if n == 512 else (2, 16)
                mid = cl // 2
                prompt = n == 512
                h = hg * 4 + j
                cs = slice(j * 128, (j + 1) * 128)
                B = HS[x]
                qs, fb, km, bb, gs = B["qs"], B["fb"], B["km"], B["bb"], B["gs"]
                q_rel, k_rel, k_relT, Sp_bf, tmpU, dd = B["q_rel"], B["k_rel"], B["k_relT"], B["Sp_bf"], B["tmpU"], B["dd"]
                on_, hv, osq = qs, km, k_relT
                Rq, Rf, Rkm, Rbb, Rgs = R(f"qs{x}"), R(f"fb{x}"), R(f"km{x}"), R(f"bb{x}"), R(f"gs{x}")
                Rqr, Rkr, RkT, Rdd, RSp, RtU = R(f"q_rel{x}"), R(f"k_rel{x}"), R(f"k_relT{x}"), R(f"dd{x}"), R(f"Sp{x}"), R(f"tU{x}")
                RS = [RSP] * 4 if prompt else RSS
                Sst = [S_p] * 4 if prompt else S_s
                msk, rmsk = (tri, Rtri) if prompt else (tri16, R("tri16"))
                scb, rscb = (B["scm"], R(f"scm{x}")) if prompt else (scm16[:, 32 * x:32 * x + 32], R(f"scm16_{x}"))
                bq, bf_, bo = x, 2 + x, 4 + x
                WQ, WF, WV, WG = A
                xin = lambda k: xbf[:, k, t0:t0 + n]
                mm(PS[bq][:, :n], [(WQ[:, k, cs], xin(k)) for k in range(8)], reads=[RXB[ti], RA[0]], bank=RPS[bq])
                mm(PS[bf_][:, :n], [(WF[:, k, cs], xin(k)) for k in range(8)], reads=[RXB[ti], RA[1]], bank=RPS[bf_])
                K.op("act", act(qs[:, :n], PS[bq][:, :n], AF.Silu), reads=[RPS[bq]], writes=[Rq], arena=True)
                mm(PS[bq][:, :n], [(WG[:, k, cs], xin(k)) for k in range(8)], reads=[RXB[ti], RA[3]], bank=RPS[bq])
                K.op("act", act(fb[:, :n], PS[bf_][:, :n], AF.Tanh, scale=0.5), reads=[RPS[bf_]], writes=[Rf], arena=True)
                K.op("act", act(gs[:, :n], PS[bq][:, :n], AF.Silu), reads=[RPS[bq]], writes=[Rgs], arena=True)
                yield
                K.op("dve", ts(km[:, :n], fb[:, :n], e3[:, 8 + h:9 + h], e3[:, h:h + 1], ALU.mult, ALU.add), reads=[Rf, Rhl], writes=[Rkm], arena=True)
                K.op("act", act(fb[:, :n], fb[:, :n], AF.Ln, bias=e3[:, 16 + h:17 + h], scale=e3[:, h:h + 1]), reads=[Rf, Rhl], writes=[Rf], arena=True)
                for c in range(nch):
                    sl = slice(c * cl, (c + 1) * cl)
                    K.op("dve", lambda e, sl=sl, cl=cl: e.tensor_tensor_scan(bb[:, sl], cst[:, 2:3].to_broadcast([128, cl]), fb[:, sl], 0.0,
                                                                             ALU.mult, ALU.add),
                         reads=[Rf, Rcst], writes=[Rbb], arena=True)
                b3 = bb[:, :n].rearrange("p (c t) -> p c t", t=cl)
                f3 = fb[:, :n].rearrange("p (c t) -> p c t", t=cl)
                K.op("act", act(dd[:, 0:nch], b3[:, :, cl - 1], AF.Exp), reads=[Rbb], writes=[Rdd], arena=True)
                K.op("act", act(dd[:, 4:4 + nch], b3[:, :, mid], AF.Exp), reads=[Rbb], writes=[Rdd], arena=True)
                K.op("dve", tt(f3, b3, b3[:, :, mid:mid + 1].to_broadcast([128, nch, cl]), ALU.subtract), reads=[Rbb], writes=[Rf], arena=True)
                K.op("act", act(dd[:, 8:8 + nch], f3[:, :, cl - 1], AF.Exp), reads=[Rf], writes=[Rdd], arena=True)
                K.op("act", act(bb[:, :n], fb[:, :n], AF.Exp, bias=cst[:, 1:2]), reads=[Rf, Rcst], writes=[Rbb], arena=True)
                K.op("act", act(fb[:, :n], fb[:, :n], AF.Exp, scale=-1.0), reads=[Rf], writes=[Rf], arena=True)
                K.op("dve", tt(q_rel[:, :n], qs[:, :n], bb[:, :n], ALU.mult), reads=[Rq, Rbb], writes=[Rqr], arena=True)
                K.op("dve", tt(k_rel[:, :n], km[:, :n], fb[:, :n], ALU.mult), reads=[Rkm, Rf], writes=[Rkr], arena=True)

                def trfn(e):
                    ins = None
                    for c in range(nch):
                        ins = e.transpose(PST[:cl, c * 128:(c + 1) * 128], k_rel[:, c * cl:(c + 1) * cl], ident[:])
                    return ins
                K.op("pe", trfn, reads=[Rkr, Rident], writes=[RPST], arena=True)
                K.op("act", act(k_relT[:cl, :nch * 128], PST[:cl, :nch * 128], AF.Copy), reads=[RPST], writes=[RkT], arena=True)

                def scfn(e):
                    ins = None
                    for c in range(nch):
                        sl = slice(c * cl, (c + 1) * cl)
                        ins = e.matmul(PS[bf_][:cl, sl], k_rel[:, sl], q_rel[:, sl], start=True, stop=True)
                    return ins
                K.op("pe", scfn, reads=[Rkr, Rqr], writes=[RPS[bf_]], arena=True)
                K.op("dve", lambda e: e.copy_predicated(scb[:cl, :nch * cl], msk[:cl, :nch * cl], PS[bf_][:cl, :nch * cl]),
                     reads=[RPS[bf_], rmsk], writes=[rscb], arena=True)
                yield
                for c in range(nch):
                    sl = slice(c * cl, (c + 1) * cl)
                    Sh = Sst[c][:, j, :]
                    K.op("act", act(Sp_bf[:], Sh, AF.Identity, scale=dd[:, 4 + c:5 + c]), reads=[RS[c], Rdd], writes=[RSp], arena=True)
                    mm(PS[bo][:, sl], [(v_tok[:cl, c, cs], scb[:cl, sl]), (Sp_bf[:], q_rel[:, sl])],
                       reads=[Rvt, rscb, RSp, Rqr], bank=RPS[bo], arena_=True)
                    mm(PS[6][:, :128], [(k_relT[:cl, c * 128:(c + 1) * 128], v_tok[:cl, c, cs])], reads=[RkT, Rvt], bank=RPS[6], arena_=True)
                    K.op("act", act(tmpU[:], PS[6][:, :128], AF.Identity, scale=dd[:, 8 + c:9 + c]), reads=[RPS[6], Rdd], writes=[RtU], arena=True)
                    K.op("dve", stt(Sh, Sh, dd[:, c:c + 1], tmpU[:], ALU.mult, ALU.add), reads=[RS[c], Rdd, RtU], writes=[RS[c]], arena=True)
                    yield
                K.op("act", act(osq[:, :n], PS[bo][:, :n], AF.Square), reads=[RPS[bo], RkT], writes=[RkT], arena=True)
                mm(PS[6][:, :n], [(ones[:], osq[:, :n])], reads=[RkT, Rones], bank=RPS[6], arena_=True)
                K.op("act", act(hv[:, :n], PS[6][:, :n], AF.Ln, bias=cst[:, 0:1], scale=1.0 / 128), reads=[RPS[6], Rcst, Rkm], writes=[Rkm], arena=True)
                K.op("act", act(hv[:, :n], hv[:, :n], AF.Exp, scale=-0.5), reads=[Rkm], writes=[Rkm], arena=True)
                K.op("dve", tt(on_[:, :n], PS[bo][:, :n], hv[:, :n], ALU.mult), reads=[RPS[bo], Rkm, Rq], writes=[Rq], arena=True)
                K.op("dve", stt(obf[:, j, :n], on_[:, :n], vc(NGO), gs[:, :n], ALU.mult, ALU.mult), reads=[Rq, Rgs, Rvec], writes=[Robf[j]], arena=True)

            for hg in range(2):
                WV = A[2]
                WO, RWO = Dm[hg], RD[hg]
                K.op("dve", lambda e: e.memset(S_p[:], 0.0), writes=[RSP], arena=True)
                for s_ in range(2):
                    K.dma("sp", f"is{hg}{s_}", lambda e, s_=s_, hg=hg: e.dma_start(out=S_s[s_][:], in_=st_in[s_, hg * 4:(hg + 1) * 4].rearrange("h k v -> k h v")),
                          writes=[RSS[s_]], arena=True)
                for ti, (t0, n) in enumerate(TILES):
                    nch, cl = (4, 128) if n == 512 else (2, 16)
                    for c in range(nch):
                        bv = 2 + c % 2
                        mm(PS[bv][:cl, :], [(xbf[:, k, t0 + c * cl:t0 + (c + 1) * cl], WV[:, k, :]) for k in range(8)],
                           reads=[RXB[ti], RA[2]], bank=RPS[bv])
                        K.op("act", act(v_tok[:cl, c, :], PS[bv][:cl, :], AF.Copy), reads=[RPS[bv]], writes=[Rvt], arena=True)
                    for pair in ((0, 1), (2, 3)):
                        gens = [head(hg, ti, j, j % 2) for j in pair]
                        while gens:
                            for g_ in list(gens):
                                try:
                                    next(g_)
                                except StopIteration:
                                    gens.remove(g_)
                    for d in range(8):
                        bi = d % 4
                        mm(PS[bi][:, :n], [(WO[:, j, d * 128:(d + 1) * 128], obf[:, j, :n]) for j in range(4)],
                           reads=[RWO] + Robf, bank=RPS[bi], arena_=True)
                        xr = xres[:, d, t0:t0 + n]
                        K.op("dve", tt(xr, PS[bi][:, :n], xr, ALU.add), reads=[RPS[bi], RXR[ti][d]], writes=[RXR[ti][d]])
                    if hg == 1:
                        res_ln(ti, 1, False)
                K.dma("sp", f"os{hg}", lambda e, hg=hg: e.dma_start(out=so_p[hg * 4:(hg + 1) * 4].rearrange("h k v -> k h v"), in_=S_p[:]), reads=[RSP], arena=True)
                for s_ in range(2):
                    K.dma("sp", f"os{hg}{s_}", lambda e, s_=s_, hg=hg: e.dma_start(out=so_s[s_, hg * 4:(hg + 1) * 4].rearrange("h k v -> k h v"), in_=S_s[s_][:]),
                          reads=[RSS[s_]], arena=True)
                if hg == 0:
                    loadA(0, w_in, 512)
                    loadA(1, w_in, 1024 + 512)
                    loadA(2, w_in, 2048 + 512)
                    loadA(3, w_in, 3072 + 512)
                else:
                    nxt[0]()
                    nxt[1]()

        hgrn_first = lambda: (loadA(0, w_in, 0), loadA(1, w_in, 1024), loadD(0, w_out, 0))
        hgrn_second = lambda: (loadA(2, w_in, 2048), loadA(3, w_in, 3072), loadD(1, w_out, 512))

        def conv_phase(nxt):
            K.fence(lambda e: e.memset(fz[:, 0:1], 0.0))
            RU = [R(f"U{c}") for c in range(8)]
            RUs = [R(f"Us{c}") for c in range(8)]
            RY = [R(f"y{c}") for c in range(8)]
            Rsgt = [R("sgt0"), R("sgt1")]
            K.op("dve", lambda e: e.memset(U[:, :, 0:30], 0.0), writes=RU, arena=True)
            for s_ in range(2):
                K.dma("sp", f"ic{s_}", lambda e, s_=s_: e.dma_start(out=U_s[:, :, s_, 0:30], in_=ccT[s_].rearrange("(c p) w -> p c w", p=128)),
                      writes=RUs, arena=True)
            for ti, (t0, n) in enumerate(TILES):
                prompt = n == 512
                for cp in range(4):
                    info = []
                    for x in range(2):
                        c = 2 * cp + x
                        ai, gi = 2 * x, 2 * x + 1
                        cs = slice((c % 4) * 128, (c % 4 + 1) * 128)
                        sgt = sgt2[x]
                        mm(PS[ai][:, :n], [(A[c // 4][:, k, cs], xbf[:, k, t0:t0 + n]) for k in range(8)], reads=[RXB[ti], RA[c // 4]], bank=RPS[ai])
                        mm(PS[gi][:, :n], [(A[2 + c // 4][:, k, cs], xbf[:, k, t0:t0 + n]) for k in range(8)], reads=[RXB[ti], RA[2 + c // 4]], bank=RPS[gi])
                        K.op("act", act(sgt[:, :n], PS[gi][:, :n], AF.Sigmoid, bias=vc(BGO + c)), reads=[RPS[gi], Rvec], writes=[Rsgt[x]], arena=True)
                        if prompt:
                            ru = RU[c]
                            unew = U[:, c, 30:542]
                            pa, sg_ = PS[ai][:, :n], sgt[:, :n]
                            tap = (lambda c: (lambda w: U[:, c, w:w + 512]))(c)
                            yc = ybuf[:, c, :]
                        else:
                            ru = RUs[c]
                            unew = U_s[:, c, :, 30:46]
                            pa = PS[ai][:, :n].rearrange("p (s t) -> p s t", t=16)
                            sg_ = sgt[:, :n].rearrange("p (s t) -> p s t", t=16)
                            tap = (lambda c: (lambda w: U_s[:, c, :, w:w + 16]))(c)
                            yc = ybuf[:, c, :n].rearrange("p (s t) -> p s t", t=16)
                        K.op("dve", stt(unew, pa, vc(BAO + c), sg_, ALU.add, ALU.mult), reads=[RPS[ai], Rsgt[x], Rvec], writes=[ru], arena=True)
                        info.append((c, ru, tap, yc))
                    for w in range(CW):
                        for (c, ru, tap, yc) in info:
                            wc = vc(WDWO + c * CW + w)
                            if w == 0:
                                K.op("dve", ts(yc, tap(0), wc, vc(BDWO + c), ALU.mult, ALU.add), reads=[ru, Rvec], writes=[RY[c]], arena=True)
                            else:
                                K.op("dve", stt(yc, tap(w), wc, yc, ALU.mult, ALU.add), reads=[ru, Rvec, RY[c]], writes=[RY[c]], arena=True)
                    if prompt and ti < 3:
                        for (c, ru, tap, yc) in info:
                            K.op("act", act(U[:, c, 0:30], U[:, c, 512:542], AF.Copy), reads=[ru], writes=[ru], arena=True)
                if ti == 3:
                    K.dma("sp", "oc", lambda e: e.dma_start(out=cco_p.rearrange("(c p) w -> p c w", p=128), in_=U[:, :, 512:542]), reads=RU, arena=True)
                if not prompt:
                    for s_ in range(2):
                        K.dma("sp", "oc", lambda e, s_=s_: e.dma_start(out=cco_s[s_].rearrange("(c p) w -> p c w", p=128), in_=U_s[:, :, s_, 16:46]),
                              reads=RUs, arena=True)
                for (h0, nh) in halves(t0, n):
                    o0 = h0 - t0
                    zap = ybuf[:, :, o0:o0 + nh]
                    ln_pre(zap, nh, RY, True)

                    def epi(o0=o0, nh=nh):
                        for c in range(8):
                            K.op("act", act(zb[:, c, :nh], ybuf[:, c, o0:o0 + nh], AF.Silu, bias=vc(CLBO + c), scale=vc(CLGO + c)),
                                 reads=[RY[c], Rvec], writes=[Rzb], arena=True)
                    ln_post(zap, nh, RY, True, epi)
                    for d in range(8):
                        bi = d % 4
                        mm(PS[bi][:, :nh], [(Dm[k // 4][:, k % 4, d * 128:(d + 1) * 128], zb[:, k, :nh]) for k in range(8)],
                           reads=[Rzb, RD[0], RD[1]], bank=RPS[bi])
                        xr = xres[:, d, h0:h0 + nh]
                        K.op("dve", stt(xr, PS[bi][:, :nh], vc(B2O + d), xr, ALU.add, ALU.add), reads=[RPS[bi], RXR[ti][d], Rvec], writes=[RXR[ti][d]])
                res_ln(ti, 4, False)
            nxt[0]()
            nxt[1]()

        conv_first = lambda: (loadA(0, w_pw1, 0), loadA(1, w_pw1, 512), loadD(0, w_pw2, 0))
        conv_second = lambda: (loadA(2, w_pw1, 1024), loadA(3, w_pw1, 1536), loadD(1, w_pw2, 512))

        ffn_first = lambda l, s: ffn_loads(l, s, 0)
        ffn_second = lambda l, s: ffn_loads(l, s, 1)
        phases = [
            ("ffn", 0, 0, 0), ("hgrn",), ("ffn", 0, 1, 2), ("ffn", 1, 0, 3), ("conv",), ("ffn", 1, 1, 5),
        ][:nsub]

        def first_second(p):
            if p is None:
                return (nop, nop)
            if p[0] == "ffn":
                return (ffn_first(p[1], p[2]), ffn_second(p[1], p[2]))
            if p[0] == "hgrn":
                return (hgrn_first, hgrn_second)
            return (conv_first, conv_second)

        for i, p in enumerate(phases):
            nxt = first_second(phases[i + 1] if i + 1 < len(phases) else None)
            last = i == len(phases) - 1
            if p[0] == "ffn":
                ffn_phase(p[1], p[2], p[3], last and nsub == 6, nxt)
            elif p[0] == "hgrn":
                hgrn_phase(nxt)
            else:
                conv_phase(nxt)
        if nsub < 6:
            for ti in range(5):
                out_tile(ti)
        spq = K.q["sp"]
        for k in list(K.dcnt):
            if k.startswith("o"):
                spq.items.append(("w", k, K.dcnt[k]))

        sems = {k: es.enter_context(nc.semaphore(f"s_{k}")) for k in K.semkeys}

        def replay(q, e):
            own = sems[q.name]
            for it in q.items:
                if it[0] == "w":
                    e.wait_ge(sems[it[1]], it[2])
                elif it[0] == "o":
                    it[1](e).then_inc(own, 1)
                else:
                    it[1](e).then_inc(sems[it[2]], 16)

        with nc.Block() as block:
            @block.sync
            def _(e):
                replay(K.q["sp"], e)

            @block.scalar
            def _(e):
                replay(K.q["act"], e)

            @block.vector
            def _(e):
                replay(K.q["dve"], e)

            @block.gpsimd
            def _(e):
                replay(K.q["pool"], e)

            @block.tensor
            def _(e):
                replay(K.q["pe"], e)
    return nc


def _vecs(ln_g, ln_b, hgrn_lb, hgrn_norm_g, conv_b_pw1, conv_w_dw, conv_b_dw, conv_ln_g, conv_ln_b, conv_b_pw2):
    cm = lambda v: np.asarray(v, np.float32).reshape(-1, 8, 128).transpose(2, 0, 1).reshape(128, -1)
    parts = [
        cm(np.asarray(ln_g).reshape(6, D)), cm(np.asarray(ln_b).reshape(6, D)), cm(np.asarray(hgrn_lb)),
        np.asarray(hgrn_norm_g, np.float32)[0].reshape(128, 1),
        cm(np.asarray(conv_b_pw1)[0][:D]), cm(np.asarray(conv_b_pw1)[0][D:]),
        np.asarray(conv_w_dw, np.float32)[0].reshape(CW, 8, 128).transpose(2, 1, 0).reshape(128, 8 * CW),
        cm(np.asarray(conv_b_dw)[0]), cm(np.asarray(conv_ln_g)[0]), cm(np.asarray(conv_ln_b)[0]), cm(np.asarray(conv_b_pw2)[0]),
    ]
    v = np.ascontiguousarray(np.concatenate(parts, axis=1).astype(np.float32))
    assert v.shape == (128, NV), v.shape
    return v


def make_in_maps(x_prompt, x_sample, state_hgrn, cache_conv, ffn_w_gate, ffn_w_up, ffn_w_down, ln_g, ln_b,
                 hgrn_w_in, hgrn_lb, hgrn_norm_g, hgrn_w_out, conv_w_pw1, conv_b_pw1, conv_w_dw, conv_b_dw,
                 conv_ln_g, conv_ln_b, conv_w_pw2, conv_b_pw2):
    f = lambda a: np.ascontiguousarray(np.asarray(a, dtype=np.float32))
    x_prompt, x_sample, state_hgrn, cache_conv = f(x_prompt), f(x_sample), f(state_hgrn), f(cache_conv)
    vecs = _vecs(ln_g, ln_b, hgrn_lb, hgrn_norm_g, conv_b_pw1, conv_w_dw, conv_b_dw, conv_ln_g, conv_ln_b, conv_b_pw2)
    s_idx = np.arange(128)[:, None]
    tri = (np.tile(np.arange(128), 4)[None, :] >= s_idx).astype(np.uint32)
    tri16 = (np.tile(np.arange(16), 2)[None, :] >= s_idx).astype(np.uint32)
    shared = {
        "wg": f(ffn_w_gate), "wu": f(ffn_w_up), "wd": f(ffn_w_down), "win": f(hgrn_w_in)[0], "wout": f(hgrn_w_out)[0],
        "pw1": f(conv_w_pw1)[0], "pw2": f(conv_w_pw2)[0], "vecs": vecs, "tri": np.ascontiguousarray(tri),
        "tri16": np.ascontiguousarray(tri16), "ident": np.eye(128, dtype=np.float32),
    }
    maps = []
    for i in range(NCORES):
        xa = np.concatenate([x_prompt[i], x_sample[2 * i], x_sample[2 * i + 1]], axis=0)
        m = dict(shared)
        m["xT"] = np.ascontiguousarray(xa.T)
        m["st"] = np.ascontiguousarray(state_hgrn[0, 2 * i:2 * i + 2])
        m["ccT"] = np.ascontiguousarray(cache_conv[0, 2 * i:2 * i + 2].transpose(0, 2, 1))
        maps.append(m)
    return maps


def gather(results):
    y_prompt = np.empty((8, TP, D), np.float32)
    y_sample = np.empty((16, TS, D), np.float32)
    hs_p = np.empty((1, 8, 8, 128, 128), np.float32)
    hs_s = np.empty((1, 16, 8, 128, 128), np.float32)
    cc_p = np.empty((1, 8, 30, D), np.float32)
    cc_s = np.empty((1, 16, 30, D), np.float32)
    for i, r in enumerate(results):
        y = np.asarray(r["yT"]).T
        y_prompt[i] = y[:TP]
        y_sample[2 * i] = y[TP:TP + TS]
        y_sample[2 * i + 1] = y[TP + TS:]
        hs_p[0, i] = np.asarray(r["so_p"])
        hs_s[0, 2 * i:2 * i + 2] = np.asarray(r["so_s"])
        cc_p[0, i] = np.asarray(r["cco_p"]).T
        cc_s[0, 2 * i:2 * i + 2] = np.asarray(r["cco_s"]).transpose(0, 2, 1)
    return (y_prompt, y_sample, hs_p, hs_s, cc_p, cc_s)


_NC_CACHE = {}


def kernel(**inputs):
    maps = make_in_maps(**inputs)
    if 6 not in _NC_CACHE:
        _NC_CACHE[6] = build_nc(6)
    res = run_bass_kernel_spmd(_NC_CACHE[6], maps, core_ids=list(range(NCORES)))
    return gather(res.results)
```

```python
import numpy as np
from contextlib import ExitStack
import concourse.bass as bass
import concourse.mybir as mybir
from concourse.bass_utils import run_bass_kernel_spmd

F32 = mybir.dt.float32
BF16 = mybir.dt.bfloat16
U32 = mybir.dt.uint32
U16 = mybir.dt.uint16
AF = mybir.ActivationFunctionType
ALU = mybir.AluOpType

NCORES = 8
D = 1024
TP = 2048
TS = 16
T = TP + 2 * TS
ALPHA = 2.0 ** 0.5
LN_EPS = 1e-5
TILES = [(0, 512), (512, 512), (1024, 512), (1536, 512), (2048, 32)]
CW = 31

LNG, LNB, LBO, NGO, BAO, BGO, WDWO, BDWO, CLGO, CLBO, B2O, NV = 0, 48, 96, 120, 121, 129, 137, 385, 393, 401, 409, 417
NDVE_TAPS = 19


class Res:
    __slots__ = ("w", "r")

    def __init__(self):
        self.w = {}
        self.r = {}


class Q:
    def __init__(self, name):
        self.name = name
        self.cnt = 0
        self.items = []
        self.seen = {}


class Kb:
    def __init__(self):
        self.res = {}
        self.dcnt = {}
        self.deferred = []
        self.q = {n: Q(n) for n in ("sp", "act", "dve", "pool", "pe")}
        self.arena = Res()
        self.semkeys = ["sp", "act", "dve", "pool", "pe"]

    def R(self, key):
        r = self.res.get(key)
        if r is None:
            r = self.res[key] = Res()
        return r

    def _wait(self, q, reads, writes):
        need = {}
        for r in reads:
            for k, v in r.w.items():
                if need.get(k, 0) < v:
                    need[k] = v
        for w in writes:
            for k, v in w.w.items():
                if k != q.name and need.get(k, 0) < v:
                    need[k] = v
            for k, v in w.r.items():
                if k != q.name and need.get(k, 0) < v:
                    need[k] = v
        for k, v in need.items():
            if q.seen.get(k, 0) < v:
                q.items.append(("w", k, v))
                q.seen[k] = v

    def op(self, qn, fn, reads=(), writes=(), arena=False):
        q = self.q[qn]
        reads = list(reads)
        if arena:
            reads.append(self.arena)
        self._wait(q, reads, writes)
        q.cnt += 1
        q.items.append(("o", fn))
        for r in reads:
            r.r[q.name] = q.cnt
        for w in writes:
            w.r = {}
            w.w = {q.name: q.cnt}

    def dma(self, qn, semkey, fn, reads=(), writes=(), arena=False):
        q = self.q[qn]
        reads = list(reads)
        if arena:
            reads.append(self.arena)
        if semkey not in self.dcnt:
            self.dcnt[semkey] = 0
            self.semkeys.append(semkey)
        self._wait(q, reads, writes)
        self.dcnt[semkey] += 16
        c = self.dcnt[semkey]
        q.items.append(("d", fn, semkey))
        for r in reads:
            r.r[semkey] = c
        for w in writes:
            w.r = {}
            w.w = {semkey: c}

    def fence(self, fn):
        self.op("dve", fn, reads=(), writes=[self.arena])

    def defer(self, n, fn):
        self.deferred.append([n, fn])

    def tick(self):
        run = []
        keep = []
        for it in self.deferred:
            it[0] -= 1
            (run if it[0] <= 0 else keep).append(it)
        self.deferred = keep
        for it in run:
            it[1]()

    def flush(self):
        while self.deferred:
            self.tick()


def act(out, in_, func, bias=None, scale=None):
    kw = {}
    if bias is not None:
        kw["bias"] = bias
    if scale is not None:
        kw["scale"] = scale
    return lambda e: e.activation(out, in_, func, **kw)


def tt(out, a, b, op):
    return lambda e: e.tensor_tensor(out, a, b, op)


def ts(out, a, s1, s2, op0, op1=None):
    if op1 is None:
        return lambda e: e.tensor_scalar(out, a, s1, None, op0)
    return lambda e: e.tensor_scalar(out, a, s1, s2, op0, op1)


def stt(out, a, s, b, op0, op1):
    return lambda e: e.scalar_tensor_tensor(out, a, s, b, op0, op1)


def build_nc(nsub=6):
    nc = bass.Bass("TRN2", target_bir_lowering=False)
    dt_in = lambda name, shape, dt=F32: nc.dram_tensor(name, shape, dt, kind="ExternalInput").ap()
    dt_out = lambda name, shape: nc.dram_tensor(name, shape, F32, kind="ExternalOutput").ap()
    xT = dt_in("xT", [D, T])
    st_in = dt_in("st", [2, 8, 128, 128])
    ccT = dt_in("ccT", [2, D, 30])
    w_gate = dt_in("wg", [2, 2, D, 4 * D])
    w_up = dt_in("wu", [2, 2, D, 4 * D])
    w_down = dt_in("wd", [2, 2, 4 * D, D])
    w_in = dt_in("win", [D, 4 * D])
    w_out = dt_in("wout", [D, D])
    w_pw1 = dt_in("pw1", [D, 2 * D])
    w_pw2 = dt_in("pw2", [D, D])
    vecs_d = dt_in("vecs", [128, NV])
    tri_d = dt_in("tri", [128, 512], U16)
    tri16_d = dt_in("tri16", [128, 32], U16)
    rmask_d = dt_in("rmask", [128, 512])
    ident_d = dt_in("ident", [128, 128])
    yT = dt_out("yT", [D, T])
    so_p = dt_out("so_p", [8, 128, 128])
    so_s = dt_out("so_s", [2, 8, 128, 128])
    cco_p = dt_out("cco_p", [D, 30])
    cco_s = dt_out("cco_s", [2, D, 30])

    K = Kb()
    R = K.R
    es = ExitStack()
    with es:
        sb = lambda name, shape, dt: es.enter_context(nc.sbuf_tensor(name, shape, dt))
        xres = sb("xres", [128, 8, T], F32)
        xbf = sb("xbf", [128, 8, T], BF16)
        A = [sb(f"A{i}", [128, 8, 512], BF16) for i in range(4)]
        Dm = [sb(f"D{i}", [128, 4, 1024], BF16) for i in range(2)]
        vecs = sb("vecs_sb", [128, NV], F32)
        vecsA = sb("vecsA", [128, 96], F32)
        lbv = sb("lbv", [128, 16], F32)
        e3 = sb("e3", [128, 32], F32)
        cst = sb("cst", [128, 4], F32)
        tri = sb("tri_sb", [128, 512], U16)
        tri16 = sb("tri16_sb", [128, 32], U16)
        rmask = sb("rmask_sb", [128, 512], BF16)
        ones = sb("ones", [128, 128], BF16)
        ident = sb("ident_sb", [128, 128], BF16)
        scm16 = sb("scm16", [128, 64], BF16)
        identf = sb("identf", [128, 128], F32)
        fz = sb("fz", [128, 2], F32)
        zb = sb("zb", [128, 8, 256], BF16)
        zsq = sb("zsq", [128, 8, 256], BF16)
        mean = sb("mean", [128, 256], F32)
        var = sb("var", [128, 256], F32)
        sd = sb("sd", [128, 256], F32)
        rstd = sb("rstd", [128, 256], F32)
        ARENA_F = 11264
        arena = sb("arena", [128, ARENA_F], F32)
        PS = [es.enter_context(nc.psum_tensor(f"ps{i}", [128, 512], F32)) for i in range(7)]
        PST = es.enter_context(nc.psum_tensor("pst", [128, 1024], BF16))

        def af(off_b, nelem):
            assert off_b % 4 == 0 and off_b // 4 + nelem <= ARENA_F
            return arena[:, off_b // 4: off_b // 4 + nelem]

        def ab(off_b, nelem):
            assert off_b % 4 == 0 and nelem % 2 == 0 and off_b // 4 + nelem // 2 <= ARENA_F
            return arena[:, off_b // 4: off_b // 4 + nelem // 2].bitcast(BF16)

        SG = [af(0, 512), af(2048, 512)]
        H = [ab(4096, 2048).rearrange("p (j t) -> p j t", t=512), ab(8192, 2048).rearrange("p (j t) -> p j t", t=512)]
        HS = []
        for x in range(2):
            b0 = x * 14336
            HS.append(dict(qs=af(b0, 512), fb=af(b0 + 2048, 512), km=af(b0 + 4096, 512), bb=af(b0 + 6144, 512),
                           Shist=af(b0 + 8192, 512).rearrange("p (c v) -> p c v", v=128),
                           q_rel=ab(b0 + 10240, 512), k_rel=ab(b0 + 11264, 512), k_relT=ab(b0 + 12288, 512), scm=ab(b0 + 13312, 512),
                           dd=af(43008 + 64 * x, 16)))
        obf = ab(28672, 2048).rearrange("p (j t) -> p j t", t=512)
        v_tok = ab(32768, 2048).rearrange("p (c v) -> p c v", v=512)
        S_p = af(36864, 512).rearrange("p (h v) -> p h v", v=128)
        S_s = [af(38912, 512).rearrange("p (h v) -> p h v", v=128), af(40960, 512).rearrange("p (h v) -> p h v", v=128)]
        U = af(0, 8 * 542).rearrange("p (c t) -> p c t", t=542)
        ybuf = af(17408, 4096).rearrange("p (c t) -> p c t", t=512)
        U_s = af(33792, 8 * 2 * 46).rearrange("p (c s t) -> p c s t", s=2, t=46)
        sgt2 = [af(36736, 512), af(38784, 512)]
        prod2 = [af(40832, 512), af(42880, 512)]

        RA = [R(f"A{i}") for i in range(4)]
        RD = [R(f"D{i}") for i in range(2)]
        RPS = [R(f"ps{i}") for i in range(7)]
        RPST = R("pst")
        RXB = [R(f"xbf{t}") for t in range(5)]
        RXR = [[R(f"xr{t}_{c}") for c in range(8)] for t in range(5)]
        Rvec, RvecA, Rlb, Rcst, Rtri, Rones, Rident = R("vecs"), R("vecsA"), R("lbv"), R("cst"), R("tri"), R("ones"), R("ident")
        Rzb, Rzsq, Rmean, Rvar, Rsd, Rrstd = R("zb"), R("zsq"), R("mean"), R("var"), R("sd"), R("rstd")

        vc = lambda col: vecs[:, col:col + 1]

        def mm(out_ap, pairs, reads, bank, arena_=False):
            def fn(e, pairs=pairs, out_ap=out_ap):
                n = len(pairs)
                ins = None
                for i, (l, r) in enumerate(pairs):
                    ins = e.matmul(out_ap, l, r, start=(i == 0), stop=(i == n - 1))
                return ins
            K.op("pe", fn, reads=reads, writes=[bank], arena=arena_)

        def loadA(slot, w2d, col0):
            src = w2d[:, col0:col0 + 512].rearrange("(k p) c -> p k c", p=128)
            K.dma("pool", f"wA{slot}", lambda e: e.dma_start(out=A[slot][:], in_=src), writes=[RA[slot]])

        def loadD(slot, w2d, row0):
            src = w2d[row0:row0 + 512, :].rearrange("(j p) c -> p j c", p=128)
            K.dma("pool", f"wD{slot}", lambda e: e.dma_start(out=Dm[slot][:], in_=src), writes=[RD[slot]])

        def ffn_loads(l, s, g):
            par = g % 2
            return lambda: (loadA(2 * par, w_gate[l, s], g * 512), loadA(2 * par + 1, w_up[l, s], g * 512),
                            loadD(par, w_down[l, s], g * 512))

        nop = lambda: None

        xTv = xT.rearrange("(c p) t -> p c t", p=128)
        yTv = yT.rearrange("(c p) t -> p c t", p=128)
        K.dma("sp", "iv", lambda e: e.dma_start(out=vecs[:], in_=vecs_d), writes=[Rvec])
        K.dma("sp", "it", lambda e: e.dma_start(out=tri[:], in_=tri_d), writes=[Rtri])
        K.dma("sp", "it16", lambda e: e.dma_start(out=tri16[:], in_=tri16_d), writes=[R("tri16")])
        K.dma("pool", "ii", lambda e: e.dma_start(out=ident[:], in_=ident_d), writes=[Rident])
        K.dma("pool", "irm", lambda e: e.dma_start(out=rmask[:], in_=rmask_d), writes=[R("rmask")])
        K.dma("sp", "iif", lambda e: e.dma_start(out=identf[:], in_=ident_d), writes=[R("identf")])
        for ti, (t0, n) in enumerate(TILES):
            K.dma("sp", f"ix{ti}", lambda e, t0=t0, n=n: e.dma_start(out=xres[:, :, t0:t0 + n], in_=xTv[:, :, t0:t0 + n]),
                  writes=RXR[ti])
            K.dma("pool", f"ib{ti}", lambda e, t0=t0, n=n: e.dma_start(out=xbf[:, :, t0:t0 + n], in_=xTv[:, :, t0:t0 + n]),
                  writes=[RXB[ti]])
            if ti == 0:
                ffn_loads(0, 0, 0)()
                ffn_loads(0, 0, 1)()
        K.op("dve", lambda e: e.memset(ones[:], 1.0), writes=[Rones])
        K.op("dve", lambda e: e.memset(cst[:, 0:1], LN_EPS), writes=[Rcst])
        K.op("dve", lambda e: e.memset(cst[:, 1:2], float(np.log(128.0 ** -0.5))), writes=[Rcst])
        K.op("dve", lambda e: e.memset(cst[:, 2:3], 1.0), writes=[Rcst])
        K.op("dve", lambda e: e.memset(scm16[:], 0.0), writes=[R("scm16")])
        K.op("dve", ts(vecsA[:], vecs[:, 0:96], ALPHA, None, ALU.mult), reads=[Rvec], writes=[RvecA])
        K.op("act", act(e3[:, 0:24], vecs[:, LBO:LBO + 24], AF.Exp), reads=[Rvec], writes=[R("e3")])
        K.op("dve", tt(e3[:, 24:32], e3[:, 0:8], e3[:, 8:16], ALU.add), reads=[R("e3")], writes=[R("e3b")])
        K.op("dve", tt(e3[:, 24:32], e3[:, 24:32], e3[:, 16:24], ALU.add), reads=[R("e3"), R("e3b")], writes=[R("e3b")])
        K.op("dve", lambda e: e.reciprocal(e3[:, 24:32], e3[:, 24:32]), reads=[R("e3b")], writes=[R("e3b")])
        K.op("dve", tt(lbv[:, 0:8], e3[:, 0:8], e3[:, 24:32], ALU.mult), reads=[R("e3"), R("e3b")], writes=[Rlb])
        K.op("dve", ts(lbv[:, 8:16], lbv[:, 0:8], -1.0, 1.0, ALU.mult, ALU.add), reads=[Rlb], writes=[Rlb])
        for ti, (t0, n) in enumerate(TILES):
            K.op("dve", ts(xres[:, :, t0:t0 + n], xres[:, :, t0:t0 + n], ALPHA, None, ALU.mult), reads=RXR[ti], writes=RXR[ti])

        def ln_pre(zap, n, rz, ar):
            K.op("act", act(zsq[:, :, :n], zap, AF.Square), reads=rz, writes=[Rzsq], arena=ar)
            K.op("dve", lambda e: e.tensor_copy(zb[:, :, :n], zap), reads=rz, writes=[Rzb], arena=ar)

        def ln_post(zap, n, rz, ar, epilogue):
            mm(PS[5][:, :n], [(ones[:], zb[:, c, :n]) for c in range(8)], reads=[Rzb, Rones], bank=RPS[5])
            mm(PS[6][:, :n], [(ones[:], zsq[:, c, :n]) for c in range(8)], reads=[Rzsq, Rones], bank=RPS[6])
            K.op("dve", ts(mean[:, :n], PS[5][:, :n], 1.0 / D, None, ALU.mult), reads=[RPS[5]], writes=[Rmean])
            K.op("dve", tt(var[:, :n], mean[:, :n], mean[:, :n], ALU.mult), reads=[Rmean], writes=[Rvar])
            K.op("dve", stt(var[:, :n], PS[6][:, :n], 1.0 / D, var[:, :n], ALU.mult, ALU.subtract), reads=[RPS[6], Rvar], writes=[Rvar])
            K.op("act", act(sd[:, :n], var[:, :n], AF.Ln, bias=cst[:, 0:1]), reads=[Rvar, Rcst], writes=[Rsd])
            K.op("act", act(rstd[:, :n], sd[:, :n], AF.Exp, scale=-0.5), reads=[Rsd], writes=[Rrstd])
            K.op("dve", tt(zap, zap, mean[:, :n].unsqueeze(1).to_broadcast([128, 8, n]), ALU.subtract),
                 reads=rz + [Rmean], writes=rz, arena=ar)
            K.op("dve", tt(zap, zap, rstd[:, :n].unsqueeze(1).to_broadcast([128, 8, n]), ALU.mult),
                 reads=rz + [Rrstd], writes=rz, arena=ar)
            epilogue()

        def halves(t0, n):
            return [(t0, 256), (t0 + 256, 256)] if n == 512 else [(t0, n)]

        def res_ln(ti, lnidx, final):
            t0, n = TILES[ti]
            rz = RXR[ti]
            for (h0, nh) in halves(t0, n):
                zap = xres[:, :, h0:h0 + nh]
                ln_pre(zap, nh, rz, False)

                def epi(h0=h0, nh=nh):
                    for c in range(8):
                        gcol, bcol = LNG + lnidx * 8 + c, LNB + lnidx * 8 + c
                        zc = xres[:, c, h0:h0 + nh]
                        if not final:
                            K.op("act", act(xbf[:, c, h0:h0 + nh], zc, AF.Identity, bias=vc(bcol), scale=vc(gcol)),
                                 reads=[RXR[ti][c], Rvec], writes=[RXB[ti]])
                            K.op("act", act(zc, zc, AF.Identity, bias=vecsA[:, bcol:bcol + 1], scale=vecsA[:, gcol:gcol + 1]),
                                 reads=[RXR[ti][c], RvecA], writes=[RXR[ti][c]])
                        else:
                            K.op("act", act(zc, zc, AF.Identity, bias=vc(bcol), scale=vc(gcol)),
                                 reads=[RXR[ti][c], Rvec], writes=[RXR[ti][c]])
                ln_post(zap, nh, rz, False, epi)
            if final:
                out_tile(ti)

        def out_tile(ti):
            t0, n = TILES[ti]
            K.dma("sp", "oy", lambda e: e.dma_start(out=yTv[:, :, t0:t0 + n], in_=xres[:, :, t0:t0 + n]), reads=RXR[ti])

        def ffn_phase(l, s, lnidx, final, nxt):
            K.fence(lambda e: e.memset(fz[:, 0:1], 0.0))
            unit = 0
            for g in range(8):
                par = g % 2
                ag, au, dw = A[2 * par], A[2 * par + 1], Dm[par]
                rag, rau, rdw = RA[2 * par], RA[2 * par + 1], RD[par]
                for ti, (t0, n) in enumerate(TILES):
                    up = unit % 2
                    unit += 1
                    hb = H[up]
                    RH = [R(f"h{up}_{j}") for j in range(4)]
                    for j in range(4):
                        gi, ui = (j % 2) * 2, (j % 2) * 2 + 1
                        mm(PS[gi][:, :n], [(ag[:, k, j * 128:(j + 1) * 128], xbf[:, k, t0:t0 + n]) for k in range(8)],
                           reads=[rag, RXB[ti]], bank=RPS[gi])
                        mm(PS[ui][:, :n], [(au[:, k, j * 128:(j + 1) * 128], xbf[:, k, t0:t0 + n]) for k in range(8)],
                           reads=[rau, RXB[ti]], bank=RPS[ui])
                        sgb = SG[j % 2]
                        rsg = R(f"sg{j % 2}")
                        K.op("act", act(sgb[:, :n], PS[gi][:, :n], AF.Silu), reads=[RPS[gi]], writes=[rsg], arena=True)
                        K.op("dve", tt(hb[:, j, :n], sgb[:, :n], PS[ui][:, :n], ALU.mult), reads=[rsg, RPS[ui]], writes=[RH[j]], arena=True)

                    def down(g=g, ti=ti, t0=t0, n=n, hb=hb, RH=RH, dw=dw, rdw=rdw):
                        for d in range(8):
                            bi = 4 + d % 3
                            mm(PS[bi][:, :n], [(dw[:, j, d * 128:(d + 1) * 128], hb[:, j, :n]) for j in range(4)],
                               reads=[rdw] + RH, bank=RPS[bi], arena_=True)
                            xr = xres[:, d, t0:t0 + n]
                            K.op("dve", stt(xr, PS[bi][:, :n], 0.5, xr, ALU.mult, ALU.add), reads=[RPS[bi], RXR[ti][d]], writes=[RXR[ti][d]])
                        if g == 7:
                            K.defer(1, lambda: res_ln(ti, lnidx, final))
                    K.tick()
                    if ti == 0 and g >= 1:
                        if g + 1 < 8:
                            ffn_loads(l, s, g + 1)()
                        else:
                            nxt[0]()
                    K.defer(1, down)
            K.flush()
            nxt[1]()

        def hgrn_phase(nxt):
            K.fence(lambda e: e.memset(fz[:, 0:1], 0.0))
            for x in range(2):
                K.op("dve", lambda e, x=x: e.memset(HS[x]["scm"][:], 0.0), writes=[R(f"scm{x}")], arena=True)
            Rhl = R("e3")
            K.op("dve", ts(e3[:, 0:8], lbv[:, 8:16], 0.5, None, ALU.mult), reads=[Rlb, R("e3"), R("e3b")], writes=[Rhl])
            K.op("dve", ts(e3[:, 8:16], lbv[:, 8:16], -0.5, None, ALU.mult), reads=[Rlb], writes=[Rhl])
            K.op("dve", ts(e3[:, 16:24], lbv[:, 0:8], 0.5, 0.5, ALU.mult, ALU.add), reads=[Rlb], writes=[Rhl])
            K.op("dve", ts(e3[:, 24:25], vc(NGO), 128.0 ** -0.5, None, ALU.mult), reads=[Rvec, R("e3b")], writes=[R("e3b")])
            Rvt = R("v_tok")
            Robf = [R(f"obf{j}") for j in range(4)]
            RSP, RSS = R("S0"), [R("S1"), R("S2")]
            zbf = zb[:].rearrange("p c t -> p (c t)").bitcast(F32)
            zsqf = zsq[:].rearrange("p c t -> p (c t)").bitcast(F32)

            def head(hg, ti, j, x):
                t0, n = TILES[ti]
                nch, cl = (4, 128) if n == 512 else (2, 16)
                mid = cl // 2
                prompt = n == 512
                h = hg * 4 + j
                cs = slice(j * 128, (j + 1) * 128)
                B = HS[x]
                if j < 2:
                    qs, fb, Rq, Rf = B["qs"], B["fb"], R(f"qs{x}"), R(f"fb{x}")
                else:
                    zz, rz_ = (zbf, Rzb) if x == 0 else (zsqf, Rzsq)
                    qs, fb, Rq, Rf = zz[:, 0:512], zz[:, 512:1024], rz_, rz_
                km, bb, Shist = B["km"], B["bb"], B["Shist"]
                q_rel, k_rel, k_relT, dd = B["q_rel"], B["k_rel"], B["k_relT"], B["dd"]
                on_, hv, osq, gs, Sp4 = qs, km, k_relT, fb, k_rel
                Rkm, Rbb, Rsh = R(f"km{x}"), R(f"bb{x}"), R(f"sh{x}")
                Rqr, Rkr, RkT, Rdd = R(f"q_rel{x}"), R(f"k_rel{x}"), R(f"k_relT{x}"), R(f"dd{x}")
                RS = [RSP] * 4 if prompt else RSS
                Sst = [S_p] * 4 if prompt else S_s
                msk, rmsk = (tri, Rtri) if prompt else (tri16, R("tri16"))
                scb, rscb = (B["scm"], R(f"scm{x}")) if prompt else (scm16[:, 32 * x:32 * x + 32], R(f"scm16_{x}"))
                bq, bf_, bo = x, 2 + x, 4 + x
                WQ, WF, WV, WG = A
                xin = lambda k: xbf[:, k, t0:t0 + n]
                mm(PS[bq][:, :n], [(WQ[:, k, cs], xin(k)) for k in range(8)], reads=[RXB[ti], RA[0]], bank=RPS[bq])
                mm(PS[bf_][:, :n], [(WF[:, k, cs], xin(k)) for k in range(8)], reads=[RXB[ti], RA[1]], bank=RPS[bf_])
                K.op("act", act(qs[:, :n], PS[bq][:, :n], AF.Silu), reads=[RPS[bq]], writes=[Rq], arena=True)
                K.op("act", act(fb[:, :n], PS[bf_][:, :n], AF.Tanh, scale=0.5), reads=[RPS[bf_]], writes=[Rf], arena=True)
                yield
                K.op("dve", ts(km[:, :n], fb[:, :n], e3[:, 8 + h:9 + h], e3[:, h:h + 1], ALU.mult, ALU.add), reads=[Rf, Rhl], writes=[Rkm], arena=True)
                yield
                K.op("act", act(fb[:, :n], fb[:, :n], AF.Ln, bias=e3[:, 16 + h:17 + h], scale=e3[:, h:h + 1]), reads=[Rf, Rhl], writes=[Rf], arena=True)
                yield
                if prompt:
                    K.op("dve", lambda e: e.tensor_tensor_scan(bb[:, :n], rmask[:, :n], fb[:, :n], 0.0, ALU.mult, ALU.add),
                         reads=[Rf, R("rmask")], writes=[Rbb], arena=True)
                    yield
                else:
                    for c in range(nch):
                        sl = slice(c * cl, (c + 1) * cl)
                        K.op("dve", lambda e, sl=sl: e.tensor_tensor_scan(bb[:, sl], cst[:, 2:3].to_broadcast([128, cl]), fb[:, sl], 0.0,
                                                                          ALU.mult, ALU.add),
                             reads=[Rf, Rcst], writes=[Rbb], arena=True)
                        yield
                b3 = bb[:, :n].rearrange("p (c t) -> p c t", t=cl)
                f3 = fb[:, :n].rearrange("p (c t) -> p c t", t=cl)
                dd3 = dd[:, 0:2 * nch].rearrange("p (c t) -> p c t", t=2)
                K.op("act", act(dd3, b3[:, :, mid::cl - 1 - mid], AF.Exp), reads=[Rbb], writes=[Rdd], arena=True)
                yield
                K.op("dve", tt(f3, b3, b3[:, :, mid:mid + 1].to_broadcast([128, nch, cl]), ALU.subtract), reads=[Rbb], writes=[Rf], arena=True)
                yield
                K.op("act", act(bb[:, :n], fb[:, :n], AF.Exp), reads=[Rf], writes=[Rbb], arena=True)
                yield
                K.op("act", act(fb[:, :n], fb[:, :n], AF.Exp, scale=-1.0), reads=[Rf], writes=[Rf], arena=True)
                yield
                K.op("dve", tt(q_rel[:, :n], qs[:, :n], bb[:, :n], ALU.mult), reads=[Rq, Rbb], writes=[Rqr], arena=True)
                yield
                K.op("dve", tt(k_rel[:, :n], km[:, :n], fb[:, :n], ALU.mult), reads=[Rkm, Rf], writes=[Rkr], arena=True)
                yield

                def trfn(e):
                    ins = None
                    for c in range(nch):
                        ins = e.transpose(PST[:cl, c * 128:(c + 1) * 128], k_rel[:, c * cl:(c + 1) * cl], ident[:])
                    return ins
                K.op("pe", trfn, reads=[Rkr, Rident], writes=[RPST], arena=True)
                K.op("act", act(k_relT[:cl, :nch * 128], PST[:cl, :nch * 128], AF.Copy), reads=[RPST], writes=[RkT], arena=True)
                yield

                def scfn(e):
                    ins = None
                    for c in range(nch):
                        sl = slice(c * cl, (c + 1) * cl)
                        ins = e.matmul(PS[bf_][:cl, sl], k_rel[:, sl], q_rel[:, sl], start=True, stop=True)
                    return ins
                K.op("pe", scfn, reads=[Rkr, Rqr], writes=[RPS[bf_]], arena=True)
                yield
                K.op("dve", lambda e: e.copy_predicated(scb[:cl, :nch * cl], msk[:cl, :nch * cl], PS[bf_][:cl, :nch * cl]),
                     reads=[RPS[bf_], rmsk], writes=[rscb], arena=True)
                yield

                def ufn(e):
                    ins = None
                    for c in range(nch):
                        ins = e.matmul(PS[bq][:, c * 128:(c + 1) * 128], k_relT[:cl, c * 128:(c + 1) * 128], v_tok[:cl, c, cs], start=True, stop=True)
                    return ins
                K.op("pe", ufn, reads=[RkT, Rvt], writes=[RPS[bq]], arena=True)
                yield
                yield
                if prompt:
                    K.op("dve", lambda e: e.tensor_copy(Shist[:, 0, :], S_p[:, j, :]), reads=[RSP], writes=[Rsh], arena=True)
                    yield
                for c in range(nch):
                    d1c = dd[:, 2 * c + 1:2 * c + 2]
                    d2c = bb[:, c * cl + cl - 1:c * cl + cl]
                    if prompt:
                        src = Shist[:, c, :]
                        tgt = Shist[:, c + 1, :] if c < nch - 1 else S_p[:, j, :]
                        rsrc, rtgt = [Rsh], ([Rsh] if c < nch - 1 else [RSP, Rsh])
                    else:
                        K.op("dve", lambda e, c=c: e.tensor_copy(Shist[:, c, :], S_s[c][:, j, :]), reads=[RSS[c]], writes=[Rsh], arena=True)
                        yield
                        src, tgt, rsrc, rtgt = Shist[:, c, :], S_s[c][:, j, :], [Rsh], [RSS[c]]
                    K.op("dve", ts(tgt, src, d1c, None, ALU.mult), reads=rsrc + [Rdd], writes=rtgt, arena=True)
                    yield
                    K.op("dve", stt(tgt, PS[bq][:, c * 128:(c + 1) * 128], d2c, tgt, ALU.mult, ALU.add), reads=[RPS[bq], Rbb] + rtgt, writes=rtgt, arena=True)
                    yield
                K.op("dve", tt(Sp4[:, :nch * 128].rearrange("p (c v) -> p c v", v=128), Shist[:, 0:nch, :],
                               dd3[:, :, 0:1].to_broadcast([128, nch, 128]), ALU.mult), reads=[Rsh, Rdd, Rkr], writes=[Rkr], arena=True)
                yield
                yield
                for c in range(nch):
                    sl = slice(c * cl, (c + 1) * cl)
                    mm(PS[bo][:, sl], [(v_tok[:cl, c, cs], scb[:cl, sl]), (Sp4[:, c * 128:(c + 1) * 128], q_rel[:, sl])],
                       reads=[Rvt, rscb, Rkr, Rqr], bank=RPS[bo], arena_=True)
                    yield
                mm(PS[bq][:, :n], [(WG[:, k, cs], xin(k)) for k in range(8)], reads=[RXB[ti], RA[3]], bank=RPS[bq])
                yield
                K.op("act", act(gs[:, :n], PS[bq][:, :n], AF.Silu), reads=[RPS[bq], Rf], writes=[Rf], arena=True)
                yield
                yield
                K.op("act", act(osq[:, :n], PS[bo][:, :n], AF.Square), reads=[RPS[bo], RkT], writes=[RkT], arena=True)
                yield
                mm(PS[bf_][:, :n], [(ones[:], osq[:, :n])], reads=[RkT, Rones], bank=RPS[bf_], arena_=True)
                yield
                K.op("act", act(hv[:, :n], PS[bf_][:, :n], AF.Ln, bias=cst[:, 0:1], scale=1.0 / (128.0 * 128.0)), reads=[RPS[bf_], Rcst, Rkm], writes=[Rkm], arena=True)
                yield
                K.op("act", act(hv[:, :n], hv[:, :n], AF.Exp, scale=-0.5), reads=[Rkm], writes=[Rkm], arena=True)
                yield
                K.op("dve", tt(on_[:, :n], PS[bo][:, :n], hv[:, :n], ALU.mult), reads=[RPS[bo], Rkm, Rq], writes=[Rq], arena=True)
                yield
                K.op("dve", stt(obf[:, j, :n], on_[:, :n], e3[:, 24:25], gs[:, :n], ALU.mult, ALU.mult), reads=[Rq, Rf, R("e3b")], writes=[Robf[j]], arena=True)
                yield

            def drive(gens):
                gens = list(gens)
                while gens:
                    for g_ in list(gens):
                        try:
                            next(g_)
                        except StopIteration:
                            gens.remove(g_)

            for hg in range(2):
                WV = A[2]
                WO, RWO = Dm[hg], RD[hg]
                K.op("dve", lambda e: e.memset(S_p[:], 0.0), writes=[RSP], arena=True)
                for s_ in range(2):
                    K.dma("sp", f"is{hg}{s_}", lambda e, s_=s_, hg=hg: e.dma_start(out=S_s[s_][:], in_=st_in[s_, hg * 4:(hg + 1) * 4].rearrange("h k v -> k h v")),
                          writes=[RSS[s_]], arena=True)
                gens_next = None
                for ti, (t0, n) in enumerate(TILES):
                    nch, cl = (4, 128) if n == 512 else (2, 16)
                    if gens_next is None:
                        gens = [head(hg, ti, j, j % 2) for j in range(4)]
                        next(gens[0])
                        next(gens[1])
                    else:
                        gens = gens_next
                    next(gens[2])
                    next(gens[3])
                    for c in range(nch):
                        bv = 4 + c % 2
                        mm(PS[bv][:cl, :], [(xbf[:, k, t0 + c * cl:t0 + (c + 1) * cl], WV[:, k, :]) for k in range(8)],
                           reads=[RXB[ti], RA[2]], bank=RPS[bv])
                        K.op("act", act(v_tok[:cl, c, :], PS[bv][:cl, :], AF.Copy), reads=[RPS[bv]], writes=[Rvt], arena=True)
                    drive(gens[0:2])
                    if ti + 1 < len(TILES):
                        gens_next = [head(hg, ti + 1, j, j % 2) for j in range(4)]
                        next(gens_next[0])
                        next(gens_next[1])
                    else:
                        gens_next = None
                    drive(gens[2:4])
                    for d in range(8):
                        bi = d % 4
                        mm(PS[bi][:, :n], [(WO[:, j, d * 128:(d + 1) * 128], obf[:, j, :n]) for j in range(4)],
                           reads=[RWO] + Robf, bank=RPS[bi], arena_=True)
                        xr = xres[:, d, t0:t0 + n]
                        K.op("dve", tt(xr, PS[bi][:, :n], xr, ALU.add), reads=[RPS[bi], RXR[ti][d]], writes=[RXR[ti][d]])
                    if hg == 1:
                        res_ln(ti, 1, False)
                K.dma("sp", f"os{hg}", lambda e, hg=hg: e.dma_start(out=so_p[hg * 4:(hg + 1) * 4].rearrange("h k v -> k h v"), in_=S_p[:]), reads=[RSP], arena=True)
                for s_ in range(2):
                    K.dma("sp", f"os{hg}{s_}", lambda e, s_=s_, hg=hg: e.dma_start(out=so_s[s_, hg * 4:(hg + 1) * 4].rearrange("h k v -> k h v"), in_=S_s[s_][:]),
                          reads=[RSS[s_]], arena=True)
                if hg == 0:
                    loadA(0, w_in, 512)
                    loadA(1, w_in, 1024 + 512)
                    loadA(2, w_in, 2048 + 512)
                    loadA(3, w_in, 3072 + 512)
                else:
                    nxt[0]()
                    nxt[1]()

        hgrn_first = lambda: (loadA(0, w_in, 0), loadA(1, w_in, 1024), loadD(0, w_out, 0))
        hgrn_second = lambda: (loadA(2, w_in, 2048), loadA(3, w_in, 3072), loadD(1, w_out, 512))

        def conv_phase(nxt):
            K.fence(lambda e: e.memset(fz[:, 0:1], 0.0))
            RU = [R(f"U{c}") for c in range(8)]
            RUs = [R(f"Us{c}") for c in range(8)]
            RY = [R(f"y{c}") for c in range(8)]
            Rsgt = [R("sgt0"), R("sgt1")]
            K.op("dve", lambda e: e.memset(U[:, :, 0:30], 0.0), writes=RU, arena=True)
            for s_ in range(2):
                K.dma("sp", f"ic{s_}", lambda e, s_=s_: e.dma_start(out=U_s[:, :, s_, 0:30], in_=ccT[s_].rearrange("(c p) w -> p c w", p=128)),
                      writes=RUs, arena=True)
            for ti, (t0, n) in enumerate(TILES):
                prompt = n == 512
                for cp in range(4):
                    info = []
                    for x in range(2):
                        c = 2 * cp + x
                        ai, gi = 2 * x, 2 * x + 1
                        cs = slice((c % 4) * 128, (c % 4 + 1) * 128)
                        sgt = sgt2[x]
                        mm(PS[ai][:, :n], [(A[c // 4][:, k, cs], xbf[:, k, t0:t0 + n]) for k in range(8)], reads=[RXB[ti], RA[c // 4]], bank=RPS[ai])
                        mm(PS[gi][:, :n], [(A[2 + c // 4][:, k, cs], xbf[:, k, t0:t0 + n]) for k in range(8)], reads=[RXB[ti], RA[2 + c // 4]], bank=RPS[gi])
                        K.op("act", act(sgt[:, :n], PS[gi][:, :n], AF.Sigmoid, bias=vc(BGO + c)), reads=[RPS[gi], Rvec], writes=[Rsgt[x]], arena=True)
                        if prompt:
                            ru = RU[c]
                            unew = U[:, c, 30:542]
                            pa, sg_ = PS[ai][:, :n], sgt[:, :n]
                            tap = (lambda c: (lambda w: U[:, c, w:w + 512]))(c)
                            yc = ybuf[:, c, :]
                        else:
                            ru = RUs[c]
                            unew = U_s[:, c, :, 30:46]
                            pa = PS[ai][:, :n].rearrange("p (s t) -> p s t", t=16)
                            sg_ = sgt[:, :n].rearrange("p (s t) -> p s t", t=16)
                            tap = (lambda c: (lambda w: U_s[:, c, :, w:w + 16]))(c)
                            yc = ybuf[:, c, :n].rearrange("p (s t) -> p s t", t=16)
                        K.op("dve", stt(unew, pa, vc(BAO + c), sg_, ALU.add, ALU.mult), reads=[RPS[ai], Rsgt[x], Rvec], writes=[ru], arena=True)
                        info.append((c, ru, tap, yc))
                    npe = CW - NDVE_TAPS
                    for i in range(max(NDVE_TAPS, npe)):
                        for x, (c, ru, tap, yc) in enumerate(info):
                            if i < NDVE_TAPS:
                                w = i
                                wc = vc(WDWO + c * CW + w)
                                if w == 0:
                                    K.op("dve", ts(yc, tap(0), wc, vc(BDWO + c), ALU.mult, ALU.add), reads=[ru, Rvec], writes=[RY[c]], arena=True)
                                else:
                                    K.op("dve", stt(yc, tap(w), wc, yc, ALU.mult, ALU.add), reads=[ru, Rvec, RY[c]], writes=[RY[c]], arena=True)
                            if i < npe:
                                w = NDVE_TAPS + i
                                wc = vc(WDWO + c * CW + w)
                                pv_ = prod2[x][:, :n] if prompt else prod2[x][:, :n].rearrange("p (s t) -> p s t", t=16)
                                K.op("act", act(pv_, tap(w), AF.Identity, scale=wc), reads=[ru, Rvec], writes=[R(f"prod{x}")], arena=True)
                                K.op("pe", lambda e, x=x, i=i, n=n, npe=npe: e.matmul(PS[4 + x][:, :n], identf[:], prod2[x][:, :n], start=(i == 0), stop=(i == npe - 1)),
                                     reads=[R(f"prod{x}"), R("identf")], writes=[RPS[4 + x]], arena=True)
                    for x, (c, ru, tap, yc) in enumerate(info):
                        ycf = ybuf[:, c, :n]
                        K.op("dve", tt(ycf, ycf, PS[4 + x][:, :n], ALU.add), reads=[RY[c], RPS[4 + x]], writes=[RY[c]], arena=True)
                    if prompt and ti < 3:
                        for (c, ru, tap, yc) in info:
                            K.op("act", act(U[:, c, 0:30], U[:, c, 512:542], AF.Copy), reads=[ru], writes=[ru], arena=True)
                if ti == 3:
                    K.dma("sp", "oc", lambda e: e.dma_start(out=cco_p.rearrange("(c p) w -> p c w", p=128), in_=U[:, :, 512:542]), reads=RU, arena=True)
                if not prompt:
                    for s_ in range(2):
                        K.dma("sp", "oc", lambda e, s_=s_: e.dma_start(out=cco_s[s_].rearrange("(c p) w -> p c w", p=128), in_=U_s[:, :, s_, 16:46]),
                              reads=RUs, arena=True)
                for (h0, nh) in halves(t0, n):
                    o0 = h0 - t0
                    zap = ybuf[:, :, o0:o0 + nh]
                    ln_pre(zap, nh, RY, True)

                    def epi(o0=o0, nh=nh):
                        for c in range(8):
                            K.op("act", act(zb[:, c, :nh], ybuf[:, c, o0:o0 + nh], AF.Silu, bias=vc(CLBO + c), scale=vc(CLGO + c)),
                                 reads=[RY[c], Rvec], writes=[Rzb], arena=True)
                    ln_post(zap, nh, RY, True, epi)
                    for d in range(8):
                        bi = d % 4
                        mm(PS[bi][:, :nh], [(Dm[k // 4][:, k % 4, d * 128:(d + 1) * 128], zb[:, k, :nh]) for k in range(8)],
                           reads=[Rzb, RD[0], RD[1]], bank=RPS[bi])
                        xr = xres[:, d, h0:h0 + nh]
                        K.op("dve", stt(xr, PS[bi][:, :nh], vc(B2O + d), xr, ALU.add, ALU.add), reads=[RPS[bi], RXR[ti][d], Rvec], writes=[RXR[ti][d]])
                res_ln(ti, 4, False)
            nxt[0]()
            nxt[1]()

        conv_first = lambda: (loadA(0, w_pw1, 0), loadA(1, w_pw1, 512), loadD(0, w_pw2, 0))
        conv_second = lambda: (loadA(2, w_pw1, 1024), loadA(3, w_pw1, 1536), loadD(1, w_pw2, 512))

        ffn_first = lambda l, s: ffn_loads(l, s, 0)
        ffn_second = lambda l, s: ffn_loads(l, s, 1)
        phases = [
            ("ffn", 0, 0, 0), ("hgrn",), ("ffn", 0, 1, 2), ("ffn", 1, 0, 3), ("conv",), ("ffn", 1, 1, 5),
        ][:nsub]

        def first_second(p):
            if p is None:
                return (nop, nop)
            if p[0] == "ffn":
                return (ffn_first(p[1], p[2]), ffn_second(p[1], p[2]))
            if p[0] == "hgrn":
                return (hgrn_first, hgrn_second)
            return (conv_first, conv_second)

        for i, p in enumerate(phases):
            nxt = first_second(phases[i + 1] if i + 1 < len(phases) else None)
            last = i == len(phases) - 1
            if p[0] == "ffn":
                ffn_phase(p[1], p[2], p[3], last and nsub == 6, nxt)
            elif p[0] == "hgrn":
                hgrn_phase(nxt)
            else:
                conv_phase(nxt)
        if nsub < 6:
            for ti in range(5):
                out_tile(ti)
        spq = K.q["sp"]
        for k in list(K.dcnt):
            if k.startswith("o"):
                spq.items.append(("w", k, K.dcnt[k]))

        sems = {k: es.enter_context(nc.semaphore(f"s_{k}")) for k in K.semkeys}

        def replay(q, e):
            own = sems[q.name]
            for it in q.items:
                if it[0] == "w":
                    e.wait_ge(sems[it[1]], it[2])
                elif it[0] == "o":
                    it[1](e).then_inc(own, 1)
                else:
                    it[1](e).then_inc(sems[it[2]], 16)

        with nc.Block() as block:
            @block.sync
            def _(e):
                replay(K.q["sp"], e)

            @block.scalar
            def _(e):
                replay(K.q["act"], e)

            @block.vector
            def _(e):
                replay(K.q["dve"], e)

            @block.gpsimd
            def _(e):
                replay(K.q["pool"], e)

            @block.tensor
            def _(e):
                replay(K.q["pe"], e)
    return nc


def _vecs(ln_g, ln_b, hgrn_lb, hgrn_norm_g, conv_b_pw1, conv_w_dw, conv_b_dw, conv_ln_g, conv_ln_b, conv_b_pw2):
    cm = lambda v: np.asarray(v, np.float32).reshape(-1, 8, 128).transpose(2, 0, 1).reshape(128, -1)
    parts = [
        cm(np.asarray(ln_g).reshape(6, D)), cm(np.asarray(ln_b).reshape(6, D)), cm(np.asarray(hgrn_lb)),
        np.asarray(hgrn_norm_g, np.float32)[0].reshape(128, 1),
        cm(np.asarray(conv_b_pw1)[0][:D]), cm(np.asarray(conv_b_pw1)[0][D:]),
        np.asarray(conv_w_dw, np.float32)[0].reshape(CW, 8, 128).transpose(2, 1, 0).reshape(128, 8 * CW),
        cm(np.asarray(conv_b_dw)[0]), cm(np.asarray(conv_ln_g)[0]), cm(np.asarray(conv_ln_b)[0]), cm(np.asarray(conv_b_pw2)[0]),
    ]
    v = np.ascontiguousarray(np.concatenate(parts, axis=1).astype(np.float32))
    assert v.shape == (128, NV), v.shape
    return v


def make_in_maps(x_prompt, x_sample, state_hgrn, cache_conv, ffn_w_gate, ffn_w_up, ffn_w_down, ln_g, ln_b,
                 hgrn_w_in, hgrn_lb, hgrn_norm_g, hgrn_w_out, conv_w_pw1, conv_b_pw1, conv_w_dw, conv_b_dw,
                 conv_ln_g, conv_ln_b, conv_w_pw2, conv_b_pw2):
    f = lambda a: np.ascontiguousarray(np.asarray(a, dtype=np.float32))
    x_prompt, x_sample, state_hgrn, cache_conv = f(x_prompt), f(x_sample), f(state_hgrn), f(cache_conv)
    vecs = _vecs(ln_g, ln_b, hgrn_lb, hgrn_norm_g, conv_b_pw1, conv_w_dw, conv_b_dw, conv_ln_g, conv_ln_b, conv_b_pw2)
    s_idx = np.arange(128)[:, None]
    tri = (np.tile(np.arange(128), 4)[None, :] >= s_idx).astype(np.uint16)
    tri16 = (np.tile(np.arange(16), 2)[None, :] >= s_idx).astype(np.uint16)
    rmask = np.ascontiguousarray(np.broadcast_to((np.arange(512) % 128 != 0).astype(np.float32)[None, :], (128, 512)))
    shared = {
        "wg": f(ffn_w_gate), "wu": f(ffn_w_up), "wd": f(ffn_w_down), "win": f(hgrn_w_in)[0], "wout": f(hgrn_w_out)[0],
        "pw1": f(conv_w_pw1)[0], "pw2": f(conv_w_pw2)[0], "vecs": vecs, "tri": np.ascontiguousarray(tri),
        "tri16": np.ascontiguousarray(tri16), "rmask": rmask, "ident": np.eye(128, dtype=np.float32),
    }
    maps = []
    for i in range(NCORES):
        xa = np.concatenate([x_prompt[i], x_sample[2 * i], x_sample[2 * i + 1]], axis=0)
        m = dict(shared)
        m["xT"] = np.ascontiguousarray(xa.T)
        m["st"] = np.ascontiguousarray(state_hgrn[0, 2 * i:2 * i + 2])
        m["ccT"] = np.ascontiguousarray(cache_conv[0, 2 * i:2 * i + 2].transpose(0, 2, 1))
        maps.append(m)
    return maps


def gather(results):
    y_prompt = np.empty((8, TP, D), np.float32)
    y_sample = np.empty((16, TS, D), np.float32)
    hs_p = np.empty((1, 8, 8, 128, 128), np.float32)
    hs_s = np.empty((1, 16, 8, 128, 128), np.float32)
    cc_p = np.empty((1, 8, 30, D), np.float32)
    cc_s = np.empty((1, 16, 30, D), np.float32)
    for i, r in enumerate(results):
        y = np.asarray(r["yT"]).T
        y_prompt[i] = y[:TP]
        y_sample[2 * i] = y[TP:TP + TS]
        y_sample[2 * i + 1] = y[TP + TS:]
        hs_p[0, i] = np.asarray(r["so_p"])
        hs_s[0, 2 * i:2 * i + 2] = np.asarray(r["so_s"])
        cc_p[0, i] = np.asarray(r["cco_p"]).T
        cc_s[0, 2 * i:2 * i + 2] = np.asarray(r["cco_s"]).transpose(0, 2, 1)
    return (y_prompt, y_sample, hs_p, hs_s, cc_p, cc_s)


_NC_CACHE = {}


def kernel(**inputs):
    maps = make_in_maps(**inputs)
    if 6 not in _NC_CACHE:
        _NC_CACHE[6] = build_nc(6)
    res = run_bass_kernel_spmd(_NC_CACHE[6], maps, core_ids=list(range(NCORES)))
    return gather(res.results)
```

```python
import numpy as np
from contextlib import ExitStack
import concourse.bass as bass
import concourse.mybir as mybir
from concourse.bass_utils import run_bass_kernel_spmd

F32 = mybir.dt.float32
BF16 = mybir.dt.bfloat16
U32 = mybir.dt.uint32
U16 = mybir.dt.uint16
AF = mybir.ActivationFunctionType
ALU = mybir.AluOpType

NCORES = 8
D = 1024
TP = 2048
TS = 16
T = TP + 2 * TS
ALPHA = 2.0 ** 0.5
LN_EPS = 1e-5
TILES = [(0, 512), (512, 512), (1024, 512), (1536, 512), (2048, 32)]
CW = 31

LNG, LNB, LBO, NGO, BAO, BGO, WDWO, BDWO, CLGO, CLBO, B2O, NV = 0, 48, 96, 120, 121, 129, 137, 385, 393, 401, 409, 417
NDVE_TAPS = 17


class Res:
    __slots__ = ("w", "r")

    def __init__(self):
        self.w = {}
        self.r = {}


class Q:
    def __init__(self, name):
        self.name = name
        self.cnt = 0
        self.items = []
        self.seen = {}


class Kb:
    def __init__(self):
        self.res = {}
        self.dcnt = {}
        self.deferred = []
        self.q = {n: Q(n) for n in ("sp", "act", "dve", "pool", "pe")}
        self.arena = Res()
        self.semkeys = ["sp", "act", "dve", "pool", "pe"]

    def R(self, key):
        r = self.res.get(key)
        if r is None:
            r = self.res[key] = Res()
        return r

    def _wait(self, q, reads, writes):
        need = {}
        for r in reads:
            for k, v in r.w.items():
                if need.get(k, 0) < v:
                    need[k] = v
        skip = q.name if q.name == "pe" else None
        for w in writes:
            for k, v in w.w.items():
                if k != skip and need.get(k, 0) < v:
                    need[k] = v
            for k, v in w.r.items():
                if k != skip and need.get(k, 0) < v:
                    need[k] = v
        for k, v in need.items():
            if q.seen.get(k, 0) < v:
                q.items.append(("w", k, v))
                q.seen[k] = v

    def op(self, qn, fn, reads=(), writes=(), arena=False):
        q = self.q[qn]
        reads = list(reads)
        if arena:
            reads.append(self.arena)
        self._wait(q, reads, writes)
        q.cnt += 1
        q.items.append(("o", fn))
        for r in reads:
            r.r[q.name] = q.cnt
        for w in writes:
            w.r = {}
            w.w = {q.name: q.cnt}

    def dma(self, qn, semkey, fn, reads=(), writes=(), arena=False):
        q = self.q[qn]
        reads = list(reads)
        if arena:
            reads.append(self.arena)
        if semkey not in self.dcnt:
            self.dcnt[semkey] = 0
            self.semkeys.append(semkey)
        self._wait(q, reads, writes)
        self.dcnt[semkey] += 16
        c = self.dcnt[semkey]
        q.items.append(("d", fn, semkey))
        for r in reads:
            r.r[semkey] = c
        for w in writes:
            w.r = {}
            w.w = {semkey: c}

    def fence(self, fn):
        self.op("dve", fn, reads=(), writes=[self.arena])

    def defer(self, n, fn):
        self.deferred.append([n, fn])

    def tick(self):
        run = []
        keep = []
        for it in self.deferred:
            it[0] -= 1
            (run if it[0] <= 0 else keep).append(it)
        self.deferred = keep
        for it in run:
            it[1]()

    def flush(self):
        while self.deferred:
            self.tick()


def act(out, in_, func, bias=None, scale=None):
    kw = {}
    if bias is not None:
        kw["bias"] = bias
    if scale is not None:
        kw["scale"] = scale
    return lambda e: e.activation(out, in_, func, **kw)


def tt(out, a, b, op):
    return lambda e: e.tensor_tensor(out, a, b, op)


def ts(out, a, s1, s2, op0, op1=None):
    if op1 is None:
        return lambda e: e.tensor_scalar(out, a, s1, None, op0)
    return lambda e: e.tensor_scalar(out, a, s1, s2, op0, op1)


def stt(out, a, s, b, op0, op1):
    return lambda e: e.scalar_tensor_tensor(out, a, s, b, op0, op1)


def build_nc(nsub=6):
    nc = bass.Bass("TRN2", target_bir_lowering=False)
    dt_in = lambda name, shape, dt=F32: nc.dram_tensor(name, shape, dt, kind="ExternalInput").ap()
    dt_out = lambda name, shape: nc.dram_tensor(name, shape, F32, kind="ExternalOutput").ap()
    xT = dt_in("xT", [D, T])
    st_in = dt_in("st", [2, 8, 128, 128])
    ccT = dt_in("ccT", [2, D, 30])
    w_gate = dt_in("wg", [2, 2, D, 4 * D])
    w_up = dt_in("wu", [2, 2, D, 4 * D])
    w_down = dt_in("wd", [2, 2, 4 * D, D])
    w_in = dt_in("win", [D, 4 * D])
    w_out = dt_in("wout", [D, D])
    w_pw1 = dt_in("pw1", [D, 2 * D])
    w_pw2 = dt_in("pw2", [D, D])
    vecs_d = dt_in("vecs", [128, NV])
    tri_d = dt_in("tri", [128, 512], U16)
    tri16_d = dt_in("tri16", [128, 32], U16)
    rmask_d = dt_in("rmask", [128, 512])
    ident_d = dt_in("ident", [128, 128])
    yT = dt_out("yT", [D, T])
    so_p = dt_out("so_p", [8, 128, 128])
    so_s = dt_out("so_s", [2, 8, 128, 128])
    cco_p = dt_out("cco_p", [D, 30])
    cco_s = dt_out("cco_s", [2, D, 30])

    K = Kb()
    R = K.R
    global _LASTK
    _LASTK = K
    es = ExitStack()
    with es:
        sb = lambda name, shape, dt: es.enter_context(nc.sbuf_tensor(name, shape, dt))
        xres = sb("xres", [128, 8, T], F32)
        xbf = sb("xbf", [128, 8, T], BF16)
        A = [sb(f"A{i}", [128, 8, 512], BF16) for i in range(4)]
        Dm = [sb(f"D{i}", [128, 4, 1024], BF16) for i in range(2)]
        vecs = sb("vecs_sb", [128, NV], F32)
        vecsA = sb("vecsA", [128, 96], F32)
        lbv = sb("lbv", [128, 16], F32)
        e3 = sb("e3", [128, 32], F32)
        cst = sb("cst", [128, 4], F32)
        tri = sb("tri_sb", [128, 512], U16)
        tri16 = sb("tri16_sb", [128, 32], U16)
        rmask = sb("rmask_sb", [128, 512], BF16)
        ones = sb("ones", [128, 128], BF16)
        ident = sb("ident_sb", [128, 128], BF16)
        scm16 = sb("scm16", [128, 64], BF16)
        identf = sb("identf", [128, 128], F32)
        fz = sb("fz", [128, 2], F32)
        zb = sb("zb", [128, 8, 256], BF16)
        zsq = sb("zsq", [128, 8, 256], BF16)
        mean = sb("mean", [128, 256], F32)
        var = sb("var", [128, 256], F32)
        sd = sb("sd", [128, 256], F32)
        rstd = sb("rstd", [128, 256], F32)
        ARENA_F = 11264
        arena = sb("arena", [128, ARENA_F], F32)
        PS = [es.enter_context(nc.psum_tensor(f"ps{i}", [128, 512], F32)) for i in range(7)]
        PST = es.enter_context(nc.psum_tensor("pst", [128, 1024], BF16))

        def af(off_b, nelem):
            assert off_b % 4 == 0 and off_b // 4 + nelem <= ARENA_F
            return arena[:, off_b // 4: off_b // 4 + nelem]

        def ab(off_b, nelem):
            assert off_b % 4 == 0 and nelem % 2 == 0 and off_b // 4 + nelem // 2 <= ARENA_F
            return arena[:, off_b // 4: off_b // 4 + nelem // 2].bitcast(BF16)

        SG = [af(0, 512), af(2048, 512)]
        H = [ab(4096, 2048).rearrange("p (j t) -> p j t", t=512), ab(8192, 2048).rearrange("p (j t) -> p j t", t=512)]
        HS = []
        for x in range(2):
            b0 = x * 14336
            HS.append(dict(qs=af(b0, 512), fb=af(b0 + 2048, 512), km=af(b0 + 4096, 512), bb=af(b0 + 6144, 512),
                           Shist=af(b0 + 8192, 512).rearrange("p (c v) -> p c v", v=128),
                           q_rel=ab(b0 + 10240, 512), k_rel=ab(b0 + 11264, 512), k_relT=ab(b0 + 12288, 512), scm=ab(b0 + 13312, 512),
                           dd=af(43008 + 64 * x, 16)))
        obf = ab(28672, 2048).rearrange("p (j t) -> p j t", t=512)
        v_tok = ab(32768, 2048).rearrange("p (c v) -> p c v", v=512)
        S_p = af(36864, 512).rearrange("p (h v) -> p h v", v=128)
        S_s = [af(38912, 512).rearrange("p (h v) -> p h v", v=128), af(40960, 512).rearrange("p (h v) -> p h v", v=128)]
        U = af(0, 8 * 542).rearrange("p (c t) -> p c t", t=542)
        ybuf = af(17408, 4096).rearrange("p (c t) -> p c t", t=512)
        U_s = af(33792, 8 * 2 * 46).rearrange("p (c s t) -> p c s t", s=2, t=46)
        prod4 = [af(36736, 512), af(38784, 512), af(40832, 512), af(42880, 512)]

        RA = [R(f"A{i}") for i in range(4)]
        RD = [R(f"D{i}") for i in range(2)]
        RPS = [R(f"ps{i}") for i in range(7)]
        RPST = R("pst")
        RXB = [R(f"xbf{t}") for t in range(5)]
        RXR = [[R(f"xr{t}_{c}") for c in range(8)] for t in range(5)]
        Rvec, RvecA, Rlb, Rcst, Rtri, Rones, Rident = R("vecs"), R("vecsA"), R("lbv"), R("cst"), R("tri"), R("ones"), R("ident")
        Rzb, Rzsq, Rmean, Rvar, Rsd, Rrstd = R("zb"), R("zsq"), R("mean"), R("var"), R("sd"), R("rstd")

        vc = lambda col: vecs[:, col:col + 1]

        def mm(out_ap, pairs, reads, bank, arena_=False):
            def fn(e, pairs=pairs, out_ap=out_ap):
                n = len(pairs)
                ins = None
                for i, (l, r) in enumerate(pairs):
                    ins = e.matmul(out_ap, l, r, start=(i == 0), stop=(i == n - 1))
                return ins
            K.op("pe", fn, reads=reads, writes=[bank], arena=arena_)

        def loadA(slot, w2d, col0):
            src = w2d[:, col0:col0 + 512].rearrange("(k p) c -> p k c", p=128)
            K.dma("pool", f"wA{slot}", lambda e: e.dma_start(out=A[slot][:], in_=src), writes=[RA[slot]])

        def loadD(slot, w2d, row0):
            src = w2d[row0:row0 + 512, :].rearrange("(j p) c -> p j c", p=128)
            K.dma("pool", f"wD{slot}", lambda e: e.dma_start(out=Dm[slot][:], in_=src), writes=[RD[slot]])

        def ffn_loads(l, s, g):
            par = g % 2
            return lambda: (loadA(2 * par, w_gate[l, s], g * 512), loadA(2 * par + 1, w_up[l, s], g * 512),
                            loadD(par, w_down[l, s], g * 512))

        nop = lambda: None

        xTv = xT.rearrange("(c p) t -> p c t", p=128)
        yTv = yT.rearrange("(c p) t -> p c t", p=128)
        K.dma("sp", "iv", lambda e: e.dma_start(out=vecs[:], in_=vecs_d), writes=[Rvec])
        K.dma("sp", "it", lambda e: e.dma_start(out=tri[:], in_=tri_d), writes=[Rtri])
        K.dma("sp", "it16", lambda e: e.dma_start(out=tri16[:], in_=tri16_d), writes=[R("tri16")])
        K.dma("pool", "ii", lambda e: e.dma_start(out=ident[:], in_=ident_d), writes=[Rident])
        K.dma("pool", "irm", lambda e: e.dma_start(out=rmask[:], in_=rmask_d), writes=[R("rmask")])
        K.dma("sp", "iif", lambda e: e.dma_start(out=identf[:], in_=ident_d), writes=[R("identf")])
        for oi, ti in enumerate((4, 0, 1, 2, 3)):
            t0, n = TILES[ti]
            K.dma("sp", f"ix{ti}", lambda e, t0=t0, n=n: e.dma_start(out=xres[:, :, t0:t0 + n], in_=xTv[:, :, t0:t0 + n]),
                  writes=RXR[ti])
            K.dma("pool", f"ib{ti}", lambda e, t0=t0, n=n: e.dma_start(out=xbf[:, :, t0:t0 + n], in_=xTv[:, :, t0:t0 + n]),
                  writes=[RXB[ti]])
            if oi == 0:
                ffn_loads(0, 0, 0)()
            if oi == 1:
                ffn_loads(0, 0, 1)()
        K.op("dve", lambda e: e.memset(ones[:], 1.0), writes=[Rones])
        K.op("dve", lambda e: e.memset(cst[:, 0:1], LN_EPS), writes=[Rcst])
        K.op("dve", lambda e: e.memset(cst[:, 1:2], float(np.log(128.0 ** -0.5))), writes=[Rcst])
        K.op("dve", lambda e: e.memset(cst[:, 2:3], 1.0), writes=[Rcst])
        K.op("dve", lambda e: e.memset(scm16[:], 0.0), writes=[R("scm16")])
        K.op("dve", ts(vecsA[:], vecs[:, 0:96], ALPHA, None, ALU.mult), reads=[Rvec], writes=[RvecA])
        K.op("act", act(e3[:, 0:24], vecs[:, LBO:LBO + 24], AF.Exp), reads=[Rvec], writes=[R("e3")])
        K.op("dve", tt(e3[:, 24:32], e3[:, 0:8], e3[:, 8:16], ALU.add), reads=[R("e3")], writes=[R("e3b")])
        K.op("dve", tt(e3[:, 24:32], e3[:, 24:32], e3[:, 16:24], ALU.add), reads=[R("e3"), R("e3b")], writes=[R("e3b")])
        K.op("dve", lambda e: e.reciprocal(e3[:, 24:32], e3[:, 24:32]), reads=[R("e3b")], writes=[R("e3b")])
        K.op("dve", tt(lbv[:, 0:8], e3[:, 0:8], e3[:, 24:32], ALU.mult), reads=[R("e3"), R("e3b")], writes=[Rlb])
        K.op("dve", ts(lbv[:, 8:16], lbv[:, 0:8], -1.0, 1.0, ALU.mult, ALU.add), reads=[Rlb], writes=[Rlb])
        for ti in (4, 0, 1, 2, 3):
            t0, n = TILES[ti]
            K.op("dve", ts(xres[:, :, t0:t0 + n], xres[:, :, t0:t0 + n], ALPHA, None, ALU.mult), reads=RXR[ti], writes=RXR[ti])

        def run(gen):
            for _ in gen:
                pass

        bg = []

        def bg_step(k=1):
            for _ in range(k):
                if not bg:
                    return
                try:
                    next(bg[0])
                except StopIteration:
                    bg.pop(0)

        def bg_drain():
            while bg:
                bg_step()

        def ln_gen(zap, n, rz, ar, epilogue, one_bank=False, zt=None, spacer=0):
            zb_, zsq_, Rzb_, Rzsq_ = zt if zt is not None else (zb, zsq, Rzb, Rzsq)
            if one_bank:
                bs_, bq_, rbs, rbq = PS[6][:, 0:n], PS[6][:, 256:256 + n], RPS[6], RPS[6]
            else:
                bs_, bq_, rbs, rbq = PS[5][:, :n], PS[6][:, :n], RPS[5], RPS[6]
            K.op("act", act(zsq_[:, :, :n], zap, AF.Square), reads=rz, writes=[Rzsq_], arena=ar)
            yield
            K.op("dve", lambda e: e.tensor_copy(zb_[:, :, :n], zap), reads=rz, writes=[Rzb_], arena=ar)
            yield
            mm(bs_, [(ones[:], zb_[:, c, :n]) for c in range(8)], reads=[Rzb_, Rones], bank=rbs)
            mm(bq_, [(ones[:], zsq_[:, c, :n]) for c in range(8)], reads=[Rzsq_, Rones], bank=rbq)
            loose = (not one_bank) or spacer > 0
            if loose:
                yield
                for _ in range(spacer):
                    yield
            K.op("dve", ts(mean[:, :n], bs_, 1.0 / D, None, ALU.mult), reads=[rbs], writes=[Rmean])
            if loose:
                yield
            K.op("dve", tt(var[:, :n], mean[:, :n], mean[:, :n], ALU.mult), reads=[Rmean], writes=[Rvar])
            if loose:
                yield
            K.op("dve", stt(var[:, :n], bq_, 1.0 / D, var[:, :n], ALU.mult, ALU.subtract), reads=[rbq, Rvar], writes=[Rvar])
            yield
            K.op("act", act(sd[:, :n], var[:, :n], AF.Ln, bias=cst[:, 0:1]), reads=[Rvar, Rcst], writes=[Rsd])
            K.op("act", act(rstd[:, :n], sd[:, :n], AF.Exp, scale=-0.5), reads=[Rsd], writes=[Rrstd])
            yield
            K.op("dve", tt(zap, zap, mean[:, :n].unsqueeze(1).to_broadcast([128, 8, n]), ALU.subtract),
                 reads=rz + [Rmean], writes=rz, arena=ar)
            yield
            for _ in range(spacer):
                yield
            K.op("dve", tt(zap, zap, rstd[:, :n].unsqueeze(1).to_broadcast([128, 8, n]), ALU.mult),
                 reads=rz + [Rrstd], writes=rz, arena=ar)
            yield
            yield from epilogue()

        def halves(t0, n):
            return [(t0, 256), (t0 + 256, 256)] if n == 512 else [(t0, n)]

        def res_ln_gen(ti, lnidx, final, one_bank=False, xbf_eng="dve", zt=None, spacer=0):
            t0, n = TILES[ti]
            rz = RXR[ti]
            for (h0, nh) in halves(t0, n):
                zap = xres[:, :, h0:h0 + nh]

                def epi(h0=h0, nh=nh):
                    for c in range(8):
                        gcol, bcol = LNG + lnidx * 8 + c, LNB + lnidx * 8 + c
                        zc = xres[:, c, h0:h0 + nh]
                        if not final:
                            if xbf_eng == "dve":
                                K.op("dve", ts(xbf[:, c, h0:h0 + nh], zc, vc(gcol), vc(bcol), ALU.mult, ALU.add),
                                     reads=[RXR[ti][c], Rvec], writes=[RXB[ti]])
                            else:
                                K.op("act", act(xbf[:, c, h0:h0 + nh], zc, AF.Identity, bias=vc(bcol), scale=vc(gcol)),
                                     reads=[RXR[ti][c], Rvec], writes=[RXB[ti]])
                            yield
                            K.op("act", act(zc, zc, AF.Identity, bias=vecsA[:, bcol:bcol + 1], scale=vecsA[:, gcol:gcol + 1]),
                                 reads=[RXR[ti][c], RvecA], writes=[RXR[ti][c]])
                            yield
                        else:
                            K.op("act", act(zc, zc, AF.Identity, bias=vc(bcol), scale=vc(gcol)),
                                 reads=[RXR[ti][c], Rvec], writes=[RXR[ti][c]])
                            yield
                yield from ln_gen(zap, nh, rz, False, epi, one_bank, zt, spacer)
            if final:
                out_tile(ti)

        def res_ln(ti, lnidx, final):
            run(res_ln_gen(ti, lnidx, final))

        def out_tile(ti):
            t0, n = TILES[ti]
            K.dma("sp", "oy", lambda e: e.dma_start(out=yTv[:, :, t0:t0 + n], in_=xres[:, :, t0:t0 + n]), reads=RXR[ti])

        def ffn_phase(l, s, lnidx, final, nxt):
            K.fence(lambda e: e.memset(fz[:, 0:1], 0.0))
            unit = 0
            for g in range(8):
                par = g % 2
                ag, au, dw = A[2 * par], A[2 * par + 1], Dm[par]
                rag, rau, rdw = RA[2 * par], RA[2 * par + 1], RD[par]
                for oi, ti in enumerate((4, 0, 1, 2, 3)):
                    t0, n = TILES[ti]
                    up = unit % 2
                    unit += 1
                    hb = H[up]
                    RH = [R(f"h{up}_{j}") for j in range(4)]
                    for j in range(4):
                        gi, ui = (j % 2) * 2, (j % 2) * 2 + 1
                        mm(PS[gi][:, :n], [(ag[:, k, j * 128:(j + 1) * 128], xbf[:, k, t0:t0 + n]) for k in range(8)],
                           reads=[rag, RXB[ti]], bank=RPS[gi])
                        mm(PS[ui][:, :n], [(au[:, k, j * 128:(j + 1) * 128], xbf[:, k, t0:t0 + n]) for k in range(8)],
                           reads=[rau, RXB[ti]], bank=RPS[ui])
                        sgb = SG[j % 2]
                        rsg = R(f"sg{j % 2}")
                        K.op("act", act(sgb[:, :n], PS[gi][:, :n], AF.Silu), reads=[RPS[gi]], writes=[rsg], arena=True)
                        K.op("dve", tt(hb[:, j, :n], sgb[:, :n], PS[ui][:, :n], ALU.mult), reads=[rsg, RPS[ui]], writes=[RH[j]], arena=True)
                        bg_step(3)

                    def down(g=g, ti=ti, t0=t0, n=n, hb=hb, RH=RH, dw=dw, rdw=rdw):
                        for d in range(8):
                            bi = 4 + d % 3
                            mm(PS[bi][:, :n], [(dw[:, j, d * 128:(d + 1) * 128], hb[:, j, :n]) for j in range(4)],
                               reads=[rdw] + RH, bank=RPS[bi], arena_=True)
                            xr = xres[:, d, t0:t0 + n]
                            K.op("dve", stt(xr, PS[bi][:, :n], 0.5, xr, ALU.mult, ALU.add), reads=[RPS[bi], RXR[ti][d]], writes=[RXR[ti][d]])
                            bg_step(3)
                        if g == 7:
                            bg.append(res_ln_gen(ti, lnidx, final, True))
                    K.tick()
                    if oi == 0 and g >= 1:
                        if g + 1 < 8:
                            ffn_loads(l, s, g + 1)()
                        else:
                            nxt[0]()
                            nxt[1]()
                            nxt[2]()
                    K.defer(1, down)
            nxt[3]()
            nxt[4]()
            K.flush()
            nxt[5]()

        def hgrn_phase(nxt):
            K.fence(lambda e: e.memset(fz[:, 0:1], 0.0))
            for x in range(2):
                K.op("dve", lambda e, x=x: e.memset(HS[x]["scm"][:], 0.0), writes=[R(f"scm{x}")], arena=True)
            Rhl = R("e3")
            K.op("dve", ts(e3[:, 0:8], lbv[:, 8:16], 0.5, None, ALU.mult), reads=[Rlb, R("e3"), R("e3b")], writes=[Rhl])
            K.op("dve", ts(e3[:, 8:16], lbv[:, 8:16], -0.5, None, ALU.mult), reads=[Rlb], writes=[Rhl])
            K.op("dve", ts(e3[:, 16:24], lbv[:, 0:8], 0.5, 0.5, ALU.mult, ALU.add), reads=[Rlb], writes=[Rhl])
            K.op("dve", ts(e3[:, 24:25], vc(NGO), 128.0 ** -0.5, None, ALU.mult), reads=[Rvec, R("e3b")], writes=[R("e3b")])
            Rvt = R("v_tok")
            Robf = [R(f"obf{j}") for j in range(4)]
            RSP, RSS = R("S0"), [R("S1"), R("S2")]
            zbf = zb[:].rearrange("p c t -> p (c t)").bitcast(F32)
            zsqf = zsq[:].rearrange("p c t -> p (c t)").bitcast(F32)

            def head(hg, ti, j, x):
                t0, n = TILES[ti]
                nch, cl = (4, 128) if n == 512 else (2, 16)
                mid = cl // 2
                prompt = n == 512
                h = hg * 4 + j
                cs = slice(j * 128, (j + 1) * 128)
                B = HS[x]
                if j < 2:
                    qs, fb, Rq, Rf = B["qs"], B["fb"], R(f"qs{x}"), R(f"fb{x}")
                else:
                    zz, rz_ = (zbf, Rzb) if x == 0 else (zsqf, Rzsq)
                    qs, fb, Rq, Rf = zz[:, 0:512], zz[:, 512:1024], rz_, rz_
                km, bb, Shist = B["km"], B["bb"], B["Shist"]
                q_rel, k_rel, k_relT, dd = B["q_rel"], B["k_rel"], B["k_relT"], B["dd"]
                on_, hv, osq, gs, Sp4 = qs, qs, k_relT, fb, k_rel
                Rkm, Rbb, Rsh = R(f"km{x}"), R(f"bb{x}"), R(f"sh{x}")
                Rqr, Rkr, RkT, Rdd = R(f"q_rel{x}"), R(f"k_rel{x}"), R(f"k_relT{x}"), R(f"dd{x}")
                RS = [RSP] * 4 if prompt else RSS
                Sst = [S_p] * 4 if prompt else S_s
                msk, rmsk = (tri, Rtri) if prompt else (tri16, R("tri16"))
                scb, rscb = (B["scm"], R(f"scm{x}")) if prompt else (scm16[:, 32 * x:32 * x + 32], R(f"scm16_{x}"))
                bq, bf_, bo = x, 2 + x, 4 + x
                WQ, WF, WV, WG = A
                xin = lambda k: xbf[:, k, t0:t0 + n]
                mm(PS[bq][:, :n], [(WQ[:, k, cs], xin(k)) for k in range(8)], reads=[RXB[ti], RA[0]], bank=RPS[bq])
                mm(PS[bf_][:, :n], [(WF[:, k, cs], xin(k)) for k in range(8)], reads=[RXB[ti], RA[1]], bank=RPS[bf_])
                K.op("act", act(qs[:, :n], PS[bq][:, :n], AF.Silu), reads=[RPS[bq]], writes=[Rq], arena=True)
                K.op("act", act(fb[:, :n], PS[bf_][:, :n], AF.Tanh, scale=0.5), reads=[RPS[bf_]], writes=[Rf], arena=True)
                yield
                K.op("dve", ts(km[:, :n], fb[:, :n], e3[:, 8 + h:9 + h], e3[:, h:h + 1], ALU.mult, ALU.add), reads=[Rf, Rhl], writes=[Rkm], arena=True)
                yield
                K.op("act", act(fb[:, :n], fb[:, :n], AF.Ln, bias=e3[:, 16 + h:17 + h], scale=e3[:, h:h + 1]), reads=[Rf, Rhl], writes=[Rf], arena=True)
                yield
                if prompt:
                    K.op("dve", lambda e: e.tensor_tensor_scan(bb[:, :n], rmask[:, :n], fb[:, :n], 0.0, ALU.mult, ALU.add),
                         reads=[Rf, R("rmask")], writes=[Rbb], arena=True)
                    yield
                else:
                    for c in range(nch):
                        sl = slice(c * cl, (c + 1) * cl)
                        K.op("dve", lambda e, sl=sl: e.tensor_tensor_scan(bb[:, sl], cst[:, 2:3].to_broadcast([128, cl]), fb[:, sl], 0.0,
                                                                          ALU.mult, ALU.add),
                             reads=[Rf, Rcst], writes=[Rbb], arena=True)
                        yield
                b3 = bb[:, :n].rearrange("p (c t) -> p c t", t=cl)
                f3 = fb[:, :n].rearrange("p (c t) -> p c t", t=cl)
                dd3 = dd[:, 0:2 * nch].rearrange("p (c t) -> p c t", t=2)
                K.op("act", act(dd3, b3[:, :, mid::cl - 1 - mid], AF.Exp), reads=[Rbb], writes=[Rdd], arena=True)
                yield
                K.op("dve", tt(f3, b3, b3[:, :, mid:mid + 1].to_broadcast([128, nch, cl]), ALU.subtract), reads=[Rbb], writes=[Rf], arena=True)
                yield
                K.op("act", act(bb[:, :n], fb[:, :n], AF.Exp), reads=[Rf], writes=[Rbb], arena=True)
                yield
                K.op("act", act(fb[:, :n], fb[:, :n], AF.Exp, scale=-1.0), reads=[Rf], writes=[Rf], arena=True)
                yield
                K.op("dve", tt(q_rel[:, :n], qs[:, :n], bb[:, :n], ALU.mult), reads=[Rq, Rbb], writes=[Rqr], arena=True)
                yield
                K.op("dve", tt(k_rel[:, :n], km[:, :n], fb[:, :n], ALU.mult), reads=[Rkm, Rf], writes=[Rkr], arena=True)
                yield

                def trfn(e):
                    ins = None
                    for c in range(nch):
                        ins = e.transpose(PST[:cl, c * 128:(c + 1) * 128], k_rel[:, c * cl:(c + 1) * cl], ident[:])
                    return ins
                K.op("pe", trfn, reads=[Rkr, Rident], writes=[RPST], arena=True)
                K.op("act", act(k_relT[:cl, :nch * 128], PST[:cl, :nch * 128], AF.Copy), reads=[RPST], writes=[RkT], arena=True)
                yield

                def scfn(e):
                    ins = None
                    for c in range(nch):
                        sl = slice(c * cl, (c + 1) * cl)
                        if cl == 128:
                            ins = e.matmul(PS[bf_][0:64, sl], k_rel[:, c * cl:c * cl + 64], q_rel[:, sl], start=True, stop=True)
                            ins = e.matmul(PS[bf_][64:128, c * cl + 64:(c + 1) * cl], k_rel[:, c * cl + 64:(c + 1) * cl],
                                           q_rel[:, c * cl + 64:(c + 1) * cl], start=True, stop=True)
                        else:
                            ins = e.matmul(PS[bf_][:cl, sl], k_rel[:, sl], q_rel[:, sl], start=True, stop=True)
                    return ins
                K.op("pe", scfn, reads=[Rkr, Rqr], writes=[RPS[bf_]], arena=True)
                yield
                K.op("dve", lambda e: e.copy_predicated(scb[:cl, :nch * cl], msk[:cl, :nch * cl], PS[bf_][:cl, :nch * cl]),
                     reads=[RPS[bf_], rmsk], writes=[rscb], arena=True)
                yield

                def ufn(e):
                    ins = None
                    for c in range(nch):
                        ins = e.matmul(PS[bq][:, c * 128:(c + 1) * 128], k_relT[:cl, c * 128:(c + 1) * 128], v_tok[:cl, c, cs], start=True, stop=True)
                    return ins
                K.op("pe", ufn, reads=[RkT, Rvt], writes=[RPS[bq]], arena=True)
                yield
                mm(PS[bo][:, :n], [(WG[:, k, cs], xin(k)) for k in range(8)], reads=[RXB[ti], RA[3]], bank=RPS[bo])
                yield
                K.op("act", act(gs[:, :n], PS[bo][:, :n], AF.Silu), reads=[RPS[bo], Rf], writes=[Rf], arena=True)
                yield
                yield
                if prompt:
                    K.op("dve", lambda e: e.tensor_copy(Shist[:, 0, :], S_p[:, j, :]), reads=[RSP], writes=[Rsh], arena=True)
                    yield
                tmpU4 = km[:, :nch * 128].rearrange("p (c v) -> p c v", v=128)
                K.op("dve", tt(tmpU4, PS[bq][:, :nch * 128].rearrange("p (c v) -> p c v", v=128),
                               b3[:, :, cl - 1:cl].to_broadcast([128, nch, 128]), ALU.mult),
                     reads=[RPS[bq], Rbb, Rkm], writes=[Rkm], arena=True)
                yield
                for c in range(nch):
                    d1c = dd[:, 2 * c + 1:2 * c + 2]
                    if prompt:
                        src = Shist[:, c, :]
                        tgt = Shist[:, c + 1, :] if c < nch - 1 else S_p[:, j, :]
                        rsrc, rtgt = [Rsh], ([Rsh] if c < nch - 1 else [RSP, Rsh])
                    else:
                        K.op("dve", lambda e, c=c: e.tensor_copy(Shist[:, c, :], S_s[c][:, j, :]), reads=[RSS[c]], writes=[Rsh], arena=True)
                        yield
                        src, tgt, rsrc, rtgt = Shist[:, c, :], S_s[c][:, j, :], [Rsh], [RSS[c]]
                    K.op("dve", stt(tgt, src, d1c, tmpU4[:, c, :], ALU.mult, ALU.add), reads=rsrc + [Rdd, Rkm], writes=rtgt, arena=True)
                    yield
                K.op("dve", tt(Sp4[:, :nch * 128].rearrange("p (c v) -> p c v", v=128), Shist[:, 0:nch, :],
                               dd3[:, :, 0:1].to_broadcast([128, nch, 128]), ALU.mult), reads=[Rsh, Rdd, Rkr], writes=[Rkr], arena=True)
                yield
                yield
                for c in range(nch):
                    sl = slice(c * cl, (c + 1) * cl)
                    mm(PS[bo][:, sl], [(v_tok[:cl, c, cs], scb[:cl, sl]), (Sp4[:, c * 128:(c + 1) * 128], q_rel[:, sl])],
                       reads=[Rvt, rscb, Rkr, Rqr], bank=RPS[bo], arena_=True)
                    yield
                yield "m3b"
                K.op("act", act(osq[:, :n], PS[bo][:, :n], AF.Square), reads=[RPS[bo], RkT], writes=[RkT], arena=True)
                yield
                mm(PS[bf_][:, :n], [(ones[:], osq[:, :n])], reads=[RkT, Rones], bank=RPS[bf_], arena_=True)
                yield
                K.op("act", act(hv[:, :n], PS[bf_][:, :n], AF.Ln, bias=cst[:, 0:1], scale=1.0 / (128.0 * 128.0)), reads=[RPS[bf_], Rcst, Rq], writes=[Rq], arena=True)
                yield
                K.op("act", act(hv[:, :n], hv[:, :n], AF.Exp, scale=-0.5), reads=[Rq], writes=[Rq], arena=True)
                yield
                K.op("dve", tt(on_[:, :n], PS[bo][:, :n], hv[:, :n], ALU.mult), reads=[RPS[bo], Rq], writes=[Rq], arena=True)
                yield
                K.op("dve", stt(obf[:, j, :n], on_[:, :n], e3[:, 24:25], gs[:, :n], ALU.mult, ALU.mult), reads=[Rq, Rf, R("e3b")], writes=[Robf[j]], arena=True)
                yield

            def drive(gens):
                gens = list(gens)
                while gens:
                    for g_ in list(gens):
                        try:
                            next(g_)
                        except StopIteration:
                            gens.remove(g_)
                    bg_step()

            def tail(hg, ti):
                t0, n = TILES[ti]
                WO, RWO = Dm[hg], RD[hg]
                for d in range(8):
                    bi = 6 if d % 2 == 0 else 4
                    mm(PS[bi][:, :n], [(WO[:, j, d * 128:(d + 1) * 128], obf[:, j, :n]) for j in range(4)],
                       reads=[RWO] + Robf, bank=RPS[bi], arena_=True)
                    xr = xres[:, d, t0:t0 + n]
                    K.op("dve", tt(xr, PS[bi][:, :n], xr, ALU.add), reads=[RPS[bi], RXR[ti][d]], writes=[RXR[ti][d]])
                yield
                if hg == 1:
                    d0v = Dm[0][:, 0:2, :].rearrange("p j (c t) -> p (j c) t", t=256)
                    d1v = Dm[0][:, 2:4, :].rearrange("p j (c t) -> p (j c) t", t=256)
                    yield from res_ln_gen(ti, 1, False, True, "act", (d0v, d1v, RD[0], RD[0]))

            for hg in range(2):
                WV = A[2]
                WO, RWO = Dm[hg], RD[hg]
                K.op("dve", lambda e: e.memset(S_p[:], 0.0), writes=[RSP], arena=True)
                for s_ in range(2):
                    K.dma("sp", f"is{hg}{s_}", lambda e, s_=s_, hg=hg: e.dma_start(out=S_s[s_][:], in_=st_in[s_, hg * 4:(hg + 1) * 4].rearrange("h k v -> k h v")),
                          writes=[RSS[s_]], arena=True)
                order = [4, 0, 1, 2, 3]
                pairs = [(oi, ti, pb) for oi, ti in enumerate(order) for pb in (0, 1)]
                gen_of = {}

                def get_gens(ti):
                    if ti not in gen_of:
                        gen_of[ti] = [head(hg, ti, j, j % 2) for j in range(4)]
                    return gen_of[ti]

                def emit_s1(ti, pb):
                    g4 = get_gens(ti)
                    next(g4[2 * pb])
                    next(g4[2 * pb + 1])

                emit_s1(order[0], 0)
                first_b_pending = True
                prev = []
                prev_tile_end = None
                for (oi, ti, pb) in pairs:
                    t0, n = TILES[ti]
                    lastt = oi == len(order) - 1
                    nch, cl = (4, 128) if n == 512 else (2, 16)
                    cur = get_gens(ti)[2 * pb:2 * pb + 2]
                    if pb == 0:
                        for c in range(nch):
                            mm(PS[6][:cl, :], [(xbf[:, k, t0 + c * cl:t0 + (c + 1) * cl], WV[:, k, :]) for k in range(8)],
                               reads=[RXB[ti], RA[2]], bank=RPS[6])
                            K.op("act", act(v_tok[:cl, c, :], PS[6][:cl, :], AF.Copy), reads=[RPS[6]], writes=[Rvt], arena=True)
                        if lastt:
                            if hg == 0:
                                loadA(2, w_in, 2048 + 512)
                            else:
                                nxt[3]()
                    reached = set()

                    def step(g_, ti=ti, oi=oi, pb=pb, lastt=lastt):
                        nonlocal prev_tile_end
                        try:
                            v_ = next(g_)
                            if v_ == "m3b":
                                reached.add(id(g_))
                            return True
                        except StopIteration:
                            if g_ in prev:
                                prev.remove(g_)
                                if not prev and prev_tile_end is not None:
                                    tg_ = tail(hg, prev_tile_end)
                                    next(tg_)
                                    bg.append(tg_)
                                    emit_s1(ti, 1)
                                    prev_tile_end = None
                                elif not prev and pb == 1:
                                    if not lastt:
                                        emit_s1(order[oi + 1], 0)
                                    else:
                                        if hg == 0:
                                            loadA(0, w_in, 512)
                                            loadA(1, w_in, 1024 + 512)
                                        else:
                                            nxt[0]()
                                            nxt[1]()
                            return False

                    while len(reached) < 2:
                        for g_ in list(prev):
                            step(g_)
                        for g_ in cur:
                            if id(g_) not in reached:
                                step(g_)
                        bg_step()
                    while prev:
                        for g_ in list(prev):
                            step(g_)
                        bg_step()
                    if first_b_pending:
                        bg_drain()
                        emit_s1(order[0], 1)
                        first_b_pending = False
                    prev = list(cur)
                    prev_tile_end = ti if pb == 1 else None
                while prev:
                    for g_ in list(prev):
                        try:
                            next(g_)
                        except StopIteration:
                            prev.remove(g_)
                    bg_step()
                if hg == 0:
                    loadA(3, w_in, 3072 + 512)
                else:
                    nxt[4]()
                bg.append(tail(hg, order[-1]))
                while bg:
                    bg_step()
                if hg == 1:
                    nxt[2]()
                    nxt[5]()
                K.dma("sp", f"os{hg}", lambda e, hg=hg: e.dma_start(out=so_p[hg * 4:(hg + 1) * 4].rearrange("h k v -> k h v"), in_=S_p[:]), reads=[RSP], arena=True)
                for s_ in range(2):
                    K.dma("sp", f"os{hg}{s_}", lambda e, s_=s_, hg=hg: e.dma_start(out=so_s[s_, hg * 4:(hg + 1) * 4].rearrange("h k v -> k h v"), in_=S_s[s_][:]),
                          reads=[RSS[s_]], arena=True)

        hgrn_first = lambda: (loadA(0, w_in, 0), loadA(1, w_in, 1024), loadD(0, w_out, 0))
        hgrn_second = lambda: (loadA(2, w_in, 2048), loadA(3, w_in, 3072), loadD(1, w_out, 512))

        def conv_phase(nxt):
            K.fence(lambda e: e.memset(fz[:, 0:1], 0.0))
            RU = [R(f"U{c}") for c in range(8)]
            RUs = [R(f"Us{c}") for c in range(8)]
            RY = [R(f"y{c}") for c in range(8)]
            Rsgt = [R("sgt0"), R("sgt1")]
            K.op("dve", lambda e: e.memset(U[:, :, 0:30], 0.0), writes=RU, arena=True)
            for s_ in range(2):
                K.dma("sp", f"ic{s_}", lambda e, s_=s_: e.dma_start(out=U_s[:, :, s_, 0:30], in_=ccT[s_].rearrange("(c p) w -> p c w", p=128)),
                      writes=RUs, arena=True)
            def front(ti, cp):
                t0, n = TILES[ti]
                prompt = n == 512
                info = []
                for x in range(2):
                    c = 2 * cp + x
                    ai, gi = 2 * x, 2 * x + 1
                    cs = slice((c % 4) * 128, (c % 4 + 1) * 128)
                    mm(PS[ai][:, :n], [(A[c // 4][:, k, cs], xbf[:, k, t0:t0 + n]) for k in range(8)], reads=[RXB[ti], RA[c // 4]], bank=RPS[ai])
                    mm(PS[gi][:, :n], [(A[2 + c // 4][:, k, cs], xbf[:, k, t0:t0 + n]) for k in range(8)], reads=[RXB[ti], RA[2 + c // 4]], bank=RPS[gi])
                    if prompt:
                        ru = RU[c]
                        unew = U[:, c, 30:542]
                        pa, pg_ = PS[ai][:, :n], PS[gi][:, :n]
                        tap = (lambda c: (lambda w: U[:, c, w:w + 512]))(c)
                        yc = ybuf[:, c, :]
                    else:
                        ru = RUs[c]
                        unew = U_s[:, c, :, 30:46]
                        pa = PS[ai][:, :n].rearrange("p (s t) -> p s t", t=16)
                        pg_ = PS[gi][:, :n].rearrange("p (s t) -> p s t", t=16)
                        tap = (lambda c: (lambda w: U_s[:, c, :, w:w + 16]))(c)
                        yc = ybuf[:, c, :n].rearrange("p (s t) -> p s t", t=16)
                    K.op("act", act(unew, pg_, AF.Sigmoid, bias=vc(BGO + c)), reads=[RPS[gi], Rvec], writes=[ru], arena=True)
                    info.append(dict(c=c, x=x, ru=ru, tap=tap, yc=yc, unew=unew, pa=pa, ai=ai))
                return info

            def uop(info):
                for d_ in info:
                    c, x = d_["c"], d_["x"]
                    K.op("dve", stt(d_["unew"], d_["pa"], vc(BAO + c), d_["unew"], ALU.add, ALU.mult),
                         reads=[RPS[d_["ai"]], d_["ru"], Rvec], writes=[d_["ru"]], arena=True)

            def taps(ti, cp, info):
                t0, n = TILES[ti]
                prompt = n == 512
                npe = CW - NDVE_TAPS
                for i in range(max(NDVE_TAPS, npe)):
                    for d_ in info:
                        c, x, ru, tap, yc = d_["c"], d_["x"], d_["ru"], d_["tap"], d_["yc"]
                        if i < NDVE_TAPS:
                            w = i
                            wc = vc(WDWO + c * CW + w)
                            if w == 0:
                                K.op("dve", ts(yc, tap(0), wc, vc(BDWO + c), ALU.mult, ALU.add), reads=[ru, Rvec], writes=[RY[c]], arena=True)
                            else:
                                K.op("dve", stt(yc, tap(w), wc, yc, ALU.mult, ALU.add), reads=[ru, Rvec, RY[c]], writes=[RY[c]], arena=True)
                        if i < npe:
                            w = NDVE_TAPS + i
                            wc = vc(WDWO + c * CW + w)
                            pb_ = prod4[2 * x + i % 2]
                            rpb = R(f"prod{2 * x + i % 2}")
                            pv_ = pb_[:, :n] if prompt else pb_[:, :n].rearrange("p (s t) -> p s t", t=16)
                            K.op("act", act(pv_, tap(w), AF.Identity, scale=wc), reads=[ru, Rvec], writes=[rpb], arena=True)
                            K.op("pe", lambda e, x=x, i=i, n=n, npe=npe, pb_=pb_: e.matmul(PS[4 + x][:, :n], identf[:], pb_[:, :n], start=(i == 0), stop=(i == npe - 1)),
                                 reads=[rpb, R("identf")], writes=[RPS[4 + x]], arena=True)
                    bg_step()
                for d_ in info:
                    c, x = d_["c"], d_["x"]
                    ycf = ybuf[:, c, :n]
                    K.op("dve", tt(ycf, ycf, PS[4 + x][:, :n], ALU.add), reads=[RY[c], RPS[4 + x]], writes=[RY[c]], arena=True)
                if prompt and ti < 3:
                    for d_ in info:
                        c = d_["c"]
                        K.op("act", act(U[:, c, 0:30], U[:, c, 512:542], AF.Copy), reads=[d_["ru"]], writes=[d_["ru"]], arena=True)

            def tail(ti):
                t0, n = TILES[ti]
                hv_ = halves(t0, n)
                for hi, (h0, nh) in enumerate(hv_):
                    o0 = h0 - t0
                    zap = ybuf[:, :, o0:o0 + nh]

                    def epi(o0=o0, nh=nh):
                        for c in range(8):
                            K.op("act", act(zb[:, c, :nh], ybuf[:, c, o0:o0 + nh], AF.Silu, bias=vc(CLBO + c), scale=vc(CLGO + c)),
                                 reads=[RY[c], Rvec], writes=[Rzb], arena=True)
                            yield
                    yield from ln_gen(zap, nh, RY, True, epi, True, None, 2)
                    if hi == len(hv_) - 1:
                        yield "ybuf_free"
                    pstf = PST[:].bitcast(F32)
                    pend = None
                    for d in range(9):
                        if d < 8:
                            pb, rpb = (PS[6][:, :nh], RPS[6]) if d % 2 == 0 else (pstf[:, :nh], RPST)
                            mm(pb, [(Dm[k // 4][:, k % 4, d * 128:(d + 1) * 128], zb[:, k, :nh]) for k in range(8)],
                               reads=[Rzb, RD[0], RD[1]], bank=rpb)
                            yield
                        if pend is not None:
                            d0, pb0, rpb0 = pend
                            xr = xres[:, d0, h0:h0 + nh]
                            K.op("dve", stt(xr, pb0, vc(B2O + d0), xr, ALU.add, ALU.add), reads=[rpb0, RXR[ti][d0], Rvec], writes=[RXR[ti][d0]])
                            yield
                        if d < 8:
                            pend = (d, pb, rpb)
                    if hi == len(hv_) - 1:
                        yield "pw2_done"
                yield from res_ln_gen(ti, 4, False, True, "act", None, 2 if ti != 3 else 0)

            seq = [(ti, cp) for ti in (4, 0, 1, 2, 3) for cp in range(4)]
            cur = front(*seq[0])
            uop(cur)
            for k, (ti, cp) in enumerate(seq):
                nxt_info = front(*seq[k + 1]) if k + 1 < len(seq) else None
                if k + 1 == len(seq) - 1:
                    nxt[0]()
                    nxt[1]()
                    nxt[3]()
                    nxt[4]()
                taps(ti, cp, cur)
                if nxt_info is not None:
                    uop(nxt_info)
                cur = nxt_info
                if cp == 3:
                    if ti == 3:
                        K.dma("sp", "oc", lambda e: e.dma_start(out=cco_p.rearrange("(c p) w -> p c w", p=128), in_=U[:, :, 512:542]), reads=RU, arena=True)
                    if TILES[ti][1] != 512:
                        for s_ in range(2):
                            K.dma("sp", "oc", lambda e, s_=s_: e.dma_start(out=cco_s[s_].rearrange("(c p) w -> p c w", p=128), in_=U_s[:, :, s_, 16:46]),
                                  reads=RUs, arena=True)
                    while bg:
                        bg_step()
                    g_ = tail(ti)
                    for v_ in g_:
                        if v_ == "ybuf_free":
                            break
                    bg.append(g_)
            while bg:
                g_ = bg[0]
                done_ = False
                for v_ in g_:
                    if v_ == "pw2_done" and len(bg) == 1:
                        nxt[2]()
                        nxt[5]()
                        done_ = True
                        break
                if done_:
                    break
                bg.pop(0)

        conv_first = lambda: (loadA(0, w_pw1, 0), loadA(1, w_pw1, 512), loadD(0, w_pw2, 0))
        conv_second = lambda: (loadA(2, w_pw1, 1024), loadA(3, w_pw1, 1536), loadD(1, w_pw2, 512))

        ffn_first = lambda l, s: ffn_loads(l, s, 0)
        ffn_second = lambda l, s: ffn_loads(l, s, 1)
        phases = [
            ("ffn", 0, 0, 0), ("hgrn",), ("ffn", 0, 1, 2), ("ffn", 1, 0, 3), ("conv",), ("ffn", 1, 1, 5),
        ][:nsub]

        def first_second(p):
            if p is None:
                return [nop] * 6
            if p[0] == "ffn":
                l, s_ = p[1], p[2]
                return [lambda: loadA(0, w_gate[l, s_], 0), lambda: loadA(1, w_up[l, s_], 0), lambda: loadD(0, w_down[l, s_], 0),
                        lambda: loadA(2, w_gate[l, s_], 512), lambda: loadA(3, w_up[l, s_], 512), lambda: loadD(1, w_down[l, s_], 512)]
            if p[0] == "hgrn":
                return [lambda: loadA(0, w_in, 0), lambda: loadA(1, w_in, 1024), lambda: loadD(0, w_out, 0),
                        lambda: loadA(2, w_in, 2048), lambda: loadA(3, w_in, 3072), lambda: loadD(1, w_out, 512)]
            return [lambda: loadA(0, w_pw1, 0), lambda: loadA(1, w_pw1, 512), lambda: loadD(0, w_pw2, 0),
                    lambda: loadA(2, w_pw1, 1024), lambda: loadA(3, w_pw1, 1536), lambda: loadD(1, w_pw2, 512)]

        for i, p in enumerate(phases):
            nxt = first_second(phases[i + 1] if i + 1 < len(phases) else None)
            last = i == len(phases) - 1
            if p[0] == "ffn":
                ffn_phase(p[1], p[2], p[3], last and nsub == 6, nxt)
            elif p[0] == "hgrn":
                hgrn_phase(nxt)
            else:
                conv_phase(nxt)
        bg_drain()
        if nsub < 6:
            for ti in range(5):
                out_tile(ti)
        spq = K.q["sp"]
        for k in list(K.dcnt):
            if k.startswith("o"):
                spq.items.append(("w", k, K.dcnt[k]))

        sems = {k: es.enter_context(nc.semaphore(f"s_{k}")) for k in K.semkeys}

        def replay(q, e):
            own = sems[q.name]
            for it in q.items:
                if it[0] == "w":
                    e.wait_ge(sems[it[1]], it[2])
                elif it[0] == "o":
                    it[1](e).then_inc(own, 1)
                else:
                    it[1](e).then_inc(sems[it[2]], 16)

        with nc.Block() as block:
            @block.sync
            def _(e):
                replay(K.q["sp"], e)

            @block.scalar
            def _(e):
                replay(K.q["act"], e)

            @block.vector
            def _(e):
                replay(K.q["dve"], e)

            @block.gpsimd
            def _(e):
                replay(K.q["pool"], e)

            @block.tensor
            def _(e):
                replay(K.q["pe"], e)
    return nc


def _vecs(ln_g, ln_b, hgrn_lb, hgrn_norm_g, conv_b_pw1, conv_w_dw, conv_b_dw, conv_ln_g, conv_ln_b, conv_b_pw2):
    cm = lambda v: np.asarray(v, np.float32).reshape(-1, 8, 128).transpose(2, 0, 1).reshape(128, -1)
    parts = [
        cm(np.asarray(ln_g).reshape(6, D)), cm(np.asarray(ln_b).reshape(6, D)), cm(np.asarray(hgrn_lb)),
        np.asarray(hgrn_norm_g, np.float32)[0].reshape(128, 1),
        cm(np.asarray(conv_b_pw1)[0][:D]), cm(np.asarray(conv_b_pw1)[0][D:]),
        np.asarray(conv_w_dw, np.float32)[0].reshape(CW, 8, 128).transpose(2, 1, 0).reshape(128, 8 * CW),
        cm(np.asarray(conv_b_dw)[0]), cm(np.asarray(conv_ln_g)[0]), cm(np.asarray(conv_ln_b)[0]), cm(np.asarray(conv_b_pw2)[0]),
    ]
    v = np.ascontiguousarray(np.concatenate(parts, axis=1).astype(np.float32))
    assert v.shape == (128, NV), v.shape
    return v


def make_in_maps(x_prompt, x_sample, state_hgrn, cache_conv, ffn_w_gate, ffn_w_up, ffn_w_down, ln_g, ln_b,
                 hgrn_w_in, hgrn_lb, hgrn_norm_g, hgrn_w_out, conv_w_pw1, conv_b_pw1, conv_w_dw, conv_b_dw,
                 conv_ln_g, conv_ln_b, conv_w_pw2, conv_b_pw2):
    f = lambda a: np.ascontiguousarray(np.asarray(a, dtype=np.float32))
    x_prompt, x_sample, state_hgrn, cache_conv = f(x_prompt), f(x_sample), f(state_hgrn), f(cache_conv)
    vecs = _vecs(ln_g, ln_b, hgrn_lb, hgrn_norm_g, conv_b_pw1, conv_w_dw, conv_b_dw, conv_ln_g, conv_ln_b, conv_b_pw2)
    s_idx = np.arange(128)[:, None]
    tri = (np.tile(np.arange(128), 4)[None, :] >= s_idx).astype(np.uint16)
    tri16 = (np.tile(np.arange(16), 2)[None, :] >= s_idx).astype(np.uint16)
    rmask = np.ascontiguousarray(np.broadcast_to((np.arange(512) % 128 != 0).astype(np.float32)[None, :], (128, 512)))
    shared = {
        "wg": f(ffn_w_gate), "wu": f(ffn_w_up), "wd": f(ffn_w_down), "win": f(hgrn_w_in)[0], "wout": f(hgrn_w_out)[0],
        "pw1": f(conv_w_pw1)[0], "pw2": f(conv_w_pw2)[0], "vecs": vecs, "tri": np.ascontiguousarray(tri),
        "tri16": np.ascontiguousarray(tri16), "rmask": rmask, "ident": np.eye(128, dtype=np.float32),
    }
    maps = []
    for i in range(NCORES):
        xa = np.concatenate([x_prompt[i], x_sample[2 * i], x_sample[2 * i + 1]], axis=0)
        m = dict(shared)
        m["xT"] = np.ascontiguousarray(xa.T)
        m["st"] = np.ascontiguousarray(state_hgrn[0, 2 * i:2 * i + 2])
        m["ccT"] = np.ascontiguousarray(cache_conv[0, 2 * i:2 * i + 2].transpose(0, 2, 1))
        maps.append(m)
    return maps


def gather(results):
    y_prompt = np.empty((8, TP, D), np.float32)
    y_sample = np.empty((16, TS, D), np.float32)
    hs_p = np.empty((1, 8, 8, 128, 128), np.float32)
    hs_s = np.empty((1, 16, 8, 128, 128), np.float32)
    cc_p = np.empty((1, 8, 30, D), np.float32)
    cc_s = np.empty((1, 16, 30, D), np.float32)
    for i, r in enumerate(results):
        y = np.asarray(r["yT"]).T
        y_prompt[i] = y[:TP]
        y_sample[2 * i] = y[TP:TP + TS]
        y_sample[2 * i + 1] = y[TP + TS:]
        hs_p[0, i] = np.asarray(r["so_p"])
        hs_s[0, 2 * i:2 * i + 2] = np.asarray(r["so_s"])
        cc_p[0, i] = np.asarray(r["cco_p"]).T
        cc_s[0, 2 * i:2 * i + 2] = np.asarray(r["cco_s"]).transpose(0, 2, 1)
    return (y_prompt, y_sample, hs_p, hs_s, cc_p, cc_s)


_NC_CACHE = {}


def kernel(**inputs):
    maps = make_in_maps(**inputs)
    if 6 not in _NC_CACHE:
        _NC_CACHE[6] = build_nc(6)
    res = run_bass_kernel_spmd(_NC_CACHE[6], maps, core_ids=list(range(NCORES)))
    return gather(res.results)
```

```python
import numpy as np
from contextlib import ExitStack
import concourse.bass as bass
import concourse.mybir as mybir
from concourse.bass_utils import run_bass_kernel_spmd

F32 = mybir.dt.float32
BF16 = mybir.dt.bfloat16
U32 = mybir.dt.uint32
U16 = mybir.dt.uint16
AF = mybir.ActivationFunctionType
ALU = mybir.AluOpType

NCORES = 8
D = 1024
TP = 2048
TS = 16
T = TP + 2 * TS
ALPHA = 2.0 ** 0.5
LN_EPS = 1e-5
TILES = [(0, 512), (512, 512), (1024, 512), (1536, 512), (2048, 32)]
CW = 31

LNG, LNB, LBO, NGO, BAO, BGO, WDWO, BDWO, CLGO, CLBO, B2O, NV = 0, 48, 96, 120, 121, 129, 137, 385, 393, 401, 409, 417
NDVE_TAPS = 17


class Res:
    __slots__ = ("w", "r")

    def __init__(self):
        self.w = {}
        self.r = {}


class Q:
    def __init__(self, name):
        self.name = name
        self.cnt = 0
        self.items = []
        self.seen = {}


class Kb:
    def __init__(self):
        self.res = {}
        self.dcnt = {}
        self.deferred = []
        self.q = {n: Q(n) for n in ("sp", "act", "dve", "pool", "pe")}
        self.arena = Res()
        self.semkeys = ["sp", "act", "dve", "pool", "pe"]

    def R(self, key):
        r = self.res.get(key)
        if r is None:
            r = self.res[key] = Res()
        return r

    def _wait(self, q, reads, writes):
        need = {}
        for r in reads:
            for k, v in r.w.items():
                if need.get(k, 0) < v:
                    need[k] = v
        skip = q.name if q.name == "pe" else None
        for w in writes:
            for k, v in w.w.items():
                if k != skip and need.get(k, 0) < v:
                    need[k] = v
            for k, v in w.r.items():
                if k != skip and need.get(k, 0) < v:
                    need[k] = v
        for k, v in need.items():
            if q.seen.get(k, 0) < v:
                q.items.append(("w", k, v))
                q.seen[k] = v

    def op(self, qn, fn, reads=(), writes=(), arena=False):
        q = self.q[qn]
        reads = list(reads)
        if arena:
            reads.append(self.arena)
        self._wait(q, reads, writes)
        q.cnt += 1
        q.items.append(("o", fn))
        for r in reads:
            r.r[q.name] = q.cnt
        for w in writes:
            w.r = {}
            w.w = {q.name: q.cnt}

    def dma(self, qn, semkey, fn, reads=(), writes=(), arena=False):
        q = self.q[qn]
        reads = list(reads)
        if arena:
            reads.append(self.arena)
        if semkey not in self.dcnt:
            self.dcnt[semkey] = 0
            self.semkeys.append(semkey)
        self._wait(q, reads, writes)
        self.dcnt[semkey] += 16
        c = self.dcnt[semkey]
        q.items.append(("d", fn, semkey))
        for r in reads:
            r.r[semkey] = c
        for w in writes:
            w.r = {}
            w.w = {semkey: c}

    def fence(self, fn):
        self.op("dve", fn, reads=(), writes=[self.arena])

    def defer(self, n, fn):
        self.deferred.append([n, fn])

    def tick(self):
        run = []
        keep = []
        for it in self.deferred:
            it[0] -= 1
            (run if it[0] <= 0 else keep).append(it)
        self.deferred = keep
        for it in run:
            it[1]()

    def flush(self):
        while self.deferred:
            self.tick()


def act(out, in_, func, bias=None, scale=None):
    kw = {}
    if bias is not None:
        kw["bias"] = bias
    if scale is not None:
        kw["scale"] = scale
    return lambda e: e.activation(out, in_, func, **kw)


def tt(out, a, b, op):
    return lambda e: e.tensor_tensor(out, a, b, op)


def ts(out, a, s1, s2, op0, op1=None):
    if op1 is None:
        return lambda e: e.tensor_scalar(out, a, s1, None, op0)
    return lambda e: e.tensor_scalar(out, a, s1, s2, op0, op1)


def stt(out, a, s, b, op0, op1):
    return lambda e: e.scalar_tensor_tensor(out, a, s, b, op0, op1)


def build_nc(nsub=6):
    nc = bass.Bass("TRN2", target_bir_lowering=False)
    dt_in = lambda name, shape, dt=F32: nc.dram_tensor(name, shape, dt, kind="ExternalInput").ap()
    dt_out = lambda name, shape: nc.dram_tensor(name, shape, F32, kind="ExternalOutput").ap()
    xT = dt_in("xT", [D, T])
    st_in = dt_in("st", [2, 8, 128, 128])
    ccT = dt_in("ccT", [2, D, 30])
    w_gate = dt_in("wg", [2, 2, D, 4 * D])
    w_up = dt_in("wu", [2, 2, D, 4 * D])
    w_down = dt_in("wd", [2, 2, 4 * D, D])
    w_in = dt_in("win", [D, 4 * D])
    w_out = dt_in("wout", [D, D])
    w_pw1 = dt_in("pw1", [D, 2 * D])
    w_pw2 = dt_in("pw2", [D, D])
    vecs_d = dt_in("vecs", [128, NV])
    tri_d = dt_in("tri", [128, 512], U16)
    tri16_d = dt_in("tri16", [128, 32], U16)
    rmask_d = dt_in("rmask", [128, 512])
    ident_d = dt_in("ident", [128, 128])
    yT = dt_out("yT", [D, T])
    so_p = dt_out("so_p", [8, 128, 128])
    so_s = dt_out("so_s", [2, 8, 128, 128])
    cco_p = dt_out("cco_p", [D, 30])
    cco_s = dt_out("cco_s", [2, D, 30])

    K = Kb()
    R = K.R
    global _LASTK
    _LASTK = K
    es = ExitStack()
    with es:
        sb = lambda name, shape, dt: es.enter_context(nc.sbuf_tensor(name, shape, dt))
        xres = sb("xres", [128, 8, T], F32)
        xbf = sb("xbf", [128, 8, T], BF16)
        A = [sb(f"A{i}", [128, 8, 512], BF16) for i in range(4)]
        Dm = [sb(f"D{i}", [128, 4, 1024], BF16) for i in range(2)]
        vecs = sb("vecs_sb", [128, NV], F32)
        vecsA = sb("vecsA", [128, 96], F32)
        lbv = sb("lbv", [128, 16], F32)
        e3 = sb("e3", [128, 32], F32)
        cst = sb("cst", [128, 4], F32)
        tri = sb("tri_sb", [128, 512], U16)
        tri16 = sb("tri16_sb", [128, 32], U16)
        rmask = sb("rmask_sb", [128, 512], BF16)
        ones = sb("ones", [128, 128], BF16)
        ident = sb("ident_sb", [128, 128], BF16)
        scm16 = sb("scm16", [128, 64], BF16)
        identf = sb("identf", [128, 128], F32)
        fz = sb("fz", [128, 2], F32)
        zb = sb("zb", [128, 8, 256], BF16)
        zsq = sb("zsq", [128, 8, 256], BF16)
        mean = sb("mean", [128, 256], F32)
        var = sb("var", [128, 256], F32)
        sd = sb("sd", [128, 256], F32)
        rstd = sb("rstd", [128, 256], F32)
        ARENA_F = 11264
        arena = sb("arena", [128, ARENA_F], F32)
        PS = [es.enter_context(nc.psum_tensor(f"ps{i}", [128, 512], F32)) for i in range(7)]
        PST = es.enter_context(nc.psum_tensor("pst", [128, 1024], BF16))

        def af(off_b, nelem):
            assert off_b % 4 == 0 and off_b // 4 + nelem <= ARENA_F
            return arena[:, off_b // 4: off_b // 4 + nelem]

        def ab(off_b, nelem):
            assert off_b % 4 == 0 and nelem % 2 == 0 and off_b // 4 + nelem // 2 <= ARENA_F
            return arena[:, off_b // 4: off_b // 4 + nelem // 2].bitcast(BF16)

        SG = [af(0, 512), af(2048, 512)]
        H = [ab(4096, 2048).rearrange("p (j t) -> p j t", t=512), ab(8192, 2048).rearrange("p (j t) -> p j t", t=512)]
        HS = []
        for x in range(2):
            b0 = x * 14336
            HS.append(dict(qs=af(b0, 512), fb=af(b0 + 2048, 512), km=af(b0 + 4096, 512), bb=af(b0 + 6144, 512),
                           Shist=af(b0 + 8192, 512).rearrange("p (c v) -> p c v", v=128),
                           q_rel=ab(b0 + 10240, 512), k_rel=ab(b0 + 11264, 512), k_relT=ab(b0 + 12288, 512), scm=ab(b0 + 13312, 512),
                           dd=af(43008 + 64 * x, 16)))
        obf = ab(28672, 2048).rearrange("p (j t) -> p j t", t=512)
        v_tok = ab(32768, 2048).rearrange("p (c v) -> p c v", v=512)
        S_p = af(36864, 512).rearrange("p (h v) -> p h v", v=128)
        S_s = [af(38912, 512).rearrange("p (h v) -> p h v", v=128), af(40960, 512).rearrange("p (h v) -> p h v", v=128)]
        U = af(0, 8 * 542).rearrange("p (c t) -> p c t", t=542)
        ybuf = af(17408, 4096).rearrange("p (c t) -> p c t", t=512)
        U_s = af(33792, 8 * 2 * 46).rearrange("p (c s t) -> p c s t", s=2, t=46)
        prod4 = [af(36736, 512), af(38784, 512), af(40832, 512), af(42880, 512)]

        RA = [R(f"A{i}") for i in range(4)]
        RD = [R(f"D{i}") for i in range(2)]
        RPS = [R(f"ps{i}") for i in range(7)]
        RPST = R("pst")
        RXB = [R(f"xbf{t}") for t in range(5)]
        RXR = [[R(f"xr{t}_{c}") for c in range(8)] for t in range(5)]
        Rvec, RvecA, Rlb, Rcst, Rtri, Rones, Rident = R("vecs"), R("vecsA"), R("lbv"), R("cst"), R("tri"), R("ones"), R("ident")
        Rzb, Rzsq, Rmean, Rvar, Rsd, Rrstd = R("zb"), R("zsq"), R("mean"), R("var"), R("sd"), R("rstd")

        vc = lambda col: vecs[:, col:col + 1]

        def mm(out_ap, pairs, reads, bank, arena_=False):
            def fn(e, pairs=pairs, out_ap=out_ap):
                n = len(pairs)
                ins = None
                for i, (l, r) in enumerate(pairs):
                    ins = e.matmul(out_ap, l, r, start=(i == 0), stop=(i == n - 1))
                return ins
            K.op("pe", fn, reads=reads, writes=[bank], arena=arena_)

        def loadA(slot, w2d, col0):
            src = w2d[:, col0:col0 + 512].rearrange("(k p) c -> p k c", p=128)
            K.dma("pool", f"wA{slot}", lambda e: e.dma_start(out=A[slot][:], in_=src), writes=[RA[slot]])

        def loadD(slot, w2d, row0):
            src = w2d[row0:row0 + 512, :].rearrange("(j p) c -> p j c", p=128)
            K.dma("pool", f"wD{slot}", lambda e: e.dma_start(out=Dm[slot][:], in_=src), writes=[RD[slot]])

        def ffn_loads(l, s, g):
            par = g % 2
            return lambda: (loadA(2 * par, w_gate[l, s], g * 512), loadA(2 * par + 1, w_up[l, s], g * 512),
                            loadD(par, w_down[l, s], g * 512))

        nop = lambda: None

        xTv = xT.rearrange("(c p) t -> p c t", p=128)
        yTv = yT.rearrange("(c p) t -> p c t", p=128)
        K.dma("sp", "iv", lambda e: e.dma_start(out=vecs[:], in_=vecs_d), writes=[Rvec])
        K.dma("sp", "it", lambda e: e.dma_start(out=tri[:], in_=tri_d), writes=[Rtri])
        K.dma("sp", "it16", lambda e: e.dma_start(out=tri16[:], in_=tri16_d), writes=[R("tri16")])
        K.dma("pool", "ii", lambda e: e.dma_start(out=ident[:], in_=ident_d), writes=[Rident])
        K.dma("pool", "irm", lambda e: e.dma_start(out=rmask[:], in_=rmask_d), writes=[R("rmask")])
        K.dma("sp", "iif", lambda e: e.dma_start(out=identf[:], in_=ident_d), writes=[R("identf")])
        for oi, ti in enumerate((4, 0, 1, 2, 3)):
            t0, n = TILES[ti]
            K.dma("sp", f"ix{ti}", lambda e, t0=t0, n=n: e.dma_start(out=xres[:, :, t0:t0 + n], in_=xTv[:, :, t0:t0 + n]),
                  writes=RXR[ti])
            K.dma("pool", f"ib{ti}", lambda e, t0=t0, n=n: e.dma_start(out=xbf[:, :, t0:t0 + n], in_=xTv[:, :, t0:t0 + n]),
                  writes=[RXB[ti]])
            if oi == 0:
                ffn_loads(0, 0, 0)()
            if oi == 1:
                ffn_loads(0, 0, 1)()
        K.op("dve", lambda e: e.memset(ones[:], 1.0), writes=[Rones])
        K.op("dve", lambda e: e.memset(cst[:, 0:1], LN_EPS), writes=[Rcst])
        K.op("dve", lambda e: e.memset(cst[:, 1:2], float(np.log(128.0 ** -0.5))), writes=[Rcst])
        K.op("dve", lambda e: e.memset(cst[:, 2:3], 1.0), writes=[Rcst])
        K.op("dve", lambda e: e.memset(scm16[:], 0.0), writes=[R("scm16")])
        K.op("dve", ts(vecsA[:], vecs[:, 0:96], ALPHA, None, ALU.mult), reads=[Rvec], writes=[RvecA])
        K.op("act", act(e3[:, 0:24], vecs[:, LBO:LBO + 24], AF.Exp), reads=[Rvec], writes=[R("e3")])
        K.op("dve", tt(e3[:, 24:32], e3[:, 0:8], e3[:, 8:16], ALU.add), reads=[R("e3")], writes=[R("e3b")])
        K.op("dve", tt(e3[:, 24:32], e3[:, 24:32], e3[:, 16:24], ALU.add), reads=[R("e3"), R("e3b")], writes=[R("e3b")])
        K.op("dve", lambda e: e.reciprocal(e3[:, 24:32], e3[:, 24:32]), reads=[R("e3b")], writes=[R("e3b")])
        K.op("dve", tt(lbv[:, 0:8], e3[:, 0:8], e3[:, 24:32], ALU.mult), reads=[R("e3"), R("e3b")], writes=[Rlb])
        K.op("dve", ts(lbv[:, 8:16], lbv[:, 0:8], -1.0, 1.0, ALU.mult, ALU.add), reads=[Rlb], writes=[Rlb])
        for ti in (4, 0, 1, 2, 3):
            t0, n = TILES[ti]
            K.op("dve", ts(xres[:, :, t0:t0 + n], xres[:, :, t0:t0 + n], ALPHA, None, ALU.mult), reads=RXR[ti], writes=RXR[ti])

        def run(gen):
            for _ in gen:
                pass

        bg = []

        def bg_step(k=1):
            for _ in range(k):
                if not bg:
                    return
                try:
                    next(bg[0])
                except StopIteration:
                    bg.pop(0)

        def bg_drain():
            while bg:
                bg_step()

        def ln_gen(zap, n, rz, ar, epilogue, one_bank=False, zt=None, spacer=0):
            zb_, zsq_, Rzb_, Rzsq_ = zt if zt is not None else (zb, zsq, Rzb, Rzsq)
            if one_bank:
                bs_, bq_, rbs, rbq = PS[6][:, 0:n], PS[6][:, 256:256 + n], RPS[6], RPS[6]
            else:
                bs_, bq_, rbs, rbq = PS[5][:, :n], PS[6][:, :n], RPS[5], RPS[6]
            K.op("act", act(zsq_[:, :, :n], zap, AF.Square), reads=rz, writes=[Rzsq_], arena=ar)
            yield
            K.op("dve", lambda e: e.tensor_copy(zb_[:, :, :n], zap), reads=rz, writes=[Rzb_], arena=ar)
            yield
            mm(bs_, [(ones[:], zb_[:, c, :n]) for c in range(8)], reads=[Rzb_, Rones], bank=rbs)
            mm(bq_, [(ones[:], zsq_[:, c, :n]) for c in range(8)], reads=[Rzsq_, Rones], bank=rbq)
            loose = (not one_bank) or spacer > 0
            if loose:
                yield
                for _ in range(spacer):
                    yield
            K.op("dve", ts(mean[:, :n], bs_, 1.0 / D, None, ALU.mult), reads=[rbs], writes=[Rmean])
            if loose:
                yield
            K.op("dve", tt(var[:, :n], mean[:, :n], mean[:, :n], ALU.mult), reads=[Rmean], writes=[Rvar])
            if loose:
                yield
            K.op("dve", stt(var[:, :n], bq_, 1.0 / D, var[:, :n], ALU.mult, ALU.subtract), reads=[rbq, Rvar], writes=[Rvar])
            yield
            K.op("act", act(sd[:, :n], var[:, :n], AF.Ln, bias=cst[:, 0:1]), reads=[Rvar, Rcst], writes=[Rsd])
            K.op("act", act(rstd[:, :n], sd[:, :n], AF.Exp, scale=-0.5), reads=[Rsd], writes=[Rrstd])
            yield
            K.op("dve", tt(zap, zap, mean[:, :n].unsqueeze(1).to_broadcast([128, 8, n]), ALU.subtract),
                 reads=rz + [Rmean], writes=rz, arena=ar)
            yield
            for _ in range(spacer):
                yield
            K.op("dve", tt(zap, zap, rstd[:, :n].unsqueeze(1).to_broadcast([128, 8, n]), ALU.mult),
                 reads=rz + [Rrstd], writes=rz, arena=ar)
            yield
            yield from epilogue()

        def halves(t0, n):
            return [(t0, 256), (t0 + 256, 256)] if n == 512 else [(t0, n)]

        def res_ln_gen(ti, lnidx, final, one_bank=False, xbf_eng="dve", zt=None, spacer=0):
            t0, n = TILES[ti]
            rz = RXR[ti]
            for (h0, nh) in halves(t0, n):
                zap = xres[:, :, h0:h0 + nh]

                def epi(h0=h0, nh=nh):
                    for c in range(8):
                        gcol, bcol = LNG + lnidx * 8 + c, LNB + lnidx * 8 + c
                        zc = xres[:, c, h0:h0 + nh]
                        if not final:
                            if xbf_eng == "dve":
                                K.op("dve", ts(xbf[:, c, h0:h0 + nh], zc, vc(gcol), vc(bcol), ALU.mult, ALU.add),
                                     reads=[RXR[ti][c], Rvec], writes=[RXB[ti]])
                            else:
                                K.op("act", act(xbf[:, c, h0:h0 + nh], zc, AF.Identity, bias=vc(bcol), scale=vc(gcol)),
                                     reads=[RXR[ti][c], Rvec], writes=[RXB[ti]])
                            yield
                            K.op("act", act(zc, zc, AF.Identity, bias=vecsA[:, bcol:bcol + 1], scale=vecsA[:, gcol:gcol + 1]),
                                 reads=[RXR[ti][c], RvecA], writes=[RXR[ti][c]])
                            yield
                        else:
                            K.op("act", act(zc, zc, AF.Identity, bias=vc(bcol), scale=vc(gcol)),
                                 reads=[RXR[ti][c], Rvec], writes=[RXR[ti][c]])
                            yield
                yield from ln_gen(zap, nh, rz, False, epi, one_bank, zt, spacer)
            if final:
                out_tile(ti)

        def res_ln(ti, lnidx, final):
            run(res_ln_gen(ti, lnidx, final))

        def out_tile(ti):
            t0, n = TILES[ti]
            K.dma("sp", "oy", lambda e: e.dma_start(out=yTv[:, :, t0:t0 + n], in_=xres[:, :, t0:t0 + n]), reads=RXR[ti])

        def ffn_phase(l, s, lnidx, final, nxt):
            K.fence(lambda e: e.memset(fz[:, 0:1], 0.0))
            unit = 0
            for g in range(8):
                par = g % 2
                ag, au, dw = A[2 * par], A[2 * par + 1], Dm[par]
                rag, rau, rdw = RA[2 * par], RA[2 * par + 1], RD[par]
                for oi, ti in enumerate((4, 0, 1, 2, 3)):
                    t0, n = TILES[ti]
                    up = unit % 2
                    unit += 1
                    hb = H[up]
                    RH = [R(f"h{up}_{j}") for j in range(4)]
                    for j in range(4):
                        gi, ui = (j % 2) * 2, (j % 2) * 2 + 1
                        mm(PS[gi][:, :n], [(ag[:, k, j * 128:(j + 1) * 128], xbf[:, k, t0:t0 + n]) for k in range(8)],
                           reads=[rag, RXB[ti]], bank=RPS[gi])
                        mm(PS[ui][:, :n], [(au[:, k, j * 128:(j + 1) * 128], xbf[:, k, t0:t0 + n]) for k in range(8)],
                           reads=[rau, RXB[ti]], bank=RPS[ui])
                        sgb = SG[j % 2]
                        rsg = R(f"sg{j % 2}")
                        K.op("act", act(sgb[:, :n], PS[gi][:, :n], AF.Silu), reads=[RPS[gi]], writes=[rsg], arena=True)
                        K.op("dve", tt(hb[:, j, :n], sgb[:, :n], PS[ui][:, :n], ALU.mult), reads=[rsg, RPS[ui]], writes=[RH[j]], arena=True)
                        bg_step(3)

                    def down(g=g, ti=ti, t0=t0, n=n, hb=hb, RH=RH, dw=dw, rdw=rdw):
                        for d in range(8):
                            bi = 4 + d % 3
                            mm(PS[bi][:, :n], [(dw[:, j, d * 128:(d + 1) * 128], hb[:, j, :n]) for j in range(4)],
                               reads=[rdw] + RH, bank=RPS[bi], arena_=True)
                            xr = xres[:, d, t0:t0 + n]
                            K.op("dve", stt(xr, PS[bi][:, :n], 0.5, xr, ALU.mult, ALU.add), reads=[RPS[bi], RXR[ti][d]], writes=[RXR[ti][d]])
                            bg_step(3)
                        if g == 7:
                            bg.append(res_ln_gen(ti, lnidx, final, True))
                    K.tick()
                    if oi == 0 and g >= 1:
                        if g + 1 < 8:
                            ffn_loads(l, s, g + 1)()
                        else:
                            nxt[0]()
                            nxt[1]()
                            nxt[2]()
                    K.defer(1, down)
            nxt[3]()
            nxt[4]()
            K.flush()
            nxt[5]()

        def hgrn_phase(nxt):
            K.fence(lambda e: e.memset(fz[:, 0:1], 0.0))
            for x in range(2):
                K.op("dve", lambda e, x=x: e.memset(HS[x]["scm"][:], 0.0), writes=[R(f"scm{x}")], arena=True)
            Rhl = R("e3")
            K.op("dve", ts(e3[:, 0:8], lbv[:, 8:16], 0.5, None, ALU.mult), reads=[Rlb, R("e3"), R("e3b")], writes=[Rhl])
            K.op("dve", ts(e3[:, 8:16], lbv[:, 8:16], -0.5, None, ALU.mult), reads=[Rlb], writes=[Rhl])
            K.op("dve", ts(e3[:, 16:24], lbv[:, 0:8], 0.5, 0.5, ALU.mult, ALU.add), reads=[Rlb], writes=[Rhl])
            K.op("dve", ts(e3[:, 24:25], vc(NGO), 128.0 ** -0.5, None, ALU.mult), reads=[Rvec, R("e3b")], writes=[R("e3b")])
            Rvt = R("v_tok")
            Robf = [R(f"obf{j}") for j in range(4)]
            RSP, RSS = R("S0"), [R("S1"), R("S2")]
            zbf = zb[:].rearrange("p c t -> p (c t)").bitcast(F32)
            zsqf = zsq[:].rearrange("p c t -> p (c t)").bitcast(F32)

            def head(hg, ti, j, x):
                t0, n = TILES[ti]
                nch, cl = (4, 128) if n == 512 else (2, 16)
                mid = cl // 2
                prompt = n == 512
                h = hg * 4 + j
                cs = slice(j * 128, (j + 1) * 128)
                B = HS[x]
                if j < 2:
                    qs, fb, Rq, Rf = B["qs"], B["fb"], R(f"qs{x}"), R(f"fb{x}")
                else:
                    zz, rz_ = (zbf, Rzb) if x == 0 else (zsqf, Rzsq)
                    qs, fb, Rq, Rf = zz[:, 0:512], zz[:, 512:1024], rz_, rz_
                km, bb, Shist = B["km"], B["bb"], B["Shist"]
                q_rel, k_rel, k_relT, dd = B["q_rel"], B["k_rel"], B["k_relT"], B["dd"]
                on_, hv, osq, gs, Sp4 = qs, qs, k_relT, fb, k_rel
                Rkm, Rbb, Rsh = R(f"km{x}"), R(f"bb{x}"), R(f"sh{x}")
                Rqr, Rkr, RkT, Rdd = R(f"q_rel{x}"), R(f"k_rel{x}"), R(f"k_relT{x}"), R(f"dd{x}")
                RS = [RSP] * 4 if prompt else RSS
                Sst = [S_p] * 4 if prompt else S_s
                msk, rmsk = (tri, Rtri) if prompt else (tri16, R("tri16"))
                scb, rscb = (B["scm"], R(f"scm{x}")) if prompt else (scm16[:, 32 * x:32 * x + 32], R(f"scm16_{x}"))
                bq, bf_, bo = x, 2 + x, 4 + x
                WQ, WF, WV, WG = A
                xin = lambda k: xbf[:, k, t0:t0 + n]
                mm(PS[bq][:, :n], [(WQ[:, k, cs], xin(k)) for k in range(8)], reads=[RXB[ti], RA[0]], bank=RPS[bq])
                mm(PS[bf_][:, :n], [(WF[:, k, cs], xin(k)) for k in range(8)], reads=[RXB[ti], RA[1]], bank=RPS[bf_])
                K.op("act", act(qs[:, :n], PS[bq][:, :n], AF.Silu), reads=[RPS[bq]], writes=[Rq], arena=True)
                K.op("act", act(fb[:, :n], PS[bf_][:, :n], AF.Tanh, scale=0.5), reads=[RPS[bf_]], writes=[Rf], arena=True)
                yield
                K.op("dve", ts(km[:, :n], fb[:, :n], e3[:, 8 + h:9 + h], e3[:, h:h + 1], ALU.mult, ALU.add), reads=[Rf, Rhl], writes=[Rkm], arena=True)
                yield
                K.op("act", act(fb[:, :n], fb[:, :n], AF.Ln, bias=e3[:, 16 + h:17 + h], scale=e3[:, h:h + 1]), reads=[Rf, Rhl], writes=[Rf], arena=True)
                yield
                if prompt:
                    K.op("dve", lambda e: e.tensor_tensor_scan(bb[:, :n], rmask[:, :n], fb[:, :n], 0.0, ALU.mult, ALU.add),
                         reads=[Rf, R("rmask")], writes=[Rbb], arena=True)
                    yield
                else:
                    for c in range(nch):
                        sl = slice(c * cl, (c + 1) * cl)
                        K.op("dve", lambda e, sl=sl: e.tensor_tensor_scan(bb[:, sl], cst[:, 2:3].to_broadcast([128, cl]), fb[:, sl], 0.0,
                                                                          ALU.mult, ALU.add),
                             reads=[Rf, Rcst], writes=[Rbb], arena=True)
                        yield
                b3 = bb[:, :n].rearrange("p (c t) -> p c t", t=cl)
                f3 = fb[:, :n].rearrange("p (c t) -> p c t", t=cl)
                dd3 = dd[:, 0:2 * nch].rearrange("p (c t) -> p c t", t=2)
                K.op("act", act(dd3, b3[:, :, mid::cl - 1 - mid], AF.Exp), reads=[Rbb], writes=[Rdd], arena=True)
                yield
                K.op("dve", tt(f3, b3, b3[:, :, mid:mid + 1].to_broadcast([128, nch, cl]), ALU.subtract), reads=[Rbb], writes=[Rf], arena=True)
                yield
                K.op("act", act(bb[:, :n], fb[:, :n], AF.Exp), reads=[Rf], writes=[Rbb], arena=True)
                yield
                K.op("act", act(fb[:, :n], fb[:, :n], AF.Exp, scale=-1.0), reads=[Rf], writes=[Rf], arena=True)
                yield
                K.op("dve", tt(q_rel[:, :n], qs[:, :n], bb[:, :n], ALU.mult), reads=[Rq, Rbb], writes=[Rqr], arena=True)
                yield
                K.op("dve", tt(k_rel[:, :n], km[:, :n], fb[:, :n], ALU.mult), reads=[Rkm, Rf], writes=[Rkr], arena=True)
                yield

                def trfn(e):
                    ins = None
                    for c in range(nch):
                        ins = e.transpose(PST[:cl, c * 128:(c + 1) * 128], k_rel[:, c * cl:(c + 1) * cl], ident[:])
                    return ins
                K.op("pe", trfn, reads=[Rkr, Rident], writes=[RPST], arena=True)
                K.op("act", act(k_relT[:cl, :nch * 128], PST[:cl, :nch * 128], AF.Copy), reads=[RPST], writes=[RkT], arena=True)
                yield

                def scfn(e):
                    ins = None
                    for c in range(nch):
                        sl = slice(c * cl, (c + 1) * cl)
                        if cl == 128:
                            ins = e.matmul(PS[bf_][0:64, sl], k_rel[:, c * cl:c * cl + 64], q_rel[:, sl], start=True, stop=True)
                            ins = e.matmul(PS[bf_][64:128, c * cl + 64:(c + 1) * cl], k_rel[:, c * cl + 64:(c + 1) * cl],
                                           q_rel[:, c * cl + 64:(c + 1) * cl], start=True, stop=True)
                        else:
                            ins = e.matmul(PS[bf_][:cl, sl], k_rel[:, sl], q_rel[:, sl], start=True, stop=True)
                    return ins
                K.op("pe", scfn, reads=[Rkr, Rqr], writes=[RPS[bf_]], arena=True)
                yield
                K.op("dve", lambda e: e.copy_predicated(scb[:cl, :nch * cl], msk[:cl, :nch * cl], PS[bf_][:cl, :nch * cl]),
                     reads=[RPS[bf_], rmsk], writes=[rscb], arena=True)
                yield

                def ufn(e):
                    ins = None
                    for c in range(nch):
                        ins = e.matmul(PS[bq][:, c * 128:(c + 1) * 128], k_relT[:cl, c * 128:(c + 1) * 128], v_tok[:cl, c, cs], start=True, stop=True)
                    return ins
                K.op("pe", ufn, reads=[RkT, Rvt], writes=[RPS[bq]], arena=True)
                yield
                mm(PS[bo][:, :n], [(WG[:, k, cs], xin(k)) for k in range(8)], reads=[RXB[ti], RA[3]], bank=RPS[bo])
                yield
                K.op("act", act(gs[:, :n], PS[bo][:, :n], AF.Silu), reads=[RPS[bo], Rf], writes=[Rf], arena=True)
                yield
                yield
                if prompt:
                    K.op("dve", lambda e: e.tensor_copy(Shist[:, 0, :], S_p[:, j, :]), reads=[RSP], writes=[Rsh], arena=True)
                    yield
                tmpU4 = km[:, :nch * 128].rearrange("p (c v) -> p c v", v=128)
                K.op("dve", tt(tmpU4, PS[bq][:, :nch * 128].rearrange("p (c v) -> p c v", v=128),
                               b3[:, :, cl - 1:cl].to_broadcast([128, nch, 128]), ALU.mult),
                     reads=[RPS[bq], Rbb, Rkm], writes=[Rkm], arena=True)
                yield "mU"
                for c in range(nch):
                    d1c = dd[:, 2 * c + 1:2 * c + 2]
                    if prompt:
                        src = Shist[:, c, :]
                        tgt = Shist[:, c + 1, :] if c < nch - 1 else S_p[:, j, :]
                        rsrc, rtgt = [Rsh], ([Rsh] if c < nch - 1 else [RSP, Rsh])
                    else:
                        K.op("dve", lambda e, c=c: e.tensor_copy(Shist[:, c, :], S_s[c][:, j, :]), reads=[RSS[c]], writes=[Rsh], arena=True)
                        yield
                        src, tgt, rsrc, rtgt = Shist[:, c, :], S_s[c][:, j, :], [Rsh], [RSS[c]]
                    K.op("dve", stt(tgt, src, d1c, tmpU4[:, c, :], ALU.mult, ALU.add), reads=rsrc + [Rdd, Rkm], writes=rtgt, arena=True)
                    yield
                K.op("dve", tt(Sp4[:, :nch * 128].rearrange("p (c v) -> p c v", v=128), Shist[:, 0:nch, :],
                               dd3[:, :, 0:1].to_broadcast([128, nch, 128]), ALU.mult), reads=[Rsh, Rdd, Rkr], writes=[Rkr], arena=True)
                yield
                yield
                for c in range(nch):
                    sl = slice(c * cl, (c + 1) * cl)
                    mm(PS[bo][:, sl], [(v_tok[:cl, c, cs], scb[:cl, sl]), (Sp4[:, c * 128:(c + 1) * 128], q_rel[:, sl])],
                       reads=[Rvt, rscb, Rkr, Rqr], bank=RPS[bo], arena_=True)
                    yield
                yield "m3b"
                K.op("act", act(osq[:, :n], PS[bo][:, :n], AF.Square), reads=[RPS[bo], RkT], writes=[RkT], arena=True)
                yield
                mm(PS[bf_][:, :n], [(ones[:], osq[:, :n])], reads=[RkT, Rones], bank=RPS[bf_], arena_=True)
                yield
                K.op("act", act(hv[:, :n], PS[bf_][:, :n], AF.Ln, bias=cst[:, 0:1], scale=1.0 / (128.0 * 128.0)), reads=[RPS[bf_], Rcst, Rq], writes=[Rq], arena=True)
                yield
                K.op("act", act(hv[:, :n], hv[:, :n], AF.Exp, scale=-0.5), reads=[Rq], writes=[Rq], arena=True)
                yield
                K.op("dve", tt(on_[:, :n], PS[bo][:, :n], hv[:, :n], ALU.mult), reads=[RPS[bo], Rq], writes=[Rq], arena=True)
                yield
                K.op("dve", stt(obf[:, j, :n], on_[:, :n], e3[:, 24:25], gs[:, :n], ALU.mult, ALU.mult), reads=[Rq, Rf, R("e3b")], writes=[Robf[j]], arena=True)
                yield

            def drive(gens):
                gens = list(gens)
                while gens:
                    for g_ in list(gens):
                        try:
                            next(g_)
                        except StopIteration:
                            gens.remove(g_)
                    bg_step()

            def tail(hg, ti):
                t0, n = TILES[ti]
                WO, RWO = Dm[hg], RD[hg]
                for d in range(8):
                    bi = 6 if d % 2 == 0 else 4
                    mm(PS[bi][:, :n], [(WO[:, j, d * 128:(d + 1) * 128], obf[:, j, :n]) for j in range(4)],
                       reads=[RWO] + Robf, bank=RPS[bi], arena_=True)
                    xr = xres[:, d, t0:t0 + n]
                    K.op("dve", tt(xr, PS[bi][:, :n], xr, ALU.add), reads=[RPS[bi], RXR[ti][d]], writes=[RXR[ti][d]])
                yield
                if hg == 1:
                    d0v = Dm[0][:, 0:2, :].rearrange("p j (c t) -> p (j c) t", t=256)
                    d1v = Dm[0][:, 2:4, :].rearrange("p j (c t) -> p (j c) t", t=256)
                    yield from res_ln_gen(ti, 1, False, True, "act", (d0v, d1v, RD[0], RD[0]))

            for hg in range(2):
                WV = A[2]
                WO, RWO = Dm[hg], RD[hg]
                K.op("dve", lambda e: e.memset(S_p[:], 0.0), writes=[RSP], arena=True)
                for s_ in range(2):
                    K.dma("sp", f"is{hg}{s_}", lambda e, s_=s_, hg=hg: e.dma_start(out=S_s[s_][:], in_=st_in[s_, hg * 4:(hg + 1) * 4].rearrange("h k v -> k h v")),
                          writes=[RSS[s_]], arena=True)
                order = [4, 0, 1, 2, 3]
                pairs = [(oi, ti, pb) for oi, ti in enumerate(order) for pb in (0, 1)]
                gen_of = {}

                def get_gens(ti):
                    if ti not in gen_of:
                        gen_of[ti] = [head(hg, ti, j, j % 2) for j in range(4)]
                    return gen_of[ti]

                def emit_s1(ti, pb):
                    g4 = get_gens(ti)
                    next(g4[2 * pb])
                    next(g4[2 * pb + 1])

                emit_s1(order[0], 0)
                first_b_pending = True
                prev = []
                prev_tile_end = None
                for (oi, ti, pb) in pairs:
                    t0, n = TILES[ti]
                    lastt = oi == len(order) - 1
                    nch, cl = (4, 128) if n == 512 else (2, 16)
                    cur = get_gens(ti)[2 * pb:2 * pb + 2]
                    if pb == 0:
                        for c in range(nch):
                            mm(PS[6][:cl, :], [(xbf[:, k, t0 + c * cl:t0 + (c + 1) * cl], WV[:, k, :]) for k in range(8)],
                               reads=[RXB[ti], RA[2]], bank=RPS[6])
                            K.op("act", act(v_tok[:cl, c, :], PS[6][:cl, :], AF.Copy), reads=[RPS[6]], writes=[Rvt], arena=True)
                        if lastt:
                            if hg == 0:
                                loadA(2, w_in, 2048 + 512)
                            else:
                                nxt[3]()
                    reached = set()
                    reachedU = set()
                    pend_s1 = []

                    def step(g_, ti=ti, oi=oi, pb=pb, lastt=lastt):
                        nonlocal prev_tile_end
                        try:
                            v_ = next(g_)
                            if v_ == "m3b":
                                reached.add(id(g_))
                            elif v_ == "mU":
                                reachedU.add(id(g_))
                            return True
                        except StopIteration:
                            if g_ in prev:
                                prev.remove(g_)
                                if not prev and prev_tile_end is not None:
                                    tg_ = tail(hg, prev_tile_end)
                                    next(tg_)
                                    bg.append(tg_)
                                    pend_s1.append(lambda: emit_s1(ti, 1))
                                    prev_tile_end = None
                                elif not prev and pb == 1:
                                    if not lastt:
                                        pend_s1.append(lambda: emit_s1(order[oi + 1], 0))
                                    else:
                                        if hg == 0:
                                            loadA(0, w_in, 512)
                                            loadA(1, w_in, 1024 + 512)
                                        else:
                                            nxt[0]()
                                            nxt[1]()
                            return False

                    while len(reached) < 2:
                        for g_ in list(prev):
                            step(g_)
                        for g_ in cur:
                            if id(g_) not in reached:
                                step(g_)
                        if pend_s1 and len(reachedU) == 2:
                            pend_s1.pop(0)()
                        bg_step()
                    while prev:
                        for g_ in list(prev):
                            step(g_)
                        bg_step()
                    while pend_s1:
                        pend_s1.pop(0)()
                    if first_b_pending:
                        bg_drain()
                        emit_s1(order[0], 1)
                        first_b_pending = False
                    prev = list(cur)
                    prev_tile_end = ti if pb == 1 else None
                while prev:
                    for g_ in list(prev):
                        try:
                            next(g_)
                        except StopIteration:
                            prev.remove(g_)
                    bg_step()
                if hg == 0:
                    loadA(3, w_in, 3072 + 512)
                else:
                    nxt[4]()
                bg.append(tail(hg, order[-1]))
                while bg:
                    bg_step()
                if hg == 1:
                    nxt[2]()
                    nxt[5]()
                K.dma("sp", f"os{hg}", lambda e, hg=hg: e.dma_start(out=so_p[hg * 4:(hg + 1) * 4].rearrange("h k v -> k h v"), in_=S_p[:]), reads=[RSP], arena=True)
                for s_ in range(2):
                    K.dma("sp", f"os{hg}{s_}", lambda e, s_=s_, hg=hg: e.dma_start(out=so_s[s_, hg * 4:(hg + 1) * 4].rearrange("h k v -> k h v"), in_=S_s[s_][:]),
                          reads=[RSS[s_]], arena=True)

        hgrn_first = lambda: (loadA(0, w_in, 0), loadA(1, w_in, 1024), loadD(0, w_out, 0))
        hgrn_second = lambda: (loadA(2, w_in, 2048), loadA(3, w_in, 3072), loadD(1, w_out, 512))

        def conv_phase(nxt):
            K.fence(lambda e: e.memset(fz[:, 0:1], 0.0))
            RU = [R(f"U{c}") for c in range(8)]
            RUs = [R(f"Us{c}") for c in range(8)]
            RY = [R(f"y{c}") for c in range(8)]
            Rsgt = [R("sgt0"), R("sgt1")]
            K.op("dve", lambda e: e.memset(U[:, :, 0:30], 0.0), writes=RU, arena=True)
            for s_ in range(2):
                K.dma("sp", f"ic{s_}", lambda e, s_=s_: e.dma_start(out=U_s[:, :, s_, 0:30], in_=ccT[s_].rearrange("(c p) w -> p c w", p=128)),
                      writes=RUs, arena=True)
            def front(ti, cp):
                t0, n = TILES[ti]
                prompt = n == 512
                info = []
                for x in range(2):
                    c = 2 * cp + x
                    ai, gi = 2 * x, 2 * x + 1
                    cs = slice((c % 4) * 128, (c % 4 + 1) * 128)
                    mm(PS[ai][:, :n], [(A[c // 4][:, k, cs], xbf[:, k, t0:t0 + n]) for k in range(8)], reads=[RXB[ti], RA[c // 4]], bank=RPS[ai])
                    mm(PS[gi][:, :n], [(A[2 + c // 4][:, k, cs], xbf[:, k, t0:t0 + n]) for k in range(8)], reads=[RXB[ti], RA[2 + c // 4]], bank=RPS[gi])
                    if prompt:
                        ru = RU[c]
                        unew = U[:, c, 30:542]
                        pa, pg_ = PS[ai][:, :n], PS[gi][:, :n]
                        tap = (lambda c: (lambda w: U[:, c, w:w + 512]))(c)
                        yc = ybuf[:, c, :]
                    else:
                        ru = RUs[c]
                        unew = U_s[:, c, :, 30:46]
                        pa = PS[ai][:, :n].rearrange("p (s t) -> p s t", t=16)
                        pg_ = PS[gi][:, :n].rearrange("p (s t) -> p s t", t=16)
                        tap = (lambda c: (lambda w: U_s[:, c, :, w:w + 16]))(c)
                        yc = ybuf[:, c, :n].rearrange("p (s t) -> p s t", t=16)
                    K.op("act", act(unew, pg_, AF.Sigmoid, bias=vc(BGO + c)), reads=[RPS[gi], Rvec], writes=[ru], arena=True)
                    info.append(dict(c=c, x=x, ru=ru, tap=tap, yc=yc, unew=unew, pa=pa, ai=ai))
                return info

            def uop(info):
                for d_ in info:
                    c, x = d_["c"], d_["x"]
                    K.op("dve", stt(d_["unew"], d_["pa"], vc(BAO + c), d_["unew"], ALU.add, ALU.mult),
                         reads=[RPS[d_["ai"]], d_["ru"], Rvec], writes=[d_["ru"]], arena=True)

            def taps(ti, cp, info):
                t0, n = TILES[ti]
                prompt = n == 512
                npe = CW - NDVE_TAPS
                for i in range(max(NDVE_TAPS, npe)):
                    for d_ in info:
                        c, x, ru, tap, yc = d_["c"], d_["x"], d_["ru"], d_["tap"], d_["yc"]
                        if i < NDVE_TAPS:
                            w = i
                            wc = vc(WDWO + c * CW + w)
                            if w == 0:
                                K.op("dve", ts(yc, tap(0), wc, vc(BDWO + c), ALU.mult, ALU.add), reads=[ru, Rvec], writes=[RY[c]], arena=True)
                            else:
                                K.op("dve", stt(yc, tap(w), wc, yc, ALU.mult, ALU.add), reads=[ru, Rvec, RY[c]], writes=[RY[c]], arena=True)
                        if i < npe:
                            w = NDVE_TAPS + i
                            wc = vc(WDWO + c * CW + w)
                            pb_ = prod4[2 * x + i % 2]
                            rpb = R(f"prod{2 * x + i % 2}")
                            pv_ = pb_[:, :n] if prompt else pb_[:, :n].rearrange("p (s t) -> p s t", t=16)
                            K.op("act", act(pv_, tap(w), AF.Identity, scale=wc), reads=[ru, Rvec], writes=[rpb], arena=True)
                            K.op("pe", lambda e, x=x, i=i, n=n, npe=npe, pb_=pb_: e.matmul(PS[4 + x][:, :n], identf[:], pb_[:, :n], start=(i == 0), stop=(i == npe - 1)),
                                 reads=[rpb, R("identf")], writes=[RPS[4 + x]], arena=True)
                    bg_step()
                for d_ in info:
                    c, x = d_["c"], d_["x"]
                    ycf = ybuf[:, c, :n]
                    K.op("dve", tt(ycf, ycf, PS[4 + x][:, :n], ALU.add), reads=[RY[c], RPS[4 + x]], writes=[RY[c]], arena=True)
                if prompt and ti < 3:
                    for d_ in info:
                        c = d_["c"]
                        K.op("act", act(U[:, c, 0:30], U[:, c, 512:542], AF.Copy), reads=[d_["ru"]], writes=[d_["ru"]], arena=True)

            def tail(ti):
                t0, n = TILES[ti]
                hv_ = halves(t0, n)
                for hi, (h0, nh) in enumerate(hv_):
                    o0 = h0 - t0
                    zap = ybuf[:, :, o0:o0 + nh]

                    def epi(o0=o0, nh=nh):
                        for c in range(8):
                            K.op("act", act(zb[:, c, :nh], ybuf[:, c, o0:o0 + nh], AF.Silu, bias=vc(CLBO + c), scale=vc(CLGO + c)),
                                 reads=[RY[c], Rvec], writes=[Rzb], arena=True)
                            yield
                    yield from ln_gen(zap, nh, RY, True, epi, True, None, 2)
                    if hi == len(hv_) - 1:
                        yield "ybuf_free"
                    pstf = PST[:].bitcast(F32)
                    pend = None
                    for d in range(9):
                        if d < 8:
                            pb, rpb = (PS[6][:, :nh], RPS[6]) if d % 2 == 0 else (pstf[:, :nh], RPST)
                            mm(pb, [(Dm[k // 4][:, k % 4, d * 128:(d + 1) * 128], zb[:, k, :nh]) for k in range(8)],
                               reads=[Rzb, RD[0], RD[1]], bank=rpb)
                            yield
                        if pend is not None:
                            d0, pb0, rpb0 = pend
                            xr = xres[:, d0, h0:h0 + nh]
                            K.op("dve", stt(xr, pb0, vc(B2O + d0), xr, ALU.add, ALU.add), reads=[rpb0, RXR[ti][d0], Rvec], writes=[RXR[ti][d0]])
                            yield
                        if d < 8:
                            pend = (d, pb, rpb)
                    if hi == len(hv_) - 1:
                        yield "pw2_done"
                yield from res_ln_gen(ti, 4, False, True, "act", None, 2 if ti != 3 else 0)

            seq = [(ti, cp) for ti in (4, 0, 1, 2, 3) for cp in range(4)]
            cur = front(*seq[0])
            uop(cur)
            for k, (ti, cp) in enumerate(seq):
                nxt_info = front(*seq[k + 1]) if k + 1 < len(seq) else None
                if k + 1 == len(seq) - 1:
                    nxt[0]()
                    nxt[1]()
                    nxt[3]()
                    nxt[4]()
                taps(ti, cp, cur)
                if nxt_info is not None:
                    uop(nxt_info)
                cur = nxt_info
                if cp == 3:
                    if ti == 3:
                        K.dma("sp", "oc", lambda e: e.dma_start(out=cco_p.rearrange("(c p) w -> p c w", p=128), in_=U[:, :, 512:542]), reads=RU, arena=True)
                    if TILES[ti][1] != 512:
                        for s_ in range(2):
                            K.dma("sp", "oc", lambda e, s_=s_: e.dma_start(out=cco_s[s_].rearrange("(c p) w -> p c w", p=128), in_=U_s[:, :, s_, 16:46]),
                                  reads=RUs, arena=True)
                    while bg:
                        bg_step()
                    g_ = tail(ti)
                    for v_ in g_:
                        if v_ == "ybuf_free":
                            break
                    bg.append(g_)
            while bg:
                g_ = bg[0]
                done_ = False
                for v_ in g_:
                    if v_ == "pw2_done" and len(bg) == 1:
                        nxt[2]()
                        nxt[5]()
                        done_ = True
                        break
                if done_:
                    break
                bg.pop(0)

        conv_first = lambda: (loadA(0, w_pw1, 0), loadA(1, w_pw1, 512), loadD(0, w_pw2, 0))
        conv_second = lambda: (loadA(2, w_pw1, 1024), loadA(3, w_pw1, 1536), loadD(1, w_pw2, 512))

        ffn_first = lambda l, s: ffn_loads(l, s, 0)
        ffn_second = lambda l, s: ffn_loads(l, s, 1)
        phases = [
            ("ffn", 0, 0, 0), ("hgrn",), ("ffn", 0, 1, 2), ("ffn", 1, 0, 3), ("conv",), ("ffn", 1, 1, 5),
        ][:nsub]

        def first_second(p):
            if p is None:
                return [nop] * 6
            if p[0] == "ffn":
                l, s_ = p[1], p[2]
                return [lambda: loadA(0, w_gate[l, s_], 0), lambda: loadA(1, w_up[l, s_], 0), lambda: loadD(0, w_down[l, s_], 0),
                        lambda: loadA(2, w_gate[l, s_], 512), lambda: loadA(3, w_up[l, s_], 512), lambda: loadD(1, w_down[l, s_], 512)]
            if p[0] == "hgrn":
                return [lambda: loadA(0, w_in, 0), lambda: loadA(1, w_in, 1024), lambda: loadD(0, w_out, 0),
                        lambda: loadA(2, w_in, 2048), lambda: loadA(3, w_in, 3072), lambda: loadD(1, w_out, 512)]
            return [lambda: loadA(0, w_pw1, 0), lambda: loadA(1, w_pw1, 512), lambda: loadD(0, w_pw2, 0),
                    lambda: loadA(2, w_pw1, 1024), lambda: loadA(3, w_pw1, 1536), lambda: loadD(1, w_pw2, 512)]

        for i, p in enumerate(phases):
            nxt = first_second(phases[i + 1] if i + 1 < len(phases) else None)
            last = i == len(phases) - 1
            if p[0] == "ffn":
                ffn_phase(p[1], p[2], p[3], last and nsub == 6, nxt)
            elif p[0] == "hgrn":
                hgrn_phase(nxt)
            else:
                conv_phase(nxt)
        bg_drain()
        if nsub < 6:
            for ti in range(5):
                out_tile(ti)
        spq = K.q["sp"]
        for k in list(K.dcnt):
            if k.startswith("o"):
                spq.items.append(("w", k, K.dcnt[k]))

        sems = {k: es.enter_context(nc.semaphore(f"s_{k}")) for k in K.semkeys}

        def replay(q, e):
            own = sems[q.name]
            for it in q.items:
                if it[0] == "w":
                    e.wait_ge(sems[it[1]], it[2])
                elif it[0] == "o":
                    it[1](e).then_inc(own, 1)
                else:
                    it[1](e).then_inc(sems[it[2]], 16)

        with nc.Block() as block:
            @block.sync
            def _(e):
                replay(K.q["sp"], e)

            @block.scalar
            def _(e):
                replay(K.q["act"], e)

            @block.vector
            def _(e):
                replay(K.q["dve"], e)

            @block.gpsimd
            def _(e):
                replay(K.q["pool"], e)

            @block.tensor
            def _(e):
                replay(K.q["pe"], e)
    return nc


def _vecs(ln_g, ln_b, hgrn_lb, hgrn_norm_g, conv_b_pw1, conv_w_dw, conv_b_dw, conv_ln_g, conv_ln_b, conv_b_pw2):
    cm = lambda v: np.asarray(v, np.float32).reshape(-1, 8, 128).transpose(2, 0, 1).reshape(128, -1)
    parts = [
        cm(np.asarray(ln_g).reshape(6, D)), cm(np.asarray(ln_b).reshape(6, D)), cm(np.asarray(hgrn_lb)),
        np.asarray(hgrn_norm_g, np.float32)[0].reshape(128, 1),
        cm(np.asarray(conv_b_pw1)[0][:D]), cm(np.asarray(conv_b_pw1)[0][D:]),
        np.asarray(conv_w_dw, np.float32)[0].reshape(CW, 8, 128).transpose(2, 1, 0).reshape(128, 8 * CW),
        cm(np.asarray(conv_b_dw)[0]), cm(np.asarray(conv_ln_g)[0]), cm(np.asarray(conv_ln_b)[0]), cm(np.asarray(conv_b_pw2)[0]),
    ]
    v = np.ascontiguousarray(np.concatenate(parts, axis=1).astype(np.float32))
    assert v.shape == (128, NV), v.shape
    return v


def make_in_maps(x_prompt, x_sample, state_hgrn, cache_conv, ffn_w_gate, ffn_w_up, ffn_w_down, ln_g, ln_b,
                 hgrn_w_in, hgrn_lb, hgrn_norm_g, hgrn_w_out, conv_w_pw1, conv_b_pw1, conv_w_dw, conv_b_dw,
                 conv_ln_g, conv_ln_b, conv_w_pw2, conv_b_pw2):
    f = lambda a: np.ascontiguousarray(np.asarray(a, dtype=np.float32))
    x_prompt, x_sample, state_hgrn, cache_conv = f(x_prompt), f(x_sample), f(state_hgrn), f(cache_conv)
    vecs = _vecs(ln_g, ln_b, hgrn_lb, hgrn_norm_g, conv_b_pw1, conv_w_dw, conv_b_dw, conv_ln_g, conv_ln_b, conv_b_pw2)
    s_idx = np.arange(128)[:, None]
    tri = (np.tile(np.arange(128), 4)[None, :] >= s_idx).astype(np.uint16)
    tri16 = (np.tile(np.arange(16), 2)[None, :] >= s_idx).astype(np.uint16)
    rmask = np.ascontiguousarray(np.broadcast_to((np.arange(512) % 128 != 0).astype(np.float32)[None, :], (128, 512)))
    shared = {
        "wg": f(ffn_w_gate), "wu": f(ffn_w_up), "wd": f(ffn_w_down), "win": f(hgrn_w_in)[0], "wout": f(hgrn_w_out)[0],
        "pw1": f(conv_w_pw1)[0], "pw2": f(conv_w_pw2)[0], "vecs": vecs, "tri": np.ascontiguousarray(tri),
        "tri16": np.ascontiguousarray(tri16), "rmask": rmask, "ident": np.eye(128, dtype=np.float32),
    }
    maps = []
    for i in range(NCORES):
        xa = np.concatenate([x_prompt[i], x_sample[2 * i], x_sample[2 * i + 1]], axis=0)
        m = dict(shared)
        m["xT"] = np.ascontiguousarray(xa.T)
        m["st"] = np.ascontiguousarray(state_hgrn[0, 2 * i:2 * i + 2])
        m["ccT"] = np.ascontiguousarray(cache_conv[0, 2 * i:2 * i + 2].transpose(0, 2, 1))
        maps.append(m)
    return maps


def gather(results):
    y_prompt = np.empty((8, TP, D), np.float32)
    y_sample = np.empty((16, TS, D), np.float32)
    hs_p = np.empty((1, 8, 8, 128, 128), np.float32)
    hs_s = np.empty((1, 16, 8, 128, 128), np.float32)
    cc_p = np.empty((1, 8, 30, D), np.float32)
    cc_s = np.empty((1, 16, 30, D), np.float32)
    for i, r in enumerate(results):
        y = np.asarray(r["yT"]).T
        y_prompt[i] = y[:TP]
        y_sample[2 * i] = y[TP:TP + TS]
        y_sample[2 * i + 1] = y[TP + TS:]
        hs_p[0, i] = np.asarray(r["so_p"])
        hs_s[0, 2 * i:2 * i + 2] = np.asarray(r["so_s"])
        cc_p[0, i] = np.asarray(r["cco_p"]).T
        cc_s[0, 2 * i:2 * i + 2] = np.asarray(r["cco_s"]).transpose(0, 2, 1)
    return (y_prompt, y_sample, hs_p, hs_s, cc_p, cc_s)


_NC_CACHE = {}


def kernel(**inputs):
    maps = make_in_maps(**inputs)
    if 6 not in _NC_CACHE:
        _NC_CACHE[6] = build_nc(6)
    res = run_bass_kernel_spmd(_NC_CACHE[6], maps, core_ids=list(range(NCORES)))
    return gather(res.results)
```

```python
import numpy as np
from contextlib import ExitStack
import concourse.bass as bass
import concourse.mybir as mybir
from concourse.bass_utils import run_bass_kernel_spmd

F32 = mybir.dt.float32
BF16 = mybir.dt.bfloat16
U32 = mybir.dt.uint32
U16 = mybir.dt.uint16
AF = mybir.ActivationFunctionType
ALU = mybir.AluOpType

NCORES = 8
D = 1024
TP = 2048
TS = 16
T = TP + 2 * TS
ALPHA = 2.0 ** 0.5
LN_EPS = 1e-5
TILES = [(0, 512), (512, 512), (1024, 512), (1536, 512), (2048, 32)]
CW = 31

LNG, LNB, LBO, NGO, BAO, BGO, WDWO, BDWO, CLGO, CLBO, B2O, NV = 0, 48, 96, 120, 121, 129, 137, 385, 393, 401, 409, 417
NDVE_TAPS = 19


class Res:
    __slots__ = ("w", "r")

    def __init__(self):
        self.w = {}
        self.r = {}


class Q:
    def __init__(self, name):
        self.name = name
        self.cnt = 0
        self.items = []
        self.seen = {}


class Kb:
    def __init__(self):
        self.res = {}
        self.dcnt = {}
        self.deferred = []
        self.q = {n: Q(n) for n in ("sp", "act", "dve", "pool", "pe")}
        self.arena = Res()
        self.semkeys = ["sp", "act", "dve", "pool", "pe"]

    def R(self, key):
        r = self.res.get(key)
        if r is None:
            r = self.res[key] = Res()
        return r

    def _wait(self, q, reads, writes):
        need = {}
        for r in reads:
            for k, v in r.w.items():
                if need.get(k, 0) < v:
                    need[k] = v
        skip = q.name if q.name == "pe" else None
        for w in writes:
            for k, v in w.w.items():
                if k != skip and need.get(k, 0) < v:
                    need[k] = v
            for k, v in w.r.items():
                if k != skip and need.get(k, 0) < v:
                    need[k] = v
        for k, v in need.items():
            if q.seen.get(k, 0) < v:
                q.items.append(("w", k, v))
                q.seen[k] = v

    def op(self, qn, fn, reads=(), writes=(), arena=False):
        q = self.q[qn]
        reads = list(reads)
        if arena:
            reads.append(self.arena)
        self._wait(q, reads, writes)
        q.cnt += 1
        q.items.append(("o", fn))
        for r in reads:
            r.r[q.name] = q.cnt
        for w in writes:
            w.r = {}
            w.w = {q.name: q.cnt}

    def dma(self, qn, semkey, fn, reads=(), writes=(), arena=False):
        q = self.q[qn]
        reads = list(reads)
        if arena:
            reads.append(self.arena)
        if semkey not in self.dcnt:
            self.dcnt[semkey] = 0
            self.semkeys.append(semkey)
        self._wait(q, reads, writes)
        self.dcnt[semkey] += 16
        c = self.dcnt[semkey]
        q.items.append(("d", fn, semkey))
        for r in reads:
            r.r[semkey] = c
        for w in writes:
            w.r = {}
            w.w = {semkey: c}

    def fence(self, fn):
        self.op("dve", fn, reads=(), writes=[self.arena])

    def defer(self, n, fn):
        self.deferred.append([n, fn])

    def tick(self):
        run = []
        keep = []
        for it in self.deferred:
            it[0] -= 1
            (run if it[0] <= 0 else keep).append(it)
        self.deferred = keep
        for it in run:
            it[1]()

    def flush(self):
        while self.deferred:
            self.tick()


def act(out, in_, func, bias=None, scale=None):
    kw = {}
    if bias is not None:
        kw["bias"] = bias
    if scale is not None:
        kw["scale"] = scale
    return lambda e: e.activation(out, in_, func, **kw)


def tt(out, a, b, op):
    return lambda e: e.tensor_tensor(out, a, b, op)


def ts(out, a, s1, s2, op0, op1=None):
    if op1 is None:
        return lambda e: e.tensor_scalar(out, a, s1, None, op0)
    return lambda e: e.tensor_scalar(out, a, s1, s2, op0, op1)


def stt(out, a, s, b, op0, op1):
    return lambda e: e.scalar_tensor_tensor(out, a, s, b, op0, op1)


def build_nc(nsub=6):
    nc = bass.Bass("TRN2", target_bir_lowering=False)
    dt_in = lambda name, shape, dt=F32: nc.dram_tensor(name, shape, dt, kind="ExternalInput").ap()
    dt_out = lambda name, shape: nc.dram_tensor(name, shape, F32, kind="ExternalOutput").ap()
    xT = dt_in("xT", [D, T])
    st_in = dt_in("st", [2, 8, 128, 128])
    ccT = dt_in("ccT", [2, D, 30])
    w_gate = dt_in("wg", [2, 2, D, 4 * D])
    w_up = dt_in("wu", [2, 2, D, 4 * D])
    w_down = dt_in("wd", [2, 2, 4 * D, D])
    w_in = dt_in("win", [D, 4 * D])
    w_out = dt_in("wout", [D, D])
    w_pw1 = dt_in("pw1", [D, 2 * D])
    w_pw2 = dt_in("pw2", [D, D])
    vecs_d = dt_in("vecs", [128, NV])
    tri_d = dt_in("tri", [128, 512], U16)
    tri16_d = dt_in("tri16", [128, 32], U16)
    rmask_d = dt_in("rmask", [128, 512])
    ident_d = dt_in("ident", [128, 128])
    yT = dt_out("yT", [D, T])
    so_p = dt_out("so_p", [8, 128, 128])
    so_s = dt_out("so_s", [2, 8, 128, 128])
    cco_p = dt_out("cco_p", [D, 30])
    cco_s = dt_out("cco_s", [2, D, 30])

    K = Kb()
    R = K.R
    global _LASTK
    _LASTK = K
    es = ExitStack()
    with es:
        sb = lambda name, shape, dt: es.enter_context(nc.sbuf_tensor(name, shape, dt))
        xres = sb("xres", [128, 8, T], F32)
        xbf = sb("xbf", [128, 8, T], BF16)
        A = [sb(f"A{i}", [128, 8, 512], BF16) for i in range(4)]
        Dm = [sb(f"D{i}", [128, 4, 1024], BF16) for i in range(2)]
        vecs = sb("vecs_sb", [128, NV], F32)
        vecsA = sb("vecsA", [128, 96], F32)
        lbv = sb("lbv", [128, 16], F32)
        e3 = sb("e3", [128, 32], F32)
        cst = sb("cst", [128, 4], F32)
        tri = sb("tri_sb", [128, 512], U16)
        tri16 = sb("tri16_sb", [128, 32], U16)
        rmask = sb("rmask_sb", [128, 512], BF16)
        ones = sb("ones", [128, 128], BF16)
        ident = sb("ident_sb", [128, 128], BF16)
        scm16 = sb("scm16", [128, 64], BF16)
        identf = sb("identf", [128, 128], F32)
        fz = sb("fz", [128, 2], F32)
        zb = sb("zb", [128, 8, 256], BF16)
        zsq = sb("zsq", [128, 8, 256], BF16)
        mean = sb("mean", [128, 256], F32)
        var = sb("var", [128, 256], F32)
        sd = sb("sd", [128, 256], F32)
        rstd = sb("rstd", [128, 256], F32)
        ARENA_F = 11264
        arena = sb("arena", [128, ARENA_F], F32)
        PS = [es.enter_context(nc.psum_tensor(f"ps{i}", [128, 512], F32)) for i in range(7)]
        PST = es.enter_context(nc.psum_tensor("pst", [128, 1024], BF16))

        def af(off_b, nelem):
            assert off_b % 4 == 0 and off_b // 4 + nelem <= ARENA_F
            return arena[:, off_b // 4: off_b // 4 + nelem]

        def ab(off_b, nelem):
            assert off_b % 4 == 0 and nelem % 2 == 0 and off_b // 4 + nelem // 2 <= ARENA_F
            return arena[:, off_b // 4: off_b // 4 + nelem // 2].bitcast(BF16)

        SG = [af(0, 512), af(2048, 512)]
        H = [ab(4096, 2048).rearrange("p (j t) -> p j t", t=512), ab(8192, 2048).rearrange("p (j t) -> p j t", t=512)]
        HS = []
        for x in range(2):
            b0 = x * 14336
            HS.append(dict(qs=af(b0, 512), fb=af(b0 + 2048, 512), km=af(b0 + 4096, 512), bb=af(b0 + 6144, 512),
                           Shist=af(b0 + 8192, 512).rearrange("p (c v) -> p c v", v=128),
                           q_rel=ab(b0 + 10240, 512), k_rel=ab(b0 + 11264, 512), k_relT=ab(b0 + 12288, 512), scm=ab(b0 + 13312, 512),
                           dd=af(43008 + 64 * x, 16)))
        obf = ab(28672, 2048).rearrange("p (j t) -> p j t", t=512)
        v_tok = ab(32768, 2048).rearrange("p (c v) -> p c v", v=512)
        S_p = af(36864, 512).rearrange("p (h v) -> p h v", v=128)
        S_s = [af(38912, 512).rearrange("p (h v) -> p h v", v=128), af(40960, 512).rearrange("p (h v) -> p h v", v=128)]
        U = af(0, 8 * 542).rearrange("p (c t) -> p c t", t=542)
        ybuf = af(17408, 4096).rearrange("p (c t) -> p c t", t=512)
        U_s = af(33792, 8 * 2 * 46).rearrange("p (c s t) -> p c s t", s=2, t=46)
        prod4 = [af(36736, 512), af(38784, 512), af(40832, 512), af(42880, 512)]

        RA = [R(f"A{i}") for i in range(4)]
        RD = [R(f"D{i}") for i in range(2)]
        RPS = [R(f"ps{i}") for i in range(7)]
        RPST = R("pst")
        RXB = [R(f"xbf{t}") for t in range(5)]
        RXR = [[R(f"xr{t}_{c}") for c in range(8)] for t in range(5)]
        Rvec, RvecA, Rlb, Rcst, Rtri, Rones, Rident = R("vecs"), R("vecsA"), R("lbv"), R("cst"), R("tri"), R("ones"), R("ident")
        Rzb, Rzsq, Rmean, Rvar, Rsd, Rrstd = R("zb"), R("zsq"), R("mean"), R("var"), R("sd"), R("rstd")

        vc = lambda col: vecs[:, col:col + 1]

        def mm(out_ap, pairs, reads, bank, arena_=False):
            def fn(e, pairs=pairs, out_ap=out_ap):
                n = len(pairs)
                ins = None
                for i, (l, r) in enumerate(pairs):
                    ins = e.matmul(out_ap, l, r, start=(i == 0), stop=(i == n - 1))
                return ins
            K.op("pe", fn, reads=reads, writes=[bank], arena=arena_)

        def loadA(slot, w2d, col0):
            src = w2d[:, col0:col0 + 512].rearrange("(k p) c -> p k c", p=128)
            K.dma("pool", f"wA{slot}", lambda e: e.dma_start(out=A[slot][:], in_=src), writes=[RA[slot]])

        def loadD(slot, w2d, row0):
            src = w2d[row0:row0 + 512, :].rearrange("(j p) c -> p j c", p=128)
            K.dma("pool", f"wD{slot}", lambda e: e.dma_start(out=Dm[slot][:], in_=src), writes=[RD[slot]])

        def ffn_loads(l, s, g):
            par = g % 2
            return lambda: (loadA(2 * par, w_gate[l, s], g * 512), loadA(2 * par + 1, w_up[l, s], g * 512),
                            loadD(par, w_down[l, s], g * 512))

        nop = lambda: None

        xTv = xT.rearrange("(c p) t -> p c t", p=128)
        yTv = yT.rearrange("(c p) t -> p c t", p=128)
        K.dma("sp", "iv", lambda e: e.dma_start(out=vecs[:], in_=vecs_d), writes=[Rvec])
        K.dma("sp", "it", lambda e: e.dma_start(out=tri[:], in_=tri_d), writes=[Rtri])
        K.dma("sp", "it16", lambda e: e.dma_start(out=tri16[:], in_=tri16_d), writes=[R("tri16")])
        K.dma("pool", "ii", lambda e: e.dma_start(out=ident[:], in_=ident_d), writes=[Rident])
        K.dma("pool", "irm", lambda e: e.dma_start(out=rmask[:], in_=rmask_d), writes=[R("rmask")])
        K.dma("sp", "iif", lambda e: e.dma_start(out=identf[:], in_=ident_d), writes=[R("identf")])
        for oi, ti in enumerate((4, 0, 1, 2, 3)):
            t0, n = TILES[ti]
            K.dma("sp", f"ix{ti}", lambda e, t0=t0, n=n: e.dma_start(out=xres[:, :, t0:t0 + n], in_=xTv[:, :, t0:t0 + n]),
                  writes=RXR[ti])
            K.dma("pool", f"ib{ti}", lambda e, t0=t0, n=n: e.dma_start(out=xbf[:, :, t0:t0 + n], in_=xTv[:, :, t0:t0 + n]),
                  writes=[RXB[ti]])
            if oi == 0:
                ffn_loads(0, 0, 0)()
            if oi == 1:
                ffn_loads(0, 0, 1)()
        K.op("dve", lambda e: e.memset(ones[:], 1.0), writes=[Rones])
        K.op("dve", lambda e: e.memset(cst[:, 0:1], LN_EPS), writes=[Rcst])
        K.op("dve", lambda e: e.memset(cst[:, 1:2], float(np.log(128.0 ** -0.5))), writes=[Rcst])
        K.op("dve", lambda e: e.memset(cst[:, 2:3], 1.0), writes=[Rcst])
        K.op("dve", lambda e: e.memset(scm16[:], 0.0), writes=[R("scm16")])
        K.op("dve", ts(vecsA[:], vecs[:, 0:96], ALPHA, None, ALU.mult), reads=[Rvec], writes=[RvecA])
        K.op("act", act(e3[:, 0:24], vecs[:, LBO:LBO + 24], AF.Exp), reads=[Rvec], writes=[R("e3")])
        K.op("dve", tt(e3[:, 24:32], e3[:, 0:8], e3[:, 8:16], ALU.add), reads=[R("e3")], writes=[R("e3b")])
        K.op("dve", tt(e3[:, 24:32], e3[:, 24:32], e3[:, 16:24], ALU.add), reads=[R("e3"), R("e3b")], writes=[R("e3b")])
        K.op("dve", lambda e: e.reciprocal(e3[:, 24:32], e3[:, 24:32]), reads=[R("e3b")], writes=[R("e3b")])
        K.op("dve", tt(lbv[:, 0:8], e3[:, 0:8], e3[:, 24:32], ALU.mult), reads=[R("e3"), R("e3b")], writes=[Rlb])
        K.op("dve", ts(lbv[:, 8:16], lbv[:, 0:8], -1.0, 1.0, ALU.mult, ALU.add), reads=[Rlb], writes=[Rlb])
        for ti in (4, 0, 1, 2, 3):
            t0, n = TILES[ti]
            K.op("dve", ts(xres[:, :, t0:t0 + n], xres[:, :, t0:t0 + n], ALPHA, None, ALU.mult), reads=RXR[ti], writes=RXR[ti])

        def run(gen):
            for _ in gen:
                pass

        bg = []

        def bg_step(k=1):
            for _ in range(k):
                if not bg:
                    return
                try:
                    next(bg[0])
                except StopIteration:
                    bg.pop(0)

        def bg_drain():
            while bg:
                bg_step()

        def ln_gen(zap, n, rz, ar, epilogue, one_bank=False, zt=None, spacer=0):
            zb_, zsq_, Rzb_, Rzsq_ = zt if zt is not None else (zb, zsq, Rzb, Rzsq)
            if one_bank:
                bs_, bq_, rbs, rbq = PS[6][:, 0:n], PS[6][:, 256:256 + n], RPS[6], RPS[6]
            else:
                bs_, bq_, rbs, rbq = PS[5][:, :n], PS[6][:, :n], RPS[5], RPS[6]
            K.op("act", act(zsq_[:, :, :n], zap, AF.Square), reads=rz, writes=[Rzsq_], arena=ar)
            yield
            K.op("dve", lambda e: e.tensor_copy(zb_[:, :, :n], zap), reads=rz, writes=[Rzb_], arena=ar)
            yield
            mm(bs_, [(ones[:], zb_[:, c, :n]) for c in range(8)], reads=[Rzb_, Rones], bank=rbs)
            mm(bq_, [(ones[:], zsq_[:, c, :n]) for c in range(8)], reads=[Rzsq_, Rones], bank=rbq)
            loose = (not one_bank) or spacer > 0
            if loose:
                yield
                for _ in range(spacer):
                    yield
            K.op("dve", ts(mean[:, :n], bs_, 1.0 / D, None, ALU.mult), reads=[rbs], writes=[Rmean])
            if loose:
                yield
            K.op("dve", tt(var[:, :n], mean[:, :n], mean[:, :n], ALU.mult), reads=[Rmean], writes=[Rvar])
            if loose:
                yield
            K.op("dve", stt(var[:, :n], bq_, 1.0 / D, var[:, :n], ALU.mult, ALU.subtract), reads=[rbq, Rvar], writes=[Rvar])
            yield
            K.op("act", act(sd[:, :n], var[:, :n], AF.Ln, bias=cst[:, 0:1]), reads=[Rvar, Rcst], writes=[Rsd])
            K.op("act", act(rstd[:, :n], sd[:, :n], AF.Exp, scale=-0.5), reads=[Rsd], writes=[Rrstd])
            yield
            K.op("dve", tt(zap, zap, mean[:, :n].unsqueeze(1).to_broadcast([128, 8, n]), ALU.subtract),
                 reads=rz + [Rmean], writes=rz, arena=ar)
            yield
            for _ in range(spacer):
                yield
            K.op("dve", tt(zap, zap, rstd[:, :n].unsqueeze(1).to_broadcast([128, 8, n]), ALU.mult),
                 reads=rz + [Rrstd], writes=rz, arena=ar)
            yield
            yield from epilogue()

        def halves(t0, n):
            return [(t0, 256), (t0 + 256, 256)] if n == 512 else [(t0, n)]

        def res_ln_gen(ti, lnidx, final, one_bank=False, xbf_eng="dve", zt=None, spacer=0):
            t0, n = TILES[ti]
            rz = RXR[ti]
            for (h0, nh) in halves(t0, n):
                zap = xres[:, :, h0:h0 + nh]

                def epi(h0=h0, nh=nh):
                    for c in range(8):
                        gcol, bcol = LNG + lnidx * 8 + c, LNB + lnidx * 8 + c
                        zc = xres[:, c, h0:h0 + nh]
                        if not final:
                            if xbf_eng == "dve":
                                K.op("dve", ts(xbf[:, c, h0:h0 + nh], zc, vc(gcol), vc(bcol), ALU.mult, ALU.add),
                                     reads=[RXR[ti][c], Rvec], writes=[RXB[ti]])
                            else:
                                K.op("act", act(xbf[:, c, h0:h0 + nh], zc, AF.Identity, bias=vc(bcol), scale=vc(gcol)),
                                     reads=[RXR[ti][c], Rvec], writes=[RXB[ti]])
                            yield
                            K.op("act", act(zc, zc, AF.Identity, bias=vecsA[:, bcol:bcol + 1], scale=vecsA[:, gcol:gcol + 1]),
                                 reads=[RXR[ti][c], RvecA], writes=[RXR[ti][c]])
                            yield
                        else:
                            K.op("act", act(zc, zc, AF.Identity, bias=vc(bcol), scale=vc(gcol)),
                                 reads=[RXR[ti][c], Rvec], writes=[RXR[ti][c]])
                            yield
                yield from ln_gen(zap, nh, rz, False, epi, one_bank, zt, spacer)
            if final:
                out_tile(ti)

        def res_ln(ti, lnidx, final):
            run(res_ln_gen(ti, lnidx, final))

        def out_tile(ti):
            t0, n = TILES[ti]
            K.dma("sp", "oy", lambda e: e.dma_start(out=yTv[:, :, t0:t0 + n], in_=xres[:, :, t0:t0 + n]), reads=RXR[ti])

        def ffn_phase(l, s, lnidx, final, nxt):
            K.fence(lambda e: e.memset(fz[:, 0:1], 0.0))
            unit = 0
            for g in range(8):
                par = g % 2
                ag, au, dw = A[2 * par], A[2 * par + 1], Dm[par]
                rag, rau, rdw = RA[2 * par], RA[2 * par + 1], RD[par]
                for oi, ti in enumerate((4, 0, 1, 2, 3)):
                    t0, n = TILES[ti]
                    up = unit % 2
                    unit += 1
                    hb = H[up]
                    RH = [R(f"h{up}_{j}") for j in range(4)]
                    for j in range(4):
                        gi, ui = (j % 2) * 2, (j % 2) * 2 + 1
                        mm(PS[gi][:, :n], [(ag[:, k, j * 128:(j + 1) * 128], xbf[:, k, t0:t0 + n]) for k in range(8)],
                           reads=[rag, RXB[ti]], bank=RPS[gi])
                        mm(PS[ui][:, :n], [(au[:, k, j * 128:(j + 1) * 128], xbf[:, k, t0:t0 + n]) for k in range(8)],
                           reads=[rau, RXB[ti]], bank=RPS[ui])
                        sgb = SG[j % 2]
                        rsg = R(f"sg{j % 2}")
                        K.op("act", act(sgb[:, :n], PS[gi][:, :n], AF.Silu), reads=[RPS[gi]], writes=[rsg], arena=True)
                        K.op("dve", tt(hb[:, j, :n], sgb[:, :n], PS[ui][:, :n], ALU.mult), reads=[rsg, RPS[ui]], writes=[RH[j]], arena=True)
                        bg_step(3)

                    def down(g=g, ti=ti, t0=t0, n=n, hb=hb, RH=RH, dw=dw, rdw=rdw):
                        for d in range(8):
                            bi = 4 + d % 3
                            mm(PS[bi][:, :n], [(dw[:, j, d * 128:(d + 1) * 128], hb[:, j, :n]) for j in range(4)],
                               reads=[rdw] + RH, bank=RPS[bi], arena_=True)
                            xr = xres[:, d, t0:t0 + n]
                            K.op("dve", stt(xr, PS[bi][:, :n], 0.5, xr, ALU.mult, ALU.add), reads=[RPS[bi], RXR[ti][d]], writes=[RXR[ti][d]])
                            bg_step(3)
                        if g == 7:
                            bg.append(res_ln_gen(ti, lnidx, final, True))
                    K.tick()
                    if oi == 0 and g >= 1:
                        if g + 1 < 8:
                            ffn_loads(l, s, g + 1)()
                        else:
                            nxt[0]()
                            nxt[1]()
                            nxt[2]()
                    K.defer(1, down)
            nxt[3]()
            nxt[4]()
            K.flush()
            nxt[5]()

        def hgrn_phase(nxt):
            K.fence(lambda e: e.memset(fz[:, 0:1], 0.0))
            for x in range(2):
                K.op("dve", lambda e, x=x: e.memset(HS[x]["scm"][:], 0.0), writes=[R(f"scm{x}")], arena=True)
            Rhl = R("e3")
            K.op("dve", ts(e3[:, 0:8], lbv[:, 8:16], 0.5, None, ALU.mult), reads=[Rlb, R("e3"), R("e3b")], writes=[Rhl])
            K.op("dve", ts(e3[:, 8:16], lbv[:, 8:16], -0.5, None, ALU.mult), reads=[Rlb], writes=[Rhl])
            K.op("dve", ts(e3[:, 16:24], lbv[:, 0:8], 0.5, 0.5, ALU.mult, ALU.add), reads=[Rlb], writes=[Rhl])
            K.op("dve", ts(e3[:, 24:25], vc(NGO), 128.0 ** -0.5, None, ALU.mult), reads=[Rvec, R("e3b")], writes=[R("e3b")])
            Rvt = R("v_tok")
            Robf = [R(f"obf{j}") for j in range(4)]
            RSP, RSS = R("S0"), [R("S1"), R("S2")]
            zbf = zb[:].rearrange("p c t -> p (c t)").bitcast(F32)
            zsqf = zsq[:].rearrange("p c t -> p (c t)").bitcast(F32)

            def head(hg, ti, j, x):
                t0, n = TILES[ti]
                nch, cl = (4, 128) if n == 512 else (2, 16)
                mid = cl // 2
                prompt = n == 512
                h = hg * 4 + j
                cs = slice(j * 128, (j + 1) * 128)
                B = HS[x]
                if j < 2:
                    qs, fb, Rq, Rf = B["qs"], B["fb"], R(f"qs{x}"), R(f"fb{x}")
                else:
                    zz, rz_ = (zbf, Rzb) if x == 0 else (zsqf, Rzsq)
                    qs, fb, Rq, Rf = zz[:, 0:512], zz[:, 512:1024], rz_, rz_
                km, bb, Shist = B["km"], B["bb"], B["Shist"]
                q_rel, k_rel, k_relT, dd = B["q_rel"], B["k_rel"], B["k_relT"], B["dd"]
                on_, hv, osq, gs, Sp4 = qs, qs, k_relT, fb, k_rel
                Rkm, Rbb, Rsh = R(f"km{x}"), R(f"bb{x}"), R(f"sh{x}")
                Rqr, Rkr, RkT, Rdd = R(f"q_rel{x}"), R(f"k_rel{x}"), R(f"k_relT{x}"), R(f"dd{x}")
                RS = [RSP] * 4 if prompt else RSS
                Sst = [S_p] * 4 if prompt else S_s
                msk, rmsk = (tri, Rtri) if prompt else (tri16, R("tri16"))
                scb, rscb = (B["scm"], R(f"scm{x}")) if prompt else (scm16[:, 32 * x:32 * x + 32], R(f"scm16_{x}"))
                bq, bf_, bo = x, 2 + x, 4 + x
                WQ, WF, WV, WG = A
                xin = lambda k: xbf[:, k, t0:t0 + n]
                mm(PS[bq][:, :n], [(WQ[:, k, cs], xin(k)) for k in range(8)], reads=[RXB[ti], RA[0]], bank=RPS[bq])
                mm(PS[bf_][:, :n], [(WF[:, k, cs], xin(k)) for k in range(8)], reads=[RXB[ti], RA[1]], bank=RPS[bf_])
                K.op("act", act(qs[:, :n], PS[bq][:, :n], AF.Silu), reads=[RPS[bq]], writes=[Rq], arena=True)
                K.op("act", act(fb[:, :n], PS[bf_][:, :n], AF.Tanh, scale=0.5), reads=[RPS[bf_]], writes=[Rf], arena=True)
                yield
                K.op("dve", ts(km[:, :n], fb[:, :n], e3[:, 8 + h:9 + h], e3[:, h:h + 1], ALU.mult, ALU.add), reads=[Rf, Rhl], writes=[Rkm], arena=True)
                yield
                K.op("act", act(fb[:, :n], fb[:, :n], AF.Ln, bias=e3[:, 16 + h:17 + h], scale=e3[:, h:h + 1]), reads=[Rf, Rhl], writes=[Rf], arena=True)
                yield
                if prompt:
                    K.op("dve", lambda e: e.tensor_tensor_scan(bb[:, :n], rmask[:, :n], fb[:, :n], 0.0, ALU.mult, ALU.add),
                         reads=[Rf, R("rmask")], writes=[Rbb], arena=True)
                    yield
                else:
                    for c in range(nch):
                        sl = slice(c * cl, (c + 1) * cl)
                        K.op("dve", lambda e, sl=sl: e.tensor_tensor_scan(bb[:, sl], cst[:, 2:3].to_broadcast([128, cl]), fb[:, sl], 0.0,
                                                                          ALU.mult, ALU.add),
                             reads=[Rf, Rcst], writes=[Rbb], arena=True)
                        yield
                b3 = bb[:, :n].rearrange("p (c t) -> p c t", t=cl)
                f3 = fb[:, :n].rearrange("p (c t) -> p c t", t=cl)
                dd3 = dd[:, 0:2 * nch].rearrange("p (c t) -> p c t", t=2)
                K.op("act", act(dd3, b3[:, :, mid::cl - 1 - mid], AF.Exp), reads=[Rbb], writes=[Rdd], arena=True)
                yield
                K.op("dve", tt(f3, b3, b3[:, :, mid:mid + 1].to_broadcast([128, nch, cl]), ALU.subtract), reads=[Rbb], writes=[Rf], arena=True)
                yield
                K.op("act", act(bb[:, :n], fb[:, :n], AF.Exp), reads=[Rf], writes=[Rbb], arena=True)
                yield
                K.op("act", act(fb[:, :n], fb[:, :n], AF.Exp, scale=-1.0), reads=[Rf], writes=[Rf], arena=True)
                yield
                K.op("dve", tt(q_rel[:, :n], qs[:, :n], bb[:, :n], ALU.mult), reads=[Rq, Rbb], writes=[Rqr], arena=True)
                yield
                K.op("dve", tt(k_rel[:, :n], km[:, :n], fb[:, :n], ALU.mult), reads=[Rkm, Rf], writes=[Rkr], arena=True)
                yield

                def trfn(e):
                    ins = None
                    for c in range(nch):
                        ins = e.transpose(PST[:cl, c * 128:(c + 1) * 128], k_rel[:, c * cl:(c + 1) * cl], ident[:])
                    return ins
                K.op("pe", trfn, reads=[Rkr, Rident], writes=[RPST], arena=True)
                K.op("act", act(k_relT[:cl, :nch * 128], PST[:cl, :nch * 128], AF.Copy), reads=[RPST], writes=[RkT], arena=True)
                yield

                def scfn(e):
                    ins = None
                    for c in range(nch):
                        sl = slice(c * cl, (c + 1) * cl)
                        if cl == 128:
                            ins = e.matmul(PS[bf_][0:64, sl], k_rel[:, c * cl:c * cl + 64], q_rel[:, sl], start=True, stop=True)
                            ins = e.matmul(PS[bf_][64:128, c * cl + 64:(c + 1) * cl], k_rel[:, c * cl + 64:(c + 1) * cl],
                                           q_rel[:, c * cl + 64:(c + 1) * cl], start=True, stop=True)
                        else:
                            ins = e.matmul(PS[bf_][:cl, sl], k_rel[:, sl], q_rel[:, sl], start=True, stop=True)
                    return ins
                K.op("pe", scfn, reads=[Rkr, Rqr], writes=[RPS[bf_]], arena=True)
                yield
                K.op("dve", lambda e: e.copy_predicated(scb[:cl, :nch * cl], msk[:cl, :nch * cl], PS[bf_][:cl, :nch * cl]),
                     reads=[RPS[bf_], rmsk], writes=[rscb], arena=True)
                yield

                def ufn(e):
                    ins = None
                    for c in range(nch):
                        ins = e.matmul(PS[bq][:, c * 128:(c + 1) * 128], k_relT[:cl, c * 128:(c + 1) * 128], v_tok[:cl, c, cs], start=True, stop=True)
                    return ins
                K.op("pe", ufn, reads=[RkT, Rvt], writes=[RPS[bq]], arena=True)
                yield
                mm(PS[bo][:, :n], [(WG[:, k, cs], xin(k)) for k in range(8)], reads=[RXB[ti], RA[3]], bank=RPS[bo])
                yield
                K.op("act", act(gs[:, :n], PS[bo][:, :n], AF.Silu), reads=[RPS[bo], Rf], writes=[Rf], arena=True)
                yield
                yield
                if prompt:
                    K.op("dve", lambda e: e.tensor_copy(Shist[:, 0, :], S_p[:, j, :]), reads=[RSP], writes=[Rsh], arena=True)
                    yield
                tmpU4 = km[:, :nch * 128].rearrange("p (c v) -> p c v", v=128)
                K.op("dve", tt(tmpU4, PS[bq][:, :nch * 128].rearrange("p (c v) -> p c v", v=128),
                               b3[:, :, cl - 1:cl].to_broadcast([128, nch, 128]), ALU.mult),
                     reads=[RPS[bq], Rbb, Rkm], writes=[Rkm], arena=True)
                yield "mU"
                for c in range(nch):
                    d1c = dd[:, 2 * c + 1:2 * c + 2]
                    if prompt:
                        src = Shist[:, c, :]
                        tgt = Shist[:, c + 1, :] if c < nch - 1 else S_p[:, j, :]
                        rsrc, rtgt = [Rsh], ([Rsh] if c < nch - 1 else [RSP, Rsh])
                    else:
                        K.op("dve", lambda e, c=c: e.tensor_copy(Shist[:, c, :], S_s[c][:, j, :]), reads=[RSS[c]], writes=[Rsh], arena=True)
                        yield
                        src, tgt, rsrc, rtgt = Shist[:, c, :], S_s[c][:, j, :], [Rsh], [RSS[c]]
                    K.op("dve", stt(tgt, src, d1c, tmpU4[:, c, :], ALU.mult, ALU.add), reads=rsrc + [Rdd, Rkm], writes=rtgt, arena=True)
                    yield
                K.op("dve", tt(Sp4[:, :nch * 128].rearrange("p (c v) -> p c v", v=128), Shist[:, 0:nch, :],
                               dd3[:, :, 0:1].to_broadcast([128, nch, 128]), ALU.mult), reads=[Rsh, Rdd, Rkr], writes=[Rkr], arena=True)
                yield
                yield
                for c in range(nch):
                    sl = slice(c * cl, (c + 1) * cl)
                    mm(PS[bo][:, sl], [(v_tok[:cl, c, cs], scb[:cl, sl]), (Sp4[:, c * 128:(c + 1) * 128], q_rel[:, sl])],
                       reads=[Rvt, rscb, Rkr, Rqr], bank=RPS[bo], arena_=True)
                    yield
                yield "m3b"
                K.op("act", act(osq[:, :n], PS[bo][:, :n], AF.Square), reads=[RPS[bo], RkT], writes=[RkT], arena=True)
                yield
                mm(PS[bf_][:, :n], [(ones[:], osq[:, :n])], reads=[RkT, Rones], bank=RPS[bf_], arena_=True)
                yield
                K.op("act", act(hv[:, :n], PS[bf_][:, :n], AF.Ln, bias=cst[:, 0:1], scale=1.0 / (128.0 * 128.0)), reads=[RPS[bf_], Rcst, Rq], writes=[Rq], arena=True)
                yield
                K.op("act", act(hv[:, :n], hv[:, :n], AF.Exp, scale=-0.5), reads=[Rq], writes=[Rq], arena=True)
                yield
                K.op("dve", tt(on_[:, :n], PS[bo][:, :n], hv[:, :n], ALU.mult), reads=[RPS[bo], Rq], writes=[Rq], arena=True)
                yield
                K.op("dve", stt(obf[:, j, :n], on_[:, :n], e3[:, 24:25], gs[:, :n], ALU.mult, ALU.mult), reads=[Rq, Rf, R("e3b")], writes=[Robf[j]], arena=True)
                yield

            def drive(gens):
                gens = list(gens)
                while gens:
                    for g_ in list(gens):
                        try:
                            next(g_)
                        except StopIteration:
                            gens.remove(g_)
                    bg_step()

            def tail(hg, ti):
                t0, n = TILES[ti]
                WO, RWO = Dm[hg], RD[hg]
                for d in range(8):
                    bi = 6 if d % 2 == 0 else 4
                    mm(PS[bi][:, :n], [(WO[:, j, d * 128:(d + 1) * 128], obf[:, j, :n]) for j in range(4)],
                       reads=[RWO] + Robf, bank=RPS[bi], arena_=True)
                    xr = xres[:, d, t0:t0 + n]
                    K.op("dve", tt(xr, PS[bi][:, :n], xr, ALU.add), reads=[RPS[bi], RXR[ti][d]], writes=[RXR[ti][d]])
                yield
                if hg == 1:
                    d0v = Dm[0][:, 0:2, :].rearrange("p j (c t) -> p (j c) t", t=256)
                    d1v = Dm[0][:, 2:4, :].rearrange("p j (c t) -> p (j c) t", t=256)
                    yield from res_ln_gen(ti, 1, False, True, "act", (d0v, d1v, RD[0], RD[0]))

            for hg in range(2):
                WV = A[2]
                WO, RWO = Dm[hg], RD[hg]
                K.op("dve", lambda e: e.memset(S_p[:], 0.0), writes=[RSP], arena=True)
                for s_ in range(2):
                    K.dma("sp", f"is{hg}{s_}", lambda e, s_=s_, hg=hg: e.dma_start(out=S_s[s_][:], in_=st_in[s_, hg * 4:(hg + 1) * 4].rearrange("h k v -> k h v")),
                          writes=[RSS[s_]], arena=True)
                order = [4, 0, 1, 2, 3]
                pairs = [(oi, ti, pb) for oi, ti in enumerate(order) for pb in (0, 1)]
                gen_of = {}

                def get_gens(ti):
                    if ti not in gen_of:
                        gen_of[ti] = [head(hg, ti, j, j % 2) for j in range(4)]
                    return gen_of[ti]

                def emit_s1(ti, pb):
                    g4 = get_gens(ti)
                    next(g4[2 * pb])
                    next(g4[2 * pb + 1])

                emit_s1(order[0], 0)
                first_b_pending = True
                prev = []
                prev_tile_end = None
                for (oi, ti, pb) in pairs:
                    t0, n = TILES[ti]
                    lastt = oi == len(order) - 1
                    nch, cl = (4, 128) if n == 512 else (2, 16)
                    cur = get_gens(ti)[2 * pb:2 * pb + 2]
                    if pb == 0:
                        for c in range(nch):
                            mm(PS[6][:cl, :], [(xbf[:, k, t0 + c * cl:t0 + (c + 1) * cl], WV[:, k, :]) for k in range(8)],
                               reads=[RXB[ti], RA[2]], bank=RPS[6])
                            K.op("act", act(v_tok[:cl, c, :], PS[6][:cl, :], AF.Copy), reads=[RPS[6]], writes=[Rvt], arena=True)
                        if lastt:
                            if hg == 0:
                                loadA(2, w_in, 2048 + 512)
                            else:
                                nxt[3]()
                    reached = set()
                    reachedU = set()
                    pend_s1 = []

                    def step(g_, ti=ti, oi=oi, pb=pb, lastt=lastt):
                        nonlocal prev_tile_end
                        try:
                            v_ = next(g_)
                            if v_ == "m3b":
                                reached.add(id(g_))
                            elif v_ == "mU":
                                reachedU.add(id(g_))
                            return True
                        except StopIteration:
                            if g_ in prev:
                                prev.remove(g_)
                                if not prev and prev_tile_end is not None:
                                    tg_ = tail(hg, prev_tile_end)
                                    next(tg_)
                                    bg.append(tg_)
                                    pend_s1.append(lambda: emit_s1(ti, 1))
                                    prev_tile_end = None
                                elif not prev and pb == 1:
                                    if not lastt:
                                        pend_s1.append(lambda: emit_s1(order[oi + 1], 0))
                                    else:
                                        if hg == 0:
                                            loadA(0, w_in, 512)
                                            loadA(1, w_in, 1024 + 512)
                                        else:
                                            nxt[0]()
                                            nxt[1]()
                            return False

                    while len(reached) < 2:
                        for g_ in list(prev):
                            step(g_)
                        for g_ in cur:
                            if id(g_) not in reached:
                                step(g_)
                        if pend_s1 and len(reachedU) == 2:
                            pend_s1.pop(0)()
                        bg_step()
                    while prev:
                        for g_ in list(prev):
                            step(g_)
                        bg_step()
                    while pend_s1:
                        pend_s1.pop(0)()
                    if first_b_pending:
                        bg_drain()
                        emit_s1(order[0], 1)
                        first_b_pending = False
                    prev = list(cur)
                    prev_tile_end = ti if pb == 1 else None
                while prev:
                    for g_ in list(prev):
                        try:
                            next(g_)
                        except StopIteration:
                            prev.remove(g_)
                    bg_step()
                if hg == 0:
                    loadA(3, w_in, 3072 + 512)
                else:
                    nxt[4]()
                bg.append(tail(hg, order[-1]))
                while bg:
                    bg_step()
                if hg == 1:
                    nxt[2]()
                    nxt[5]()
                K.dma("sp", f"os{hg}", lambda e, hg=hg: e.dma_start(out=so_p[hg * 4:(hg + 1) * 4].rearrange("h k v -> k h v"), in_=S_p[:]), reads=[RSP], arena=True)
                for s_ in range(2):
                    K.dma("sp", f"os{hg}{s_}", lambda e, s_=s_, hg=hg: e.dma_start(out=so_s[s_, hg * 4:(hg + 1) * 4].rearrange("h k v -> k h v"), in_=S_s[s_][:]),
                          reads=[RSS[s_]], arena=True)

        hgrn_first = lambda: (loadA(0, w_in, 0), loadA(1, w_in, 1024), loadD(0, w_out, 0))
        hgrn_second = lambda: (loadA(2, w_in, 2048), loadA(3, w_in, 3072), loadD(1, w_out, 512))

        def conv_phase(nxt):
            K.fence(lambda e: e.memset(fz[:, 0:1], 0.0))
            RU = [R(f"U{c}") for c in range(8)]
            RUs = [R(f"Us{c}") for c in range(8)]
            RY = [R(f"y{c}") for c in range(8)]
            Rsgt = [R("sgt0"), R("sgt1")]
            K.op("dve", lambda e: e.memset(U[:, :, 0:30], 0.0), writes=RU, arena=True)
            for s_ in range(2):
                K.dma("sp", f"ic{s_}", lambda e, s_=s_: e.dma_start(out=U_s[:, :, s_, 0:30], in_=ccT[s_].rearrange("(c p) w -> p c w", p=128)),
                      writes=RUs, arena=True)
            def front(ti, cp):
                t0, n = TILES[ti]
                prompt = n == 512
                info = []
                for x in range(2):
                    c = 2 * cp + x
                    ai, gi = 2 * x, 2 * x + 1
                    cs = slice((c % 4) * 128, (c % 4 + 1) * 128)
                    mm(PS[ai][:, :n], [(A[c // 4][:, k, cs], xbf[:, k, t0:t0 + n]) for k in range(8)], reads=[RXB[ti], RA[c // 4]], bank=RPS[ai])
                    mm(PS[gi][:, :n], [(A[2 + c // 4][:, k, cs], xbf[:, k, t0:t0 + n]) for k in range(8)], reads=[RXB[ti], RA[2 + c // 4]], bank=RPS[gi])
                    if prompt:
                        ru = RU[c]
                        unew = U[:, c, 30:542]
                        pa, pg_ = PS[ai][:, :n], PS[gi][:, :n]
                        tap = (lambda c: (lambda w: U[:, c, w:w + 512]))(c)
                        yc = ybuf[:, c, :]
                    else:
                        ru = RUs[c]
                        unew = U_s[:, c, :, 30:46]
                        pa = PS[ai][:, :n].rearrange("p (s t) -> p s t", t=16)
                        pg_ = PS[gi][:, :n].rearrange("p (s t) -> p s t", t=16)
                        tap = (lambda c: (lambda w: U_s[:, c, :, w:w + 16]))(c)
                        yc = ybuf[:, c, :n].rearrange("p (s t) -> p s t", t=16)
                    K.op("act", act(unew, pg_, AF.Sigmoid, bias=vc(BGO + c)), reads=[RPS[gi], Rvec], writes=[ru], arena=True)
                    info.append(dict(c=c, x=x, ru=ru, tap=tap, yc=yc, unew=unew, pa=pa, ai=ai))
                return info

            def uop(info):
                for d_ in info:
                    c, x = d_["c"], d_["x"]
                    K.op("dve", stt(d_["unew"], d_["pa"], vc(BAO + c), d_["unew"], ALU.add, ALU.mult),
                         reads=[RPS[d_["ai"]], d_["ru"], Rvec], writes=[d_["ru"]], arena=True)

            def taps(ti, cp, info):
                t0, n = TILES[ti]
                prompt = n == 512
                npe = CW - NDVE_TAPS
                for i in range(max(NDVE_TAPS, npe)):
                    for d_ in info:
                        c, x, ru, tap, yc = d_["c"], d_["x"], d_["ru"], d_["tap"], d_["yc"]
                        if i < NDVE_TAPS:
                            w = i
                            wc = vc(WDWO + c * CW + w)
                            if w == 0:
                                K.op("dve", ts(yc, tap(0), wc, vc(BDWO + c), ALU.mult, ALU.add), reads=[ru, Rvec], writes=[RY[c]], arena=True)
                            else:
                                K.op("dve", stt(yc, tap(w), wc, yc, ALU.mult, ALU.add), reads=[ru, Rvec, RY[c]], writes=[RY[c]], arena=True)
                        if i < npe:
                            w = NDVE_TAPS + i
                            wc = vc(WDWO + c * CW + w)
                            pb_ = prod4[2 * x + i % 2]
                            rpb = R(f"prod{2 * x + i % 2}")
                            pv_ = pb_[:, :n] if prompt else pb_[:, :n].rearrange("p (s t) -> p s t", t=16)
                            K.op("act", act(pv_, tap(w), AF.Identity, scale=wc), reads=[ru, Rvec], writes=[rpb], arena=True)
                            K.op("pe", lambda e, x=x, i=i, n=n, npe=npe, pb_=pb_: e.matmul(PS[4 + x][:, :n], identf[:], pb_[:, :n], start=(i == 0), stop=(i == npe - 1)),
                                 reads=[rpb, R("identf")], writes=[RPS[4 + x]], arena=True)
                    bg_step()
                for d_ in info:
                    c, x = d_["c"], d_["x"]
                    ycf = ybuf[:, c, :n]
                    K.op("dve", tt(ycf, ycf, PS[4 + x][:, :n], ALU.add), reads=[RY[c], RPS[4 + x]], writes=[RY[c]], arena=True)
                if prompt and ti < 3:
                    for d_ in info:
                        c = d_["c"]
                        K.op("act", act(U[:, c, 0:30], U[:, c, 512:542], AF.Copy), reads=[d_["ru"]], writes=[d_["ru"]], arena=True)

            def tail(ti):
                t0, n = TILES[ti]
                hv_ = halves(t0, n)
                for hi, (h0, nh) in enumerate(hv_):
                    o0 = h0 - t0
                    zap = ybuf[:, :, o0:o0 + nh]

                    def epi(o0=o0, nh=nh):
                        for c in range(8):
                            K.op("act", act(zb[:, c, :nh], ybuf[:, c, o0:o0 + nh], AF.Silu, bias=vc(CLBO + c), scale=vc(CLGO + c)),
                                 reads=[RY[c], Rvec], writes=[Rzb], arena=True)
                            yield
                    yield from ln_gen(zap, nh, RY, True, epi, True, None, 2)
                    if hi == len(hv_) - 1:
                        yield "ybuf_free"
                    pstf = PST[:].bitcast(F32)
                    pend = None
                    for d in range(9):
                        if d < 8:
                            pb, rpb = (PS[6][:, :nh], RPS[6]) if d % 2 == 0 else (pstf[:, :nh], RPST)
                            mm(pb, [(Dm[k // 4][:, k % 4, d * 128:(d + 1) * 128], zb[:, k, :nh]) for k in range(8)],
                               reads=[Rzb, RD[0], RD[1]], bank=rpb)
                            yield
                        if pend is not None:
                            d0, pb0, rpb0 = pend
                            xr = xres[:, d0, h0:h0 + nh]
                            K.op("dve", stt(xr, pb0, vc(B2O + d0), xr, ALU.add, ALU.add), reads=[rpb0, RXR[ti][d0], Rvec], writes=[RXR[ti][d0]])
                            yield
                        if d < 8:
                            pend = (d, pb, rpb)
                    if hi == len(hv_) - 1:
                        yield "pw2_done"
                yield from res_ln_gen(ti, 4, False, True, "act", None, 2 if ti != 3 else 0)

            seq = [(ti, cp) for ti in (4, 0, 1, 2, 3) for cp in range(4)]
            cur = front(*seq[0])
            uop(cur)
            for k, (ti, cp) in enumerate(seq):
                nxt_info = front(*seq[k + 1]) if k + 1 < len(seq) else None
                if k + 1 == len(seq) - 1:
                    nxt[0]()
                    nxt[1]()
                    nxt[3]()
                    nxt[4]()
                taps(ti, cp, cur)
                if nxt_info is not None:
                    uop(nxt_info)
                cur = nxt_info
                if cp == 3:
                    if ti == 3:
                        K.dma("sp", "oc", lambda e: e.dma_start(out=cco_p.rearrange("(c p) w -> p c w", p=128), in_=U[:, :, 512:542]), reads=RU, arena=True)
                    if TILES[ti][1] != 512:
                        for s_ in range(2):
                            K.dma("sp", "oc", lambda e, s_=s_: e.dma_start(out=cco_s[s_].rearrange("(c p) w -> p c w", p=128), in_=U_s[:, :, s_, 16:46]),
                                  reads=RUs, arena=True)
                    while bg:
                        bg_step()
                    g_ = tail(ti)
                    for v_ in g_:
                        if v_ == "ybuf_free":
                            break
                    bg.append(g_)
            while bg:
                g_ = bg[0]
                done_ = False
                for v_ in g_:
                    if v_ == "pw2_done" and len(bg) == 1:
                        nxt[2]()
                        nxt[5]()
                        done_ = True
                        break
                if done_:
                    break
                bg.pop(0)

        conv_first = lambda: (loadA(0, w_pw1, 0), loadA(1, w_pw1, 512), loadD(0, w_pw2, 0))
        conv_second = lambda: (loadA(2, w_pw1, 1024), loadA(3, w_pw1, 1536), loadD(1, w_pw2, 512))

        ffn_first = lambda l, s: ffn_loads(l, s, 0)
        ffn_second = lambda l, s: ffn_loads(l, s, 1)
        phases = [
            ("ffn", 0, 0, 0), ("hgrn",), ("ffn", 0, 1, 2), ("ffn", 1, 0, 3), ("conv",), ("ffn", 1, 1, 5),
        ][:nsub]

        def first_second(p):
            if p is None:
                return [nop] * 6
            if p[0] == "ffn":
                l, s_ = p[1], p[2]
                return [lambda: loadA(0, w_gate[l, s_], 0), lambda: loadA(1, w_up[l, s_], 0), lambda: loadD(0, w_down[l, s_], 0),
                        lambda: loadA(2, w_gate[l, s_], 512), lambda: loadA(3, w_up[l, s_], 512), lambda: loadD(1, w_down[l, s_], 512)]
            if p[0] == "hgrn":
                return [lambda: loadA(0, w_in, 0), lambda: loadA(1, w_in, 1024), lambda: loadD(0, w_out, 0),
                        lambda: loadA(2, w_in, 2048), lambda: loadA(3, w_in, 3072), lambda: loadD(1, w_out, 512)]
            return [lambda: loadA(0, w_pw1, 0), lambda: loadA(1, w_pw1, 512), lambda: loadD(0, w_pw2, 0),
                    lambda: loadA(2, w_pw1, 1024), lambda: loadA(3, w_pw1, 1536), lambda: loadD(1, w_pw2, 512)]

        for i, p in enumerate(phases):
            nxt = first_second(phases[i + 1] if i + 1 < len(phases) else None)
            last = i == len(phases) - 1
            if p[0] == "ffn":
                ffn_phase(p[1], p[2], p[3], last and nsub == 6, nxt)
            elif p[0] == "hgrn":
                hgrn_phase(nxt)
            else:
                conv_phase(nxt)
        bg_drain()
        if nsub < 6:
            for ti in range(5):
                out_tile(ti)
        spq = K.q["sp"]
        for k in list(K.dcnt):
            if k.startswith("o"):
                spq.items.append(("w", k, K.dcnt[k]))

        sems = {k: es.enter_context(nc.semaphore(f"s_{k}")) for k in K.semkeys}

        def replay(q, e):
            own = sems[q.name]
            for it in q.items:
                if it[0] == "w":
                    e.wait_ge(sems[it[1]], it[2])
                elif it[0] == "o":
                    it[1](e).then_inc(own, 1)
                else:
                    it[1](e).then_inc(sems[it[2]], 16)

        with nc.Block() as block:
            @block.sync
            def _(e):
                replay(K.q["sp"], e)

            @block.scalar
            def _(e):
                replay(K.q["act"], e)

            @block.vector
            def _(e):
                replay(K.q["dve"], e)

            @block.gpsimd
            def _(e):
                replay(K.q["pool"], e)

            @block.tensor
            def _(e):
                replay(K.q["pe"], e)
    return nc


def _vecs(ln_g, ln_b, hgrn_lb, hgrn_norm_g, conv_b_pw1, conv_w_dw, conv_b_dw, conv_ln_g, conv_ln_b, conv_b_pw2):
    cm = lambda v: np.asarray(v, np.float32).reshape(-1, 8, 128).transpose(2, 0, 1).reshape(128, -1)
    parts = [
        cm(np.asarray(ln_g).reshape(6, D)), cm(np.asarray(ln_b).reshape(6, D)), cm(np.asarray(hgrn_lb)),
        np.asarray(hgrn_norm_g, np.float32)[0].reshape(128, 1),
        cm(np.asarray(conv_b_pw1)[0][:D]), cm(np.asarray(conv_b_pw1)[0][D:]),
        np.asarray(conv_w_dw, np.float32)[0].reshape(CW, 8, 128).transpose(2, 1, 0).reshape(128, 8 * CW),
        cm(np.asarray(conv_b_dw)[0]), cm(np.asarray(conv_ln_g)[0]), cm(np.asarray(conv_ln_b)[0]), cm(np.asarray(conv_b_pw2)[0]),
    ]
    v = np.ascontiguousarray(np.concatenate(parts, axis=1).astype(np.float32))
    assert v.shape == (128, NV), v.shape
    return v


def make_in_maps(x_prompt, x_sample, state_hgrn, cache_conv, ffn_w_gate, ffn_w_up, ffn_w_down, ln_g, ln_b,
                 hgrn_w_in, hgrn_lb, hgrn_norm_g, hgrn_w_out, conv_w_pw1, conv_b_pw1, conv_w_dw, conv_b_dw,
                 conv_ln_g, conv_ln_b, conv_w_pw2, conv_b_pw2):
    f = lambda a: np.ascontiguousarray(np.asarray(a, dtype=np.float32))
    x_prompt, x_sample, state_hgrn, cache_conv = f(x_prompt), f(x_sample), f(state_hgrn), f(cache_conv)
    vecs = _vecs(ln_g, ln_b, hgrn_lb, hgrn_norm_g, conv_b_pw1, conv_w_dw, conv_b_dw, conv_ln_g, conv_ln_b, conv_b_pw2)
    s_idx = np.arange(128)[:, None]
    tri = (np.tile(np.arange(128), 4)[None, :] >= s_idx).astype(np.uint16)
    tri16 = (np.tile(np.arange(16), 2)[None, :] >= s_idx).astype(np.uint16)
    rmask = np.ascontiguousarray(np.broadcast_to((np.arange(512) % 128 != 0).astype(np.float32)[None, :], (128, 512)))
    shared = {
        "wg": f(ffn_w_gate), "wu": f(ffn_w_up), "wd": f(ffn_w_down), "win": f(hgrn_w_in)[0], "wout": f(hgrn_w_out)[0],
        "pw1": f(conv_w_pw1)[0], "pw2": f(conv_w_pw2)[0], "vecs": vecs, "tri": np.ascontiguousarray(tri),
        "tri16": np.ascontiguousarray(tri16), "rmask": rmask, "ident": np.eye(128, dtype=np.float32),
    }
    maps = []
    for i in range(NCORES):
        xa = np.concatenate([x_prompt[i], x_sample[2 * i], x_sample[2 * i + 1]], axis=0)
        m = dict(shared)
        m["xT"] = np.ascontiguousarray(xa.T)
        m["st"] = np.ascontiguousarray(state_hgrn[0, 2 * i:2 * i + 2])
        m["ccT"] = np.ascontiguousarray(cache_conv[0, 2 * i:2 * i + 2].transpose(0, 2, 1))
        maps.append(m)
    return maps


def gather(results):
    y_prompt = np.empty((8, TP, D), np.float32)
    y_sample = np.empty((16, TS, D), np.float32)
    hs_p = np.empty((1, 8, 8, 128, 128), np.float32)
    hs_s = np.empty((1, 16, 8, 128, 128), np.float32)
    cc_p = np.empty((1, 8, 30, D), np.float32)
    cc_s = np.empty((1, 16, 30, D), np.float32)
    for i, r in enumerate(results):
        y = np.asarray(r["yT"]).T
        y_prompt[i] = y[:TP]
        y_sample[2 * i] = y[TP:TP + TS]
        y_sample[2 * i + 1] = y[TP + TS:]
        hs_p[0, i] = np.asarray(r["so_p"])
        hs_s[0, 2 * i:2 * i + 2] = np.asarray(r["so_s"])
        cc_p[0, i] = np.asarray(r["cco_p"]).T
        cc_s[0, 2 * i:2 * i + 2] = np.asarray(r["cco_s"]).transpose(0, 2, 1)
    return (y_prompt, y_sample, hs_p, hs_s, cc_p, cc_s)


_NC_CACHE = {}


def kernel(**inputs):
    maps = make_in_maps(**inputs)
    if 6 not in _NC_CACHE:
        _NC_CACHE[6] = build_nc(6)
    res = run_bass_kernel_spmd(_NC_CACHE[6], maps, core_ids=list(range(NCORES)))
    return gather(res.results)
```

```python
import numpy as np
from contextlib import ExitStack
import concourse.bass as bass
import concourse.mybir as mybir
from concourse.bass_utils import run_bass_kernel_spmd

F32 = mybir.dt.float32
BF16 = mybir.dt.bfloat16
U32 = mybir.dt.uint32
U16 = mybir.dt.uint16
AF = mybir.ActivationFunctionType
ALU = mybir.AluOpType

NCORES = 8
D = 1024
TP = 2048
TS = 16
T = TP + 2 * TS
ALPHA = 2.0 ** 0.5
LN_EPS = 1e-5
TILES = [(0, 512), (512, 512), (1024, 512), (1536, 512), (2048, 32)]
CW = 31

LNG, LNB, LBO, NGO, BAO, BGO, WDWO, BDWO, CLGO, CLBO, B2O, NV = 0, 48, 96, 120, 121, 129, 137, 385, 393, 401, 409, 417
NDVE_TAPS = 17


class Res:
    __slots__ = ("w", "r")

    def __init__(self):
        self.w = {}
        self.r = {}


class Q:
    def __init__(self, name):
        self.name = name
        self.cnt = 0
        self.items = []
        self.seen = {}


class Kb:
    def __init__(self):
        self.res = {}
        self.dcnt = {}
        self.deferred = []
        self.q = {n: Q(n) for n in ("sp", "act", "dve", "pool", "pe")}
        self.arena = Res()
        self.semkeys = ["sp", "act", "dve", "pool", "pe"]

    def R(self, key):
        r = self.res.get(key)
        if r is None:
            r = self.res[key] = Res()
        return r

    def _wait(self, q, reads, writes):
        need = {}
        for r in reads:
            for k, v in r.w.items():
                if need.get(k, 0) < v:
                    need[k] = v
        skip = q.name if q.name == "pe" else None
        for w in writes:
            for k, v in w.w.items():
                if k != skip and need.get(k, 0) < v:
                    need[k] = v
            for k, v in w.r.items():
                if k != skip and need.get(k, 0) < v:
                    need[k] = v
        for k, v in need.items():
            if q.seen.get(k, 0) < v:
                q.items.append(("w", k, v))
                q.seen[k] = v

    def op(self, qn, fn, reads=(), writes=(), arena=False):
        q = self.q[qn]
        reads = list(reads)
        if arena:
            reads.append(self.arena)
        self._wait(q, reads, writes)
        q.cnt += 1
        q.items.append(("o", fn))
        for r in reads:
            r.r[q.name] = q.cnt
        for w in writes:
            w.r = {}
            w.w = {q.name: q.cnt}

    def dma(self, qn, semkey, fn, reads=(), writes=(), arena=False):
        q = self.q[qn]
        reads = list(reads)
        if arena:
            reads.append(self.arena)
        if semkey not in self.dcnt:
            self.dcnt[semkey] = 0
            self.semkeys.append(semkey)
        self._wait(q, reads, writes)
        self.dcnt[semkey] += 16
        c = self.dcnt[semkey]
        q.items.append(("d", fn, semkey))
        for r in reads:
            r.r[semkey] = c
        for w in writes:
            w.r = {}
            w.w = {semkey: c}

    def fence(self, fn):
        self.op("dve", fn, reads=(), writes=[self.arena])

    def defer(self, n, fn):
        self.deferred.append([n, fn])

    def tick(self):
        run = []
        keep = []
        for it in self.deferred:
            it[0] -= 1
            (run if it[0] <= 0 else keep).append(it)
        self.deferred = keep
        for it in run:
            it[1]()

    def flush(self):
        while self.deferred:
            self.tick()


def act(out, in_, func, bias=None, scale=None):
    kw = {}
    if bias is not None:
        kw["bias"] = bias
    if scale is not None:
        kw["scale"] = scale
    return lambda e: e.activation(out, in_, func, **kw)


def tt(out, a, b, op):
    return lambda e: e.tensor_tensor(out, a, b, op)


def ts(out, a, s1, s2, op0, op1=None):
    if op1 is None:
        return lambda e: e.tensor_scalar(out, a, s1, None, op0)
    return lambda e: e.tensor_scalar(out, a, s1, s2, op0, op1)


def stt(out, a, s, b, op0, op1):
    return lambda e: e.scalar_tensor_tensor(out, a, s, b, op0, op1)


def build_nc(nsub=6):
    nc = bass.Bass("TRN2", target_bir_lowering=False)
    dt_in = lambda name, shape, dt=F32: nc.dram_tensor(name, shape, dt, kind="ExternalInput").ap()
    dt_out = lambda name, shape: nc.dram_tensor(name, shape, F32, kind="ExternalOutput").ap()
    xT = dt_in("xT", [D, T])
    st_in = dt_in("st", [2, 8, 128, 128])
    ccT = dt_in("ccT", [2, D, 30])
    w_gate = dt_in("wg", [2, 2, D, 4 * D])
    w_up = dt_in("wu", [2, 2, D, 4 * D])
    w_down = dt_in("wd", [2, 2, 4 * D, D])
    w_in = dt_in("win", [D, 4 * D])
    w_out = dt_in("wout", [D, D])
    w_pw1 = dt_in("pw1", [D, 2 * D])
    w_pw2 = dt_in("pw2", [D, D])
    vecs_d = dt_in("vecs", [128, NV])
    tri_d = dt_in("tri", [128, 512], U16)
    tri16_d = dt_in("tri16", [128, 32], U16)
    rmask_d = dt_in("rmask", [128, 512])
    ident_d = dt_in("ident", [128, 128])
    yT = dt_out("yT", [D, T])
    so_p = dt_out("so_p", [8, 128, 128])
    so_s = dt_out("so_s", [2, 8, 128, 128])
    cco_p = dt_out("cco_p", [D, 30])
    cco_s = dt_out("cco_s", [2, D, 30])

    K = Kb()
    R = K.R
    global _LASTK
    _LASTK = K
    es = ExitStack()
    with es:
        sb = lambda name, shape, dt: es.enter_context(nc.sbuf_tensor(name, shape, dt))
        xres = sb("xres", [128, 8, T], F32)
        xbf = sb("xbf", [128, 8, T], BF16)
        A = [sb(f"A{i}", [128, 8, 512], BF16) for i in range(4)]
        Dm = [sb(f"D{i}", [128, 4, 1024], BF16) for i in range(2)]
        vecs = sb("vecs_sb", [128, NV], F32)
        vecsA = sb("vecsA", [128, 96], F32)
        lbv = sb("lbv", [128, 16], F32)
        e3 = sb("e3", [128, 32], F32)
        cst = sb("cst", [128, 4], F32)
        tri = sb("tri_sb", [128, 512], U16)
        tri16 = sb("tri16_sb", [128, 32], U16)
        rmask = sb("rmask_sb", [128, 512], BF16)
        ones = sb("ones", [128, 128], BF16)
        ident = sb("ident_sb", [128, 128], BF16)
        scm16 = sb("scm16", [128, 64], BF16)
        identf = sb("identf", [128, 128], F32)
        fz = sb("fz", [128, 2], F32)
        zb = sb("zb", [128, 8, 256], BF16)
        zsq = sb("zsq", [128, 8, 256], BF16)
        mean = sb("mean", [128, 256], F32)
        var = sb("var", [128, 256], F32)
        sd = sb("sd", [128, 256], F32)
        rstd = sb("rstd", [128, 256], F32)
        ARENA_F = 11264
        arena = sb("arena", [128, ARENA_F], F32)
        PS = [es.enter_context(nc.psum_tensor(f"ps{i}", [128, 512], F32)) for i in range(7)]
        PST = es.enter_context(nc.psum_tensor("pst", [128, 1024], BF16))

        def af(off_b, nelem):
            assert off_b % 4 == 0 and off_b // 4 + nelem <= ARENA_F
            return arena[:, off_b // 4: off_b // 4 + nelem]

        def ab(off_b, nelem):
            assert off_b % 4 == 0 and nelem % 2 == 0 and off_b // 4 + nelem // 2 <= ARENA_F
            return arena[:, off_b // 4: off_b // 4 + nelem // 2].bitcast(BF16)

        SG = [af(0, 512), af(2048, 512)]
        H = [ab(4096, 2048).rearrange("p (j t) -> p j t", t=512), ab(8192, 2048).rearrange("p (j t) -> p j t", t=512)]
        HS = []
        for x in range(2):
            b0 = x * 14336
            HS.append(dict(qs=af(b0, 512), fb=af(b0 + 2048, 512), km=af(b0 + 4096, 512), bb=af(b0 + 6144, 512),
                           Shist=af(b0 + 8192, 512).rearrange("p (c v) -> p c v", v=128),
                           q_rel=ab(b0 + 10240, 512), k_rel=ab(b0 + 11264, 512), k_relT=ab(b0 + 12288, 512), scm=ab(b0 + 13312, 512),
                           dd=af(43008 + 64 * x, 16)))
        obf = ab(28672, 2048).rearrange("p (j t) -> p j t", t=512)
        v_tok = ab(32768, 2048).rearrange("p (c v) -> p c v", v=512)
        S_p = af(36864, 512).rearrange("p (h v) -> p h v", v=128)
        S_s = [af(38912, 512).rearrange("p (h v) -> p h v", v=128), af(40960, 512).rearrange("p (h v) -> p h v", v=128)]
        U = af(0, 8 * 542).rearrange("p (c t) -> p c t", t=542)
        ybuf = af(17408, 4096).rearrange("p (c t) -> p c t", t=512)
        U_s = af(33792, 8 * 2 * 46).rearrange("p (c s t) -> p c s t", s=2, t=46)
        prod4 = [af(36736, 512), af(38784, 512), af(40832, 512), af(42880, 512)]

        RA = [R(f"A{i}") for i in range(4)]
        RD = [R(f"D{i}") for i in range(2)]
        RPS = [R(f"ps{i}") for i in range(7)]
        RPST = R("pst")
        RXB = [R(f"xbf{t}") for t in range(5)]
        RXR = [[R(f"xr{t}_{c}") for c in range(8)] for t in range(5)]
        Rvec, RvecA, Rlb, Rcst, Rtri, Rones, Rident = R("vecs"), R("vecsA"), R("lbv"), R("cst"), R("tri"), R("ones"), R("ident")
        Rzb, Rzsq, Rmean, Rvar, Rsd, Rrstd = R("zb"), R("zsq"), R("mean"), R("var"), R("sd"), R("rstd")

        vc = lambda col: vecs[:, col:col + 1]

        def mm(out_ap, pairs, reads, bank, arena_=False):
            def fn(e, pairs=pairs, out_ap=out_ap):
                n = len(pairs)
                ins = None
                for i, (l, r) in enumerate(pairs):
                    ins = e.matmul(out_ap, l, r, start=(i == 0), stop=(i == n - 1))
                return ins
            K.op("pe", fn, reads=reads, writes=[bank], arena=arena_)

        def loadA(slot, w2d, col0):
            src = w2d[:, col0:col0 + 512].rearrange("(k p) c -> p k c", p=128)
            K.dma("pool", f"wA{slot}", lambda e: e.dma_start(out=A[slot][:], in_=src), writes=[RA[slot]])

        def loadD(slot, w2d, row0):
            src = w2d[row0:row0 + 512, :].rearrange("(j p) c -> p j c", p=128)
            K.dma("pool", f"wD{slot}", lambda e: e.dma_start(out=Dm[slot][:], in_=src), writes=[RD[slot]])

        def ffn_loads(l, s, g):
            par = g % 2
            return lambda: (loadA(2 * par, w_gate[l, s], g * 512), loadA(2 * par + 1, w_up[l, s], g * 512),
                            loadD(par, w_down[l, s], g * 512))

        nop = lambda: None

        xTv = xT.rearrange("(c p) t -> p c t", p=128)
        yTv = yT.rearrange("(c p) t -> p c t", p=128)
        K.dma("sp", "iv", lambda e: e.dma_start(out=vecs[:], in_=vecs_d), writes=[Rvec])
        K.dma("sp", "it", lambda e: e.dma_start(out=tri[:], in_=tri_d), writes=[Rtri])
        K.dma("sp", "it16", lambda e: e.dma_start(out=tri16[:], in_=tri16_d), writes=[R("tri16")])
        K.dma("pool", "ii", lambda e: e.dma_start(out=ident[:], in_=ident_d), writes=[Rident])
        K.dma("pool", "irm", lambda e: e.dma_start(out=rmask[:], in_=rmask_d), writes=[R("rmask")])
        K.dma("sp", "iif", lambda e: e.dma_start(out=identf[:], in_=ident_d), writes=[R("identf")])
        for oi, ti in enumerate((4, 0, 1, 2, 3)):
            t0, n = TILES[ti]
            K.dma("sp", f"ix{ti}", lambda e, t0=t0, n=n: e.dma_start(out=xres[:, :, t0:t0 + n], in_=xTv[:, :, t0:t0 + n]),
                  writes=RXR[ti])
            K.dma("pool", f"ib{ti}", lambda e, t0=t0, n=n: e.dma_start(out=xbf[:, :, t0:t0 + n], in_=xTv[:, :, t0:t0 + n]),
                  writes=[RXB[ti]])
            if oi == 0:
                ffn_loads(0, 0, 0)()
            if oi == 1:
                ffn_loads(0, 0, 1)()
        K.op("dve", lambda e: e.memset(ones[:], 1.0), writes=[Rones])
        K.op("dve", lambda e: e.memset(cst[:, 0:1], LN_EPS), writes=[Rcst])
        K.op("dve", lambda e: e.memset(cst[:, 1:2], float(np.log(128.0 ** -0.5))), writes=[Rcst])
        K.op("dve", lambda e: e.memset(cst[:, 2:3], 1.0), writes=[Rcst])
        K.op("dve", lambda e: e.memset(scm16[:], 0.0), writes=[R("scm16")])
        K.op("dve", ts(vecsA[:], vecs[:, 0:96], ALPHA, None, ALU.mult), reads=[Rvec], writes=[RvecA])
        K.op("act", act(e3[:, 0:24], vecs[:, LBO:LBO + 24], AF.Exp), reads=[Rvec], writes=[R("e3")])
        K.op("dve", tt(e3[:, 24:32], e3[:, 0:8], e3[:, 8:16], ALU.add), reads=[R("e3")], writes=[R("e3b")])
        K.op("dve", tt(e3[:, 24:32], e3[:, 24:32], e3[:, 16:24], ALU.add), reads=[R("e3"), R("e3b")], writes=[R("e3b")])
        K.op("dve", lambda e: e.reciprocal(e3[:, 24:32], e3[:, 24:32]), reads=[R("e3b")], writes=[R("e3b")])
        K.op("dve", tt(lbv[:, 0:8], e3[:, 0:8], e3[:, 24:32], ALU.mult), reads=[R("e3"), R("e3b")], writes=[Rlb])
        K.op("dve", ts(lbv[:, 8:16], lbv[:, 0:8], -1.0, 1.0, ALU.mult, ALU.add), reads=[Rlb], writes=[Rlb])
        for ti in (4, 0, 1, 2, 3):
            t0, n = TILES[ti]
            K.op("dve", ts(xres[:, :, t0:t0 + n], xres[:, :, t0:t0 + n], ALPHA, None, ALU.mult), reads=RXR[ti], writes=RXR[ti])

        def run(gen):
            for _ in gen:
                pass

        bg = []

        def bg_step(k=1):
            for _ in range(k):
                if not bg:
                    return
                try:
                    next(bg[0])
                except StopIteration:
                    bg.pop(0)

        def bg_drain():
            while bg:
                bg_step()

        def ln_gen(zap, n, rz, ar, epilogue, one_bank=False, zt=None, spacer=0):
            zb_, zsq_, Rzb_, Rzsq_ = zt if zt is not None else (zb, zsq, Rzb, Rzsq)
            if one_bank:
                bs_, bq_, rbs, rbq = PS[6][:, 0:n], PS[6][:, 256:256 + n], RPS[6], RPS[6]
            else:
                bs_, bq_, rbs, rbq = PS[5][:, :n], PS[6][:, :n], RPS[5], RPS[6]
            K.op("act", act(zsq_[:, :, :n], zap, AF.Square), reads=rz, writes=[Rzsq_], arena=ar)
            yield
            K.op("dve", lambda e: e.tensor_copy(zb_[:, :, :n], zap), reads=rz, writes=[Rzb_], arena=ar)
            yield
            mm(bs_, [(ones[:], zb_[:, c, :n]) for c in range(8)], reads=[Rzb_, Rones], bank=rbs)
            mm(bq_, [(ones[:], zsq_[:, c, :n]) for c in range(8)], reads=[Rzsq_, Rones], bank=rbq)
            loose = (not one_bank) or spacer > 0
            if loose:
                yield
                for _ in range(spacer):
                    yield
            K.op("dve", ts(mean[:, :n], bs_, 1.0 / D, None, ALU.mult), reads=[rbs], writes=[Rmean])
            if loose:
                yield
            K.op("dve", tt(var[:, :n], mean[:, :n], mean[:, :n], ALU.mult), reads=[Rmean], writes=[Rvar])
            if loose:
                yield
            K.op("dve", stt(var[:, :n], bq_, 1.0 / D, var[:, :n], ALU.mult, ALU.subtract), reads=[rbq, Rvar], writes=[Rvar])
            yield
            K.op("act", act(sd[:, :n], var[:, :n], AF.Ln, bias=cst[:, 0:1]), reads=[Rvar, Rcst], writes=[Rsd])
            K.op("act", act(rstd[:, :n], sd[:, :n], AF.Exp, scale=-0.5), reads=[Rsd], writes=[Rrstd])
            yield
            K.op("dve", tt(zap, zap, mean[:, :n].unsqueeze(1).to_broadcast([128, 8, n]), ALU.subtract),
                 reads=rz + [Rmean], writes=rz, arena=ar)
            yield
            for _ in range(spacer):
                yield
            K.op("dve", tt(zap, zap, rstd[:, :n].unsqueeze(1).to_broadcast([128, 8, n]), ALU.mult),
                 reads=rz + [Rrstd], writes=rz, arena=ar)
            yield
            yield from epilogue()

        def halves(t0, n):
            return [(t0, 256), (t0 + 256, 256)] if n == 512 else [(t0, n)]

        def res_ln_gen(ti, lnidx, final, one_bank=False, xbf_eng="dve", zt=None, spacer=0):
            t0, n = TILES[ti]
            rz = RXR[ti]
            for (h0, nh) in halves(t0, n):
                zap = xres[:, :, h0:h0 + nh]

                def epi(h0=h0, nh=nh):
                    for c in range(8):
                        gcol, bcol = LNG + lnidx * 8 + c, LNB + lnidx * 8 + c
                        zc = xres[:, c, h0:h0 + nh]
                        if not final:
                            if xbf_eng == "dve":
                                K.op("dve", ts(xbf[:, c, h0:h0 + nh], zc, vc(gcol), vc(bcol), ALU.mult, ALU.add),
                                     reads=[RXR[ti][c], Rvec], writes=[RXB[ti]])
                            else:
                                K.op("act", act(xbf[:, c, h0:h0 + nh], zc, AF.Identity, bias=vc(bcol), scale=vc(gcol)),
                                     reads=[RXR[ti][c], Rvec], writes=[RXB[ti]])
                            yield
                            K.op("act", act(zc, zc, AF.Identity, bias=vecsA[:, bcol:bcol + 1], scale=vecsA[:, gcol:gcol + 1]),
                                 reads=[RXR[ti][c], RvecA], writes=[RXR[ti][c]])
                            yield
                        else:
                            K.op("act", act(zc, zc, AF.Identity, bias=vc(bcol), scale=vc(gcol)),
                                 reads=[RXR[ti][c], Rvec], writes=[RXR[ti][c]])
                            yield
                yield from ln_gen(zap, nh, rz, False, epi, one_bank, zt, spacer)
            if final:
                out_tile(ti)

        def res_ln(ti, lnidx, final):
            run(res_ln_gen(ti, lnidx, final))

        def out_tile(ti):
            t0, n = TILES[ti]
            K.dma("sp", "oy", lambda e: e.dma_start(out=yTv[:, :, t0:t0 + n], in_=xres[:, :, t0:t0 + n]), reads=RXR[ti])

        def ffn_phase(l, s, lnidx, final, nxt):
            K.fence(lambda e: e.memset(fz[:, 0:1], 0.0))
            unit = 0
            for g in range(8):
                par = g % 2
                ag, au, dw = A[2 * par], A[2 * par + 1], Dm[par]
                rag, rau, rdw = RA[2 * par], RA[2 * par + 1], RD[par]
                for oi, ti in enumerate((4, 0, 1, 2, 3)):
                    t0, n = TILES[ti]
                    up = unit % 2
                    unit += 1
                    hb = H[up]
                    RH = [R(f"h{up}_{j}") for j in range(4)]
                    for j in range(4):
                        gi, ui = (j % 2) * 2, (j % 2) * 2 + 1
                        mm(PS[gi][:, :n], [(ag[:, k, j * 128:(j + 1) * 128], xbf[:, k, t0:t0 + n]) for k in range(8)],
                           reads=[rag, RXB[ti]], bank=RPS[gi])
                        mm(PS[ui][:, :n], [(au[:, k, j * 128:(j + 1) * 128], xbf[:, k, t0:t0 + n]) for k in range(8)],
                           reads=[rau, RXB[ti]], bank=RPS[ui])
                        sgb = SG[j % 2]
                        rsg = R(f"sg{j % 2}")
                        K.op("act", act(sgb[:, :n], PS[gi][:, :n], AF.Silu), reads=[RPS[gi]], writes=[rsg], arena=True)
                        K.op("dve", tt(hb[:, j, :n], sgb[:, :n], PS[ui][:, :n], ALU.mult), reads=[rsg, RPS[ui]], writes=[RH[j]], arena=True)
                        bg_step(3)

                    def down(g=g, ti=ti, t0=t0, n=n, hb=hb, RH=RH, dw=dw, rdw=rdw):
                        for d in range(8):
                            bi = 4 + d % 3
                            mm(PS[bi][:, :n], [(dw[:, j, d * 128:(d + 1) * 128], hb[:, j, :n]) for j in range(4)],
                               reads=[rdw] + RH, bank=RPS[bi], arena_=True)
                            xr = xres[:, d, t0:t0 + n]
                            K.op("dve", stt(xr, PS[bi][:, :n], 0.5, xr, ALU.mult, ALU.add), reads=[RPS[bi], RXR[ti][d]], writes=[RXR[ti][d]])
                            bg_step(3)
                        if g == 7:
                            bg.append(res_ln_gen(ti, lnidx, final, True))
                    K.tick()
                    if oi == 0 and g >= 1:
                        if g + 1 < 8:
                            ffn_loads(l, s, g + 1)()
                        else:
                            nxt[0]()
                            nxt[1]()
                            nxt[2]()
                    K.defer(1, down)
            nxt[3]()
            nxt[4]()
            K.flush()
            nxt[5]()

        def hgrn_phase(nxt):
            K.fence(lambda e: e.memset(fz[:, 0:1], 0.0))
            for x in range(2):
                K.op("dve", lambda e, x=x: e.memset(HS[x]["scm"][:], 0.0), writes=[R(f"scm{x}")], arena=True)
            Rhl = R("e3")
            K.op("dve", ts(e3[:, 0:8], lbv[:, 8:16], 0.5, None, ALU.mult), reads=[Rlb, R("e3"), R("e3b")], writes=[Rhl])
            K.op("dve", ts(e3[:, 8:16], lbv[:, 8:16], -0.5, None, ALU.mult), reads=[Rlb], writes=[Rhl])
            K.op("dve", ts(e3[:, 16:24], lbv[:, 0:8], 0.5, 0.5, ALU.mult, ALU.add), reads=[Rlb], writes=[Rhl])
            K.op("dve", ts(e3[:, 24:25], vc(NGO), 128.0 ** -0.5, None, ALU.mult), reads=[Rvec, R("e3b")], writes=[R("e3b")])
            Rvt = R("v_tok")
            Robf = [R(f"obf{j}") for j in range(4)]
            RSP, RSS = R("S0"), [R("S1"), R("S2")]
            zbf = zb[:].rearrange("p c t -> p (c t)").bitcast(F32)
            zsqf = zsq[:].rearrange("p c t -> p (c t)").bitcast(F32)

            def head(hg, ti, j, x):
                t0, n = TILES[ti]
                nch, cl = (4, 128) if n == 512 else (2, 16)
                mid = cl // 2
                prompt = n == 512
                h = hg * 4 + j
                cs = slice(j * 128, (j + 1) * 128)
                B = HS[x]
                if j < 2:
                    qs, fb, Rq, Rf = B["qs"], B["fb"], R(f"qs{x}"), R(f"fb{x}")
                else:
                    zz, rz_ = (zbf, Rzb) if x == 0 else (zsqf, Rzsq)
                    qs, fb, Rq, Rf = zz[:, 0:512], zz[:, 512:1024], rz_, rz_
                km, bb, Shist = B["km"], B["bb"], B["Shist"]
                q_rel, k_rel, k_relT, dd = B["q_rel"], B["k_rel"], B["k_relT"], B["dd"]
                on_, hv, osq, gs, Sp4 = qs, qs, k_relT, fb, k_rel
                Rkm, Rbb, Rsh = R(f"km{x}"), R(f"bb{x}"), R(f"sh{x}")
                Rqr, Rkr, RkT, Rdd = R(f"q_rel{x}"), R(f"k_rel{x}"), R(f"k_relT{x}"), R(f"dd{x}")
                RS = [RSP] * 4 if prompt else RSS
                Sst = [S_p] * 4 if prompt else S_s
                msk, rmsk = (tri, Rtri) if prompt else (tri16, R("tri16"))
                scb, rscb = (B["scm"], R(f"scm{x}")) if prompt else (scm16[:, 32 * x:32 * x + 32], R(f"scm16_{x}"))
                bq, bf_, bo = x, 2 + x, 4 + x
                WQ, WF, WV, WG = A
                xin = lambda k: xbf[:, k, t0:t0 + n]
                mm(PS[bq][:, :n], [(WQ[:, k, cs], xin(k)) for k in range(8)], reads=[RXB[ti], RA[0]], bank=RPS[bq])
                mm(PS[bf_][:, :n], [(WF[:, k, cs], xin(k)) for k in range(8)], reads=[RXB[ti], RA[1]], bank=RPS[bf_])
                K.op("act", act(qs[:, :n], PS[bq][:, :n], AF.Silu), reads=[RPS[bq]], writes=[Rq], arena=True)
                K.op("act", act(fb[:, :n], PS[bf_][:, :n], AF.Tanh, scale=0.5), reads=[RPS[bf_]], writes=[Rf], arena=True)
                yield
                K.op("dve", ts(km[:, :n], fb[:, :n], e3[:, 8 + h:9 + h], e3[:, h:h + 1], ALU.mult, ALU.add), reads=[Rf, Rhl], writes=[Rkm], arena=True)
                yield
                K.op("act", act(fb[:, :n], fb[:, :n], AF.Ln, bias=e3[:, 16 + h:17 + h], scale=e3[:, h:h + 1]), reads=[Rf, Rhl], writes=[Rf], arena=True)
                yield
                if prompt:
                    K.op("dve", lambda e: e.tensor_tensor_scan(bb[:, :n], rmask[:, :n], fb[:, :n], 0.0, ALU.mult, ALU.add),
                         reads=[Rf, R("rmask")], writes=[Rbb], arena=True)
                    yield
                else:
                    for c in range(nch):
                        sl = slice(c * cl, (c + 1) * cl)
                        K.op("dve", lambda e, sl=sl: e.tensor_tensor_scan(bb[:, sl], cst[:, 2:3].to_broadcast([128, cl]), fb[:, sl], 0.0,
                                                                          ALU.mult, ALU.add),
                             reads=[Rf, Rcst], writes=[Rbb], arena=True)
                        yield
                b3 = bb[:, :n].rearrange("p (c t) -> p c t", t=cl)
                f3 = fb[:, :n].rearrange("p (c t) -> p c t", t=cl)
                dd3 = dd[:, 0:2 * nch].rearrange("p (c t) -> p c t", t=2)
                K.op("act", act(dd3, b3[:, :, mid::cl - 1 - mid], AF.Exp), reads=[Rbb], writes=[Rdd], arena=True)
                yield
                K.op("dve", tt(f3, b3, b3[:, :, mid:mid + 1].to_broadcast([128, nch, cl]), ALU.subtract), reads=[Rbb], writes=[Rf], arena=True)
                yield
                K.op("act", act(bb[:, :n], fb[:, :n], AF.Exp), reads=[Rf], writes=[Rbb], arena=True)
                yield
                K.op("act", act(fb[:, :n], fb[:, :n], AF.Exp, scale=-1.0), reads=[Rf], writes=[Rf], arena=True)
                yield
                K.op("dve", tt(q_rel[:, :n], qs[:, :n], bb[:, :n], ALU.mult), reads=[Rq, Rbb], writes=[Rqr], arena=True)
                yield
                K.op("dve", tt(k_rel[:, :n], km[:, :n], fb[:, :n], ALU.mult), reads=[Rkm, Rf], writes=[Rkr], arena=True)
                yield

                def trfn(e):
                    ins = None
                    for c in range(nch):
                        ins = e.transpose(PST[:cl, c * 128:(c + 1) * 128], k_rel[:, c * cl:(c + 1) * cl], ident[:])
                    return ins
                K.op("pe", trfn, reads=[Rkr, Rident], writes=[RPST], arena=True)
                K.op("act", act(k_relT[:cl, :nch * 128], PST[:cl, :nch * 128], AF.Copy), reads=[RPST], writes=[RkT], arena=True)
                yield

                def scfn(e):
                    ins = None
                    for c in range(nch):
                        sl = slice(c * cl, (c + 1) * cl)
                        if cl == 128:
                            ins = e.matmul(PS[bf_][0:64, sl], k_rel[:, c * cl:c * cl + 64], q_rel[:, sl], start=True, stop=True)
                            ins = e.matmul(PS[bf_][64:128, c * cl + 64:(c + 1) * cl], k_rel[:, c * cl + 64:(c + 1) * cl],
                                           q_rel[:, c * cl + 64:(c + 1) * cl], start=True, stop=True)
                        else:
                            ins = e.matmul(PS[bf_][:cl, sl], k_rel[:, sl], q_rel[:, sl], start=True, stop=True)
                    return ins
                K.op("pe", scfn, reads=[Rkr, Rqr], writes=[RPS[bf_]], arena=True)
                yield
                K.op("dve", lambda e: e.copy_predicated(scb[:cl, :nch * cl], msk[:cl, :nch * cl], PS[bf_][:cl, :nch * cl]),
                     reads=[RPS[bf_], rmsk], writes=[rscb], arena=True)
                yield

                def ufn(e):
                    ins = None
                    for c in range(nch):
                        ins = e.matmul(PS[bq][:, c * 128:(c + 1) * 128], k_relT[:cl, c * 128:(c + 1) * 128], v_tok[:cl, c, cs], start=True, stop=True)
                    return ins
                K.op("pe", ufn, reads=[RkT, Rvt], writes=[RPS[bq]], arena=True)
                yield
                mm(PS[bo][:, :n], [(WG[:, k, cs], xin(k)) for k in range(8)], reads=[RXB[ti], RA[3]], bank=RPS[bo])
                yield
                K.op("act", act(gs[:, :n], PS[bo][:, :n], AF.Silu), reads=[RPS[bo], Rf], writes=[Rf], arena=True)
                yield
                yield
                if prompt:
                    K.op("dve", lambda e: e.tensor_copy(Shist[:, 0, :], S_p[:, j, :]), reads=[RSP], writes=[Rsh], arena=True)
                    yield
                tmpU4 = km[:, :nch * 128].rearrange("p (c v) -> p c v", v=128)
                K.op("dve", tt(tmpU4, PS[bq][:, :nch * 128].rearrange("p (c v) -> p c v", v=128),
                               b3[:, :, cl - 1:cl].to_broadcast([128, nch, 128]), ALU.mult),
                     reads=[RPS[bq], Rbb, Rkm], writes=[Rkm], arena=True)
                yield "mU"
                for c in range(nch):
                    d1c = dd[:, 2 * c + 1:2 * c + 2]
                    if prompt:
                        src = Shist[:, c, :]
                        tgt = Shist[:, c + 1, :] if c < nch - 1 else S_p[:, j, :]
                        rsrc, rtgt = [Rsh], ([Rsh] if c < nch - 1 else [RSP, Rsh])
                    else:
                        K.op("dve", lambda e, c=c: e.tensor_copy(Shist[:, c, :], S_s[c][:, j, :]), reads=[RSS[c]], writes=[Rsh], arena=True)
                        yield
                        src, tgt, rsrc, rtgt = Shist[:, c, :], S_s[c][:, j, :], [Rsh], [RSS[c]]
                    K.op("dve", stt(tgt, src, d1c, tmpU4[:, c, :], ALU.mult, ALU.add), reads=rsrc + [Rdd, Rkm], writes=rtgt, arena=True)
                    yield
                K.op("dve", tt(Sp4[:, :nch * 128].rearrange("p (c v) -> p c v", v=128), Shist[:, 0:nch, :],
                               dd3[:, :, 0:1].to_broadcast([128, nch, 128]), ALU.mult), reads=[Rsh, Rdd, Rkr], writes=[Rkr], arena=True)
                yield
                yield
                for c in range(nch):
                    sl = slice(c * cl, (c + 1) * cl)
                    mm(PS[bo][:, sl], [(v_tok[:cl, c, cs], scb[:cl, sl]), (Sp4[:, c * 128:(c + 1) * 128], q_rel[:, sl])],
                       reads=[Rvt, rscb, Rkr, Rqr], bank=RPS[bo], arena_=True)
                    yield
                yield "m3b"
                K.op("act", act(osq[:, :n], PS[bo][:, :n], AF.Square), reads=[RPS[bo], RkT], writes=[RkT], arena=True)
                yield
                mm(PS[bf_][:, :n], [(ones[:], osq[:, :n])], reads=[RkT, Rones], bank=RPS[bf_], arena_=True)
                yield
                K.op("act", act(hv[:, :n], PS[bf_][:, :n], AF.Ln, bias=cst[:, 0:1], scale=1.0 / (128.0 * 128.0)), reads=[RPS[bf_], Rcst, Rq], writes=[Rq], arena=True)
                yield
                K.op("act", act(hv[:, :n], hv[:, :n], AF.Exp, scale=-0.5), reads=[Rq], writes=[Rq], arena=True)
                yield
                K.op("dve", tt(on_[:, :n], PS[bo][:, :n], hv[:, :n], ALU.mult), reads=[RPS[bo], Rq], writes=[Rq], arena=True)
                yield
                K.op("dve", stt(obf[:, j, :n], on_[:, :n], e3[:, 24:25], gs[:, :n], ALU.mult, ALU.mult), reads=[Rq, Rf, R("e3b")], writes=[Robf[j]], arena=True)
                yield

            def drive(gens):
                gens = list(gens)
                while gens:
                    for g_ in list(gens):
                        try:
                            next(g_)
                        except StopIteration:
                            gens.remove(g_)
                    bg_step()

            def tail(hg, ti):
                t0, n = TILES[ti]
                WO, RWO = Dm[hg], RD[hg]
                for d in range(8):
                    bi = (6, 4, 5)[d % 3]
                    mm(PS[bi][:, :n], [(WO[:, j, d * 128:(d + 1) * 128], obf[:, j, :n]) for j in range(4)],
                       reads=[RWO] + Robf, bank=RPS[bi], arena_=True)
                    xr = xres[:, d, t0:t0 + n]
                    K.op("dve", tt(xr, PS[bi][:, :n], xr, ALU.add), reads=[RPS[bi], RXR[ti][d]], writes=[RXR[ti][d]])
                yield
                if hg == 1:
                    d0v = Dm[0][:, 0:2, :].rearrange("p j (c t) -> p (j c) t", t=256)
                    d1v = Dm[0][:, 2:4, :].rearrange("p j (c t) -> p (j c) t", t=256)
                    yield from res_ln_gen(ti, 1, False, True, "act", (d0v, d1v, RD[0], RD[0]))

            for hg in range(2):
                WV = A[2]
                WO, RWO = Dm[hg], RD[hg]
                K.op("dve", lambda e: e.memset(S_p[:], 0.0), writes=[RSP], arena=True)
                for s_ in range(2):
                    K.dma("sp", f"is{hg}{s_}", lambda e, s_=s_, hg=hg: e.dma_start(out=S_s[s_][:], in_=st_in[s_, hg * 4:(hg + 1) * 4].rearrange("h k v -> k h v")),
                          writes=[RSS[s_]], arena=True)
                order = [4, 0, 1, 2, 3]
                pairs = [(oi, ti, pb) for oi, ti in enumerate(order) for pb in (0, 1)]
                gen_of = {}

                def get_gens(ti):
                    if ti not in gen_of:
                        gen_of[ti] = [head(hg, ti, j, j % 2) for j in range(4)]
                    return gen_of[ti]

                def emit_s1(ti, pb):
                    g4 = get_gens(ti)
                    next(g4[2 * pb])
                    next(g4[2 * pb + 1])

                emit_s1(order[0], 0)
                first_b_pending = True
                prev = []
                prev_tile_end = None
                for (oi, ti, pb) in pairs:
                    t0, n = TILES[ti]
                    lastt = oi == len(order) - 1
                    nch, cl = (4, 128) if n == 512 else (2, 16)
                    cur = get_gens(ti)[2 * pb:2 * pb + 2]
                    if pb == 0:
                        for c in range(nch):
                            mm(PS[6][:cl, :], [(xbf[:, k, t0 + c * cl:t0 + (c + 1) * cl], WV[:, k, :]) for k in range(8)],
                               reads=[RXB[ti], RA[2]], bank=RPS[6])
                            K.op("act", act(v_tok[:cl, c, :], PS[6][:cl, :], AF.Copy), reads=[RPS[6]], writes=[Rvt], arena=True)
                        if lastt:
                            if hg == 0:
                                loadA(2, w_in, 2048 + 512)
                            else:
                                nxt[3]()
                    reached = set()
                    reachedU = set()
                    pend_s1 = []

                    def step(g_, ti=ti, oi=oi, pb=pb, lastt=lastt):
                        nonlocal prev_tile_end
                        try:
                            v_ = next(g_)
                            if v_ == "m3b":
                                reached.add(id(g_))
                            elif v_ == "mU":
                                reachedU.add(id(g_))
                            return True
                        except StopIteration:
                            if g_ in prev:
                                prev.remove(g_)
                                if not prev and prev_tile_end is not None:
                                    tg_ = tail(hg, prev_tile_end)
                                    next(tg_)
                                    bg.append(tg_)
                                    pend_s1.append(lambda: emit_s1(ti, 1))
                                    prev_tile_end = None
                                elif not prev and pb == 1:
                                    if not lastt:
                                        pend_s1.append(lambda: emit_s1(order[oi + 1], 0))
                                    else:
                                        if hg == 0:
                                            loadA(0, w_in, 512)
                                            loadA(1, w_in, 1024 + 512)
                                        else:
                                            nxt[0]()
                                            nxt[1]()
                            return False

                    while len(reached) < 2:
                        for g_ in list(prev):
                            step(g_)
                        for g_ in cur:
                            if id(g_) not in reached:
                                step(g_)
                        if pend_s1 and len(reachedU) == 2:
                            pend_s1.pop(0)()
                        bg_step()
                    while prev:
                        for g_ in list(prev):
                            step(g_)
                        bg_step()
                    while pend_s1:
                        pend_s1.pop(0)()
                    if first_b_pending:
                        bg_drain()
                        emit_s1(order[0], 1)
                        first_b_pending = False
                    prev = list(cur)
                    prev_tile_end = ti if pb == 1 else None
                while prev:
                    for g_ in list(prev):
                        try:
                            next(g_)
                        except StopIteration:
                            prev.remove(g_)
                    bg_step()
                if hg == 0:
                    loadA(3, w_in, 3072 + 512)
                else:
                    nxt[4]()
                bg.append(tail(hg, order[-1]))
                while bg:
                    bg_step()
                if hg == 1:
                    nxt[2]()
                    nxt[5]()
                K.dma("sp", f"os{hg}", lambda e, hg=hg: e.dma_start(out=so_p[hg * 4:(hg + 1) * 4].rearrange("h k v -> k h v"), in_=S_p[:]), reads=[RSP], arena=True)
                for s_ in range(2):
                    K.dma("sp", f"os{hg}{s_}", lambda e, s_=s_, hg=hg: e.dma_start(out=so_s[s_, hg * 4:(hg + 1) * 4].rearrange("h k v -> k h v"), in_=S_s[s_][:]),
                          reads=[RSS[s_]], arena=True)

        hgrn_first = lambda: (loadA(0, w_in, 0), loadA(1, w_in, 1024), loadD(0, w_out, 0))
        hgrn_second = lambda: (loadA(2, w_in, 2048), loadA(3, w_in, 3072), loadD(1, w_out, 512))

        def conv_phase(nxt):
            K.fence(lambda e: e.memset(fz[:, 0:1], 0.0))
            RU = [R(f"U{c}") for c in range(8)]
            RUs = [R(f"Us{c}") for c in range(8)]
            RY = [R(f"y{c}") for c in range(8)]
            Rsgt = [R("sgt0"), R("sgt1")]
            K.op("dve", lambda e: e.memset(U[:, :, 0:30], 0.0), writes=RU, arena=True)
            for s_ in range(2):
                K.dma("sp", f"ic{s_}", lambda e, s_=s_: e.dma_start(out=U_s[:, :, s_, 0:30], in_=ccT[s_].rearrange("(c p) w -> p c w", p=128)),
                      writes=RUs, arena=True)
            def front(ti, cp):
                t0, n = TILES[ti]
                prompt = n == 512
                info = []
                for x in range(2):
                    c = 2 * cp + x
                    ai, gi = 2 * x, 2 * x + 1
                    cs = slice((c % 4) * 128, (c % 4 + 1) * 128)
                    mm(PS[ai][:, :n], [(A[c // 4][:, k, cs], xbf[:, k, t0:t0 + n]) for k in range(8)], reads=[RXB[ti], RA[c // 4]], bank=RPS[ai])
                    mm(PS[gi][:, :n], [(A[2 + c // 4][:, k, cs], xbf[:, k, t0:t0 + n]) for k in range(8)], reads=[RXB[ti], RA[2 + c // 4]], bank=RPS[gi])
                    if prompt:
                        ru = RU[c]
                        unew = U[:, c, 30:542]
                        pa, pg_ = PS[ai][:, :n], PS[gi][:, :n]
                        tap = (lambda c: (lambda w: U[:, c, w:w + 512]))(c)
                        yc = ybuf[:, c, :]
                    else:
                        ru = RUs[c]
                        unew = U_s[:, c, :, 30:46]
                        pa = PS[ai][:, :n].rearrange("p (s t) -> p s t", t=16)
                        pg_ = PS[gi][:, :n].rearrange("p (s t) -> p s t", t=16)
                        tap = (lambda c: (lambda w: U_s[:, c, :, w:w + 16]))(c)
                        yc = ybuf[:, c, :n].rearrange("p (s t) -> p s t", t=16)
                    K.op("act", act(unew, pg_, AF.Sigmoid, bias=vc(BGO + c)), reads=[RPS[gi], Rvec], writes=[ru], arena=True)
                    info.append(dict(c=c, x=x, ru=ru, tap=tap, yc=yc, unew=unew, pa=pa, ai=ai))
                return info

            def uop(info):
                for d_ in info:
                    c, x = d_["c"], d_["x"]
                    K.op("dve", stt(d_["unew"], d_["pa"], vc(BAO + c), d_["unew"], ALU.add, ALU.mult),
                         reads=[RPS[d_["ai"]], d_["ru"], Rvec], writes=[d_["ru"]], arena=True)

            def taps(ti, cp, info):
                t0, n = TILES[ti]
                prompt = n == 512
                npe = CW - NDVE_TAPS
                for i in range(max(NDVE_TAPS, npe)):
                    for d_ in info:
                        c, x, ru, tap, yc = d_["c"], d_["x"], d_["ru"], d_["tap"], d_["yc"]
                        if i < NDVE_TAPS:
                            w = i
                            wc = vc(WDWO + c * CW + w)
                            if w == 0:
                                K.op("dve", ts(yc, tap(0), wc, vc(BDWO + c), ALU.mult, ALU.add), reads=[ru, Rvec], writes=[RY[c]], arena=True)
                            else:
                                K.op("dve", stt(yc, tap(w), wc, yc, ALU.mult, ALU.add), reads=[ru, Rvec, RY[c]], writes=[RY[c]], arena=True)
                        if i < npe:
                            w = NDVE_TAPS + i
                            wc = vc(WDWO + c * CW + w)
                            pb_ = prod4[2 * x + i % 2]
                            rpb = R(f"prod{2 * x + i % 2}")
                            pv_ = pb_[:, :n] if prompt else pb_[:, :n].rearrange("p (s t) -> p s t", t=16)
                            K.op("act", act(pv_, tap(w), AF.Identity, scale=wc), reads=[ru, Rvec], writes=[rpb], arena=True)
                            K.op("pe", lambda e, x=x, i=i, n=n, npe=npe, pb_=pb_: e.matmul(PS[4 + x][:, :n], identf[:], pb_[:, :n], start=(i == 0), stop=(i == npe - 1)),
                                 reads=[rpb, R("identf")], writes=[RPS[4 + x]], arena=True)
                    bg_step()
                for d_ in info:
                    c, x = d_["c"], d_["x"]
                    ycf = ybuf[:, c, :n]
                    K.op("dve", tt(ycf, ycf, PS[4 + x][:, :n], ALU.add), reads=[RY[c], RPS[4 + x]], writes=[RY[c]], arena=True)
                if prompt and ti < 3:
                    for d_ in info:
                        c = d_["c"]
                        K.op("act", act(U[:, c, 0:30], U[:, c, 512:542], AF.Copy), reads=[d_["ru"]], writes=[d_["ru"]], arena=True)

            def tail(ti):
                t0, n = TILES[ti]
                hv_ = halves(t0, n)
                for hi, (h0, nh) in enumerate(hv_):
                    o0 = h0 - t0
                    zap = ybuf[:, :, o0:o0 + nh]

                    def epi(o0=o0, nh=nh):
                        for c in range(8):
                            K.op("act", act(zb[:, c, :nh], ybuf[:, c, o0:o0 + nh], AF.Silu, bias=vc(CLBO + c), scale=vc(CLGO + c)),
                                 reads=[RY[c], Rvec], writes=[Rzb], arena=True)
                            yield
                    yield from ln_gen(zap, nh, RY, True, epi, True, None, 2)
                    if hi == len(hv_) - 1:
                        yield "ybuf_free"
                    pstf = PST[:].bitcast(F32)
                    pend = None
                    for d in range(9):
                        if d < 8:
                            pb, rpb = (PS[6][:, :nh], RPS[6]) if d % 2 == 0 else (pstf[:, :nh], RPST)
                            mm(pb, [(Dm[k // 4][:, k % 4, d * 128:(d + 1) * 128], zb[:, k, :nh]) for k in range(8)],
                               reads=[Rzb, RD[0], RD[1]], bank=rpb)
                            yield
                        if pend is not None:
                            d0, pb0, rpb0 = pend
                            xr = xres[:, d0, h0:h0 + nh]
                            K.op("dve", stt(xr, pb0, vc(B2O + d0), xr, ALU.add, ALU.add), reads=[rpb0, RXR[ti][d0], Rvec], writes=[RXR[ti][d0]])
                            yield
                        if d < 8:
                            pend = (d, pb, rpb)
                    if hi == len(hv_) - 1:
                        yield "pw2_done"
                yield from res_ln_gen(ti, 4, False, True, "act", None, 2 if ti != 3 else 0)

            seq = [(ti, cp) for ti in (4, 0, 1, 2, 3) for cp in range(4)]
            cur = front(*seq[0])
            uop(cur)
            for k, (ti, cp) in enumerate(seq):
                nxt_info = front(*seq[k + 1]) if k + 1 < len(seq) else None
                if k + 1 == len(seq) - 1:
                    nxt[0]()
                    nxt[1]()
                    nxt[3]()
                    nxt[4]()
                taps(ti, cp, cur)
                if nxt_info is not None:
                    uop(nxt_info)
                cur = nxt_info
                if cp == 3:
                    if ti == 3:
                        K.dma("sp", "oc", lambda e: e.dma_start(out=cco_p.rearrange("(c p) w -> p c w", p=128), in_=U[:, :, 512:542]), reads=RU, arena=True)
                    if TILES[ti][1] != 512:
                        for s_ in range(2):
                            K.dma("sp", "oc", lambda e, s_=s_: e.dma_start(out=cco_s[s_].rearrange("(c p) w -> p c w", p=128), in_=U_s[:, :, s_, 16:46]),
                                  reads=RUs, arena=True)
                    while bg:
                        bg_step()
                    g_ = tail(ti)
                    for v_ in g_:
                        if v_ == "ybuf_free":
                            break
                    bg.append(g_)
            while bg:
                g_ = bg[0]
                done_ = False
                for v_ in g_:
                    if v_ == "pw2_done" and len(bg) == 1:
                        nxt[2]()
                        nxt[5]()
                        done_ = True
                        break
                if done_:
                    break
                bg.pop(0)

        conv_first = lambda: (loadA(0, w_pw1, 0), loadA(1, w_pw1, 512), loadD(0, w_pw2, 0))
        conv_second = lambda: (loadA(2, w_pw1, 1024), loadA(3, w_pw1, 1536), loadD(1, w_pw2, 512))

        ffn_first = lambda l, s: ffn_loads(l, s, 0)
        ffn_second = lambda l, s: ffn_loads(l, s, 1)
        phases = [
            ("ffn", 0, 0, 0), ("hgrn",), ("ffn", 0, 1, 2), ("ffn", 1, 0, 3), ("conv",), ("ffn", 1, 1, 5),
        ][:nsub]

        def first_second(p):
            if p is None:
                return [nop] * 6
            if p[0] == "ffn":
                l, s_ = p[1], p[2]
                return [lambda: loadA(0, w_gate[l, s_], 0), lambda: loadA(1, w_up[l, s_], 0), lambda: loadD(0, w_down[l, s_], 0),
                        lambda: loadA(2, w_gate[l, s_], 512), lambda: loadA(3, w_up[l, s_], 512), lambda: loadD(1, w_down[l, s_], 512)]
            if p[0] == "hgrn":
                return [lambda: loadA(0, w_in, 0), lambda: loadA(1, w_in, 1024), lambda: loadD(0, w_out, 0),
                        lambda: loadA(2, w_in, 2048), lambda: loadA(3, w_in, 3072), lambda: loadD(1, w_out, 512)]
            return [lambda: loadA(0, w_pw1, 0), lambda: loadA(1, w_pw1, 512), lambda: loadD(0, w_pw2, 0),
                    lambda: loadA(2, w_pw1, 1024), lambda: loadA(3, w_pw1, 1536), lambda: loadD(1, w_pw2, 512)]

        for i, p in enumerate(phases):
            nxt = first_second(phases[i + 1] if i + 1 < len(phases) else None)
            last = i == len(phases) - 1
            if p[0] == "ffn":
                ffn_phase(p[1], p[2], p[3], last and nsub == 6, nxt)
            elif p[0] == "hgrn":
                hgrn_phase(nxt)
            else:
                conv_phase(nxt)
        bg_drain()
        if nsub < 6:
            for ti in range(5):
                out_tile(ti)
        spq = K.q["sp"]
        for k in list(K.dcnt):
            if k.startswith("o"):
                spq.items.append(("w", k, K.dcnt[k]))

        sems = {k: es.enter_context(nc.semaphore(f"s_{k}")) for k in K.semkeys}

        def replay(q, e):
            own = sems[q.name]
            for it in q.items:
                if it[0] == "w":
                    e.wait_ge(sems[it[1]], it[2])
                elif it[0] == "o":
                    it[1](e).then_inc(own, 1)
                else:
                    it[1](e).then_inc(sems[it[2]], 16)

        with nc.Block() as block:
            @block.sync
            def _(e):
                replay(K.q["sp"], e)

            @block.scalar
            def _(e):
                replay(K.q["act"], e)

            @block.vector
            def _(e):
                replay(K.q["dve"], e)

            @block.gpsimd
            def _(e):
                replay(K.q["pool"], e)

            @block.tensor
            def _(e):
                replay(K.q["pe"], e)
    return nc


def _vecs(ln_g, ln_b, hgrn_lb, hgrn_norm_g, conv_b_pw1, conv_w_dw, conv_b_dw, conv_ln_g, conv_ln_b, conv_b_pw2):
    cm = lambda v: np.asarray(v, np.float32).reshape(-1, 8, 128).transpose(2, 0, 1).reshape(128, -1)
    parts = [
        cm(np.asarray(ln_g).reshape(6, D)), cm(np.asarray(ln_b).reshape(6, D)), cm(np.asarray(hgrn_lb)),
        np.asarray(hgrn_norm_g, np.float32)[0].reshape(128, 1),
        cm(np.asarray(conv_b_pw1)[0][:D]), cm(np.asarray(conv_b_pw1)[0][D:]),
        np.asarray(conv_w_dw, np.float32)[0].reshape(CW, 8, 128).transpose(2, 1, 0).reshape(128, 8 * CW),
        cm(np.asarray(conv_b_dw)[0]), cm(np.asarray(conv_ln_g)[0]), cm(np.asarray(conv_ln_b)[0]), cm(np.asarray(conv_b_pw2)[0]),
    ]
    v = np.ascontiguousarray(np.concatenate(parts, axis=1).astype(np.float32))
    assert v.shape == (128, NV), v.shape
    return v


def make_in_maps(x_prompt, x_sample, state_hgrn, cache_conv, ffn_w_gate, ffn_w_up, ffn_w_down, ln_g, ln_b,
                 hgrn_w_in, hgrn_lb, hgrn_norm_g, hgrn_w_out, conv_w_pw1, conv_b_pw1, conv_w_dw, conv_b_dw,
                 conv_ln_g, conv_ln_b, conv_w_pw2, conv_b_pw2):
    f = lambda a: np.ascontiguousarray(np.asarray(a, dtype=np.float32))
    x_prompt, x_sample, state_hgrn, cache_conv = f(x_prompt), f(x_sample), f(state_hgrn), f(cache_conv)
    vecs = _vecs(ln_g, ln_b, hgrn_lb, hgrn_norm_g, conv_b_pw1, conv_w_dw, conv_b_dw, conv_ln_g, conv_ln_b, conv_b_pw2)
    s_idx = np.arange(128)[:, None]
    tri = (np.tile(np.arange(128), 4)[None, :] >= s_idx).astype(np.uint16)
    tri16 = (np.tile(np.arange(16), 2)[None, :] >= s_idx).astype(np.uint16)
    rmask = np.ascontiguousarray(np.broadcast_to((np.arange(512) % 128 != 0).astype(np.float32)[None, :], (128, 512)))
    shared = {
        "wg": f(ffn_w_gate), "wu": f(ffn_w_up), "wd": f(ffn_w_down), "win": f(hgrn_w_in)[0], "wout": f(hgrn_w_out)[0],
        "pw1": f(conv_w_pw1)[0], "pw2": f(conv_w_pw2)[0], "vecs": vecs, "tri": np.ascontiguousarray(tri),
        "tri16": np.ascontiguousarray(tri16), "rmask": rmask, "ident": np.eye(128, dtype=np.float32),
    }
    maps = []
    for i in range(NCORES):
        xa = np.concatenate([x_prompt[i], x_sample[2 * i], x_sample[2 * i + 1]], axis=0)
        m = dict(shared)
        m["xT"] = np.ascontiguousarray(xa.T)
        m["st"] = np.ascontiguousarray(state_hgrn[0, 2 * i:2 * i + 2])
        m["ccT"] = np.ascontiguousarray(cache_conv[0, 2 * i:2 * i + 2].transpose(0, 2, 1))
        maps.append(m)
    return maps


def gather(results):
    y_prompt = np.empty((8, TP, D), np.float32)
    y_sample = np.empty((16, TS, D), np.float32)
    hs_p = np.empty((1, 8, 8, 128, 128), np.float32)
    hs_s = np.empty((1, 16, 8, 128, 128), np.float32)
    cc_p = np.empty((1, 8, 30, D), np.float32)
    cc_s = np.empty((1, 16, 30, D), np.float32)
    for i, r in enumerate(results):
        y = np.asarray(r["yT"]).T
        y_prompt[i] = y[:TP]
        y_sample[2 * i] = y[TP:TP + TS]
        y_sample[2 * i + 1] = y[TP + TS:]
        hs_p[0, i] = np.asarray(r["so_p"])
        hs_s[0, 2 * i:2 * i + 2] = np.asarray(r["so_s"])
        cc_p[0, i] = np.asarray(r["cco_p"]).T
        cc_s[0, 2 * i:2 * i + 2] = np.asarray(r["cco_s"]).transpose(0, 2, 1)
    return (y_prompt, y_sample, hs_p, hs_s, cc_p, cc_s)


_NC_CACHE = {}


def kernel(**inputs):
    maps = make_in_maps(**inputs)
    if 6 not in _NC_CACHE:
        _NC_CACHE[6] = build_nc(6)
    res = run_bass_kernel_spmd(_NC_CACHE[6], maps, core_ids=list(range(NCORES)))
    return gather(res.results)
```
